# Optimizing a Trainium2 kernel written in Bass

```python
import jax, jax.numpy as jnp
from jax import lax
import numpy as np

D_MODEL = 1024
BATCH = 2
SEQ = 8192
DEPTH = 1
DEC_BATCH = 16
DEC_SEQ = 64
PAST_LEN = 1024

CHUNK = 64
M_HEADS = 4
M_HEAD_DIM = D_MODEL // M_HEADS
M_WIDTH = M_HEADS * M_HEAD_DIM
CONV_W = 4
A_HEADS = 8
A_HEAD_DIM = D_MODEL // A_HEADS
A_WIDTH = A_HEADS * A_HEAD_DIM
IDX_HEADS = 8
IDX_DIM = 64
TOPK_MAX = 256
Q_BLOCK = 128
D_FF = ((8 * D_MODEL + 3 * 256 - 1) // (3 * 256)) * 256
EPS = 1e-6
IN_SPLITS = (M_WIDTH, M_WIDTH, M_WIDTH, M_WIDTH, M_HEADS, M_HEADS,
             A_WIDTH, A_WIDTH, A_WIDTH,
             IDX_HEADS * IDX_DIM, IDX_DIM, IDX_HEADS,
             D_MODEL, D_MODEL)
D_IN = sum(IN_SPLITS)

kernel_name = 'hybrid_mlstm_dsa_stream_step'


def rms_norm(x, g):
    x32 = x.astype(jnp.float32)
    y = x32 * lax.rsqrt(jnp.mean(x32 * x32, axis=-1, keepdims=True) + EPS)
    return (y * g.astype(jnp.float32)).astype(x.dtype)


def split_cols(z):
    out, start = [], 0
    for w in IN_SPLITS:
        out.append(z[..., start:start + w])
        start += w
    return out


def causal_conv(u, buf, w, b):
    T = u.shape[1]
    full = jnp.concatenate([buf.astype(u.dtype), u], axis=1)
    out = b.astype(u.dtype) + full[:, 0:T] * w[0]
    for j in range(1, CONV_W):
        out = out + full[:, j:j + T] * w[j]
    return out, full[:, -(CONV_W - 1):]


def mlstm_chunkwise(q, k, v, li, lf, C0, n0, m0):
    B, T, H, D = q.shape
    L = min(CHUNK, T)
    nc = T // L
    f32 = jnp.float32

    def chunks(a):
        a = a.astype(f32)
        return jnp.moveaxis(a.reshape((B, nc, L) + a.shape[2:]), 1, 0)

    causal = jnp.tril(jnp.ones((L, L), dtype=bool))

    def step(carry, xs):
        C, n, m = carry
        qc, kc, vc, lic, lfc = xs
        b = jnp.swapaxes(jnp.cumsum(lfc, axis=1), 1, 2)
        ig = jnp.swapaxes(lic, 1, 2)
        dmat = jnp.where(causal, b[..., :, None] - b[..., None, :] + ig[..., None, :], -jnp.inf)
        inter = b + m[..., None]
        m_t = jnp.maximum(inter, jnp.max(dmat, axis=-1))
        w_intra = jnp.exp(dmat - m_t[..., None])
        w_inter = jnp.exp(inter - m_t)
        s = jnp.einsum('blhd,bshd->bhls', qc, kc) * w_intra
        num = (jnp.einsum('bhls,bshe->blhe', s, vc)
               + jnp.einsum('bhl,bhed,blhd->blhe', w_inter, C, qc))
        den = jnp.sum(s, axis=-1) + w_inter * jnp.einsum('bhd,blhd->bhl', n, qc)
        denom = jnp.maximum(jnp.abs(den), jnp.exp(-m_t))
        h = num / jnp.swapaxes(denom, 1, 2)[..., None]
        m_new = m_t[..., -1]
        g = jnp.exp(b[..., -1:] - b + ig - m_new[..., None])
        decay = jnp.exp(b[..., -1] + m - m_new)
        C_new = decay[..., None, None] * C + jnp.einsum('bhs,bshe,bshd->bhed', g, vc, kc)
        n_new = decay[..., None] * n + jnp.einsum('bhs,bshd->bhd', g, kc)
        return (C_new, n_new, m_new), h

    (C, n, m), hs = lax.scan(step, (C0.astype(f32), n0.astype(f32), m0.astype(f32)),
                             (chunks(q), chunks(k), chunks(v), chunks(li), chunks(lf)))
    h = jnp.moveaxis(hs, 0, 1).reshape(B, T, H, D)
    return h, C, n, m


def mlstm_branch(mq, mk, mv, mo, mi, mf, conv_buf, w_conv, b_conv, b_if, g_mnorm, C0, n0, m0):
    B, T, _ = mq.shape
    f32 = jnp.float32
    qk, conv_new = causal_conv(jnp.concatenate([mq, mk], axis=-1), conv_buf, w_conv, b_conv)
    qk = jax.nn.silu(qk)
    q = qk[..., :M_WIDTH].reshape(B, T, M_HEADS, M_HEAD_DIM)
    k = qk[..., M_WIDTH:].reshape(B, T, M_HEADS, M_HEAD_DIM) * (M_HEAD_DIM ** -0.5)
    v = mv.reshape(B, T, M_HEADS, M_HEAD_DIM)
    b32 = b_if.astype(f32)
    li = mi.astype(f32) + b32[:M_HEADS]
    lf = jax.nn.log_sigmoid(mf.astype(f32) + b32[M_HEADS:])
    h, C, n, m = mlstm_chunkwise(q, k, v, li, lf, C0, n0, m0)
    h = rms_norm(h, g_mnorm.reshape(M_HEADS, M_HEAD_DIM)).reshape(B, T, M_WIDTH).astype(mq.dtype)
    return h * jax.nn.sigmoid(mo), conv_new, C, n, m


def dsa_attention(q, k, v, qi, ki, wi, q_pos0):
    B, Tq, H, Dh = q.shape
    S = k.shape[1]
    f32 = jnp.float32
    topk = min(TOPK_MAX, S // 4)
    blk = Q_BLOCK if Tq % Q_BLOCK == 0 else Tq
    nb = Tq // blk
    k_chunk = jnp.arange(S) // CHUNK
    q_pos = (q_pos0 + jnp.arange(Tq)).reshape(nb, blk)
    w_scale = (IDX_HEADS * IDX_DIM) ** -0.5
    ki32 = ki.astype(f32)

    def blocks(a):
        return jnp.moveaxis(a.reshape((B, nb, blk) + a.shape[2:]), 1, 0)

    def one_block(args):
        qb, qib, wib, pos = args
        rel = jax.nn.relu(jnp.einsum('bqhd,bsd->bqhs', qib.astype(f32), ki32))
        score = jnp.einsum('bqh,bqhs->bqs', wib.astype(f32) * w_scale, rel)
        q_chunk = pos // CHUNK
        adm = k_chunk[None, :] <= q_chunk[:, None]
        score = jnp.where(adm[None], score, -jnp.inf)
        _, idx = lax.top_k(score, topk)
        valid = k_chunk[idx] <= q_chunk[None, :, None]
        kg = jax.vmap(lambda kb, ib: kb[ib])(k, idx)
        vg = jax.vmap(lambda vb, ib: vb[ib])(v, idx)
        logits = jnp.einsum('bqhd,bqkhd->bqhk', qb.astype(f32), kg.astype(f32)) * (Dh ** -0.5)
        logits = jnp.where(valid[:, :, None, :], logits, -jnp.inf)
        p = jax.nn.softmax(logits, axis=-1)
        return jnp.einsum('bqhk,bqkhd->bqhd', p, vg.astype(f32)).astype(q.dtype)

    out = lax.map(one_block, (blocks(q), blocks(qi), blocks(wi), q_pos))
    return jnp.moveaxis(out, 0, 1).reshape(B, Tq, H * Dh)


def layer(x, past, p):
    k_past, v_past, ki_past, C0, n0, m0, conv_buf = past
    (g_norm1, w_in, b_if, w_conv, b_conv, g_mnorm, g_q, g_k,
     w_a_out, w_b_out, w_o, g_norm2, w_ffn_in, w_ffn_out) = p
    B, T, _ = x.shape
    h = rms_norm(x, g_norm1)
    (mq, mk, mv, mo, mi, mf, aq, ak, av, iq, ik, iw, ga, gb) = split_cols(h @ w_in)
    y_a, conv_new, C, n, m = mlstm_branch(mq, mk, mv, mo, mi, mf, conv_buf, w_conv, b_conv,
                                          b_if, g_mnorm, C0, n0, m0)
    q = rms_norm(aq.reshape(B, T, A_HEADS, A_HEAD_DIM), g_q)
    k = rms_norm(ak.reshape(B, T, A_HEADS, A_HEAD_DIM), g_k)
    v = av.reshape(B, T, A_HEADS, A_HEAD_DIM)
    k_all = jnp.concatenate([k_past.astype(k.dtype), k], axis=1)
    v_all = jnp.concatenate([v_past.astype(v.dtype), v], axis=1)
    ki_all = jnp.concatenate([ki_past.astype(ik.dtype), ik], axis=1)
    y_b = dsa_attention(q, k_all, v_all, iq.reshape(B, T, IDX_HEADS, IDX_DIM), ki_all, iw,
                        k_past.shape[1])
    mix = jax.nn.sigmoid(ga) * (y_a @ w_a_out) + jax.nn.sigmoid(gb) * (y_b @ w_b_out)
    x = x + mix @ w_o
    h2 = rms_norm(x, g_norm2)
    gate, up = jnp.split(h2 @ w_ffn_in, 2, axis=-1)
    x = x + (jax.nn.silu(gate) * up) @ w_ffn_out
    return x, (k, v, ik, C, n, m, conv_new)


def setup_inputs(seed: int = 0) -> dict:
    key = jax.random.key(seed)
    ks = jax.random.split(key, 26)
    f32 = jnp.float32

    def nrm(k, shape, s=1.0):
        return s * jax.random.normal(k, shape, f32)

    def gain(k, shape):
        return 1.0 + 0.1 * jax.random.normal(k, shape, f32)

    b_if = jnp.concatenate([-1.0 + nrm(ks[0], (DEPTH, M_HEADS), 0.1),
                            3.0 + nrm(ks[1], (DEPTH, M_HEADS), 0.5)], axis=-1)
    return {
        'x_prompt': nrm(ks[2], (BATCH, SEQ, D_MODEL)),
        'x_sample': nrm(ks[3], (DEC_BATCH, DEC_SEQ, D_MODEL)),
        'cache_k': nrm(ks[4], (DEPTH, DEC_BATCH, PAST_LEN, A_HEADS, A_HEAD_DIM)),
        'cache_v': nrm(ks[5], (DEPTH, DEC_BATCH, PAST_LEN, A_HEADS, A_HEAD_DIM)),
        'cache_kidx': nrm(ks[6], (DEPTH, DEC_BATCH, PAST_LEN, IDX_DIM)),
        'state_C': nrm(ks[7], (DEPTH, DEC_BATCH, M_HEADS, M_HEAD_DIM, M_HEAD_DIM), 0.1),
        'state_n': nrm(ks[8], (DEPTH, DEC_BATCH, M_HEADS, M_HEAD_DIM), 0.1),
        'state_m': nrm(ks[9], (DEPTH, DEC_BATCH, M_HEADS)),
        'state_conv': nrm(ks[10], (DEPTH, DEC_BATCH, CONV_W - 1, 2 * M_WIDTH)),
        'g_norm1': gain(ks[11], (DEPTH, D_MODEL)),
        'w_in': nrm(ks[12], (DEPTH, D_MODEL, D_IN), D_MODEL ** -0.5),
        'b_if': b_if,
        'w_conv': nrm(ks[13], (DEPTH, CONV_W, 2 * M_WIDTH), CONV_W ** -0.5),
        'b_conv': nrm(ks[14], (DEPTH, 2 * M_WIDTH), 0.01),
        'g_mnorm': gain(ks[15], (DEPTH, M_WIDTH)),
        'g_q': gain(ks[16], (DEPTH, A_HEAD_DIM)),
        'g_k': gain(ks[17], (DEPTH, A_HEAD_DIM)),
        'w_a_out': nrm(ks[18], (DEPTH, M_WIDTH, D_MODEL), M_WIDTH ** -0.5),
        'w_b_out': nrm(ks[19], (DEPTH, A_WIDTH, D_MODEL), A_WIDTH ** -0.5),
        'w_o': nrm(ks[20], (DEPTH, D_MODEL, D_MODEL), D_MODEL ** -0.5),
        'g_norm2': gain(ks[21], (DEPTH, D_MODEL)),
        'w_ffn_in': nrm(ks[22], (DEPTH, D_MODEL, 2 * D_FF), D_MODEL ** -0.5),
        'w_ffn_out': nrm(ks[23], (DEPTH, D_FF, D_MODEL), D_FF ** -0.5),
    }


def reference(x_prompt, x_sample, cache_k, cache_v, cache_kidx, state_C, state_n, state_m, state_conv,
              g_norm1, w_in, b_if, w_conv, b_conv, g_mnorm, g_q, g_k, w_a_out, w_b_out, w_o,
              g_norm2, w_ffn_in, w_ffn_out):
    B = x_prompt.shape[0]
    dt = x_prompt.dtype
    f32 = jnp.float32
    empty_past = (jnp.zeros((B, 0, A_HEADS, A_HEAD_DIM), dt),
                  jnp.zeros((B, 0, A_HEADS, A_HEAD_DIM), dt),
                  jnp.zeros((B, 0, IDX_DIM), dt),
                  jnp.zeros((B, M_HEADS, M_HEAD_DIM, M_HEAD_DIM), f32),
                  jnp.zeros((B, M_HEADS, M_HEAD_DIM), f32),
                  jnp.zeros((B, M_HEADS), f32),
                  jnp.zeros((B, CONV_W - 1, 2 * M_WIDTH), dt))
    yp, ys = x_prompt, x_sample
    new_p, new_s = [], []
    for l in range(DEPTH):
        p = (g_norm1[l], w_in[l], b_if[l], w_conv[l], b_conv[l], g_mnorm[l], g_q[l], g_k[l],
             w_a_out[l], w_b_out[l], w_o[l], g_norm2[l], w_ffn_in[l], w_ffn_out[l])
        yp, sp = layer(yp, empty_past, p)
        ys, ss = layer(ys, (cache_k[l], cache_v[l], cache_kidx[l], state_C[l], state_n[l],
                            state_m[l], state_conv[l]), p)
        new_p.append(sp)
        new_s.append(ss)

    def stk(lst, i):
        return jnp.stack([s[i] for s in lst])

    return (yp, ys,
            stk(new_p, 0), stk(new_p, 1), stk(new_p, 2), stk(new_p, 3), stk(new_p, 4), stk(new_p, 5), stk(new_p, 6),
            stk(new_s, 0), stk(new_s, 1), stk(new_s, 2), stk(new_s, 3), stk(new_s, 4), stk(new_s, 5), stk(new_s, 6))
```

```python
import math
import numpy as np
from contextlib import ExitStack
import concourse.bass as bass
import concourse.mybir as mybir
from concourse.bass_utils import run_bass_kernel_spmd

F32 = mybir.dt.float32
BF16 = mybir.dt.bfloat16
AF = mybir.ActivationFunctionType
ALU = mybir.AluOpType
AX = mybir.AxisListType

D = 1024
SEQ = 8192
NTA = 65
NTOK = NTA * 128
NOWN = 17
NQ = NOWN * 128
NH = NOWN * 131
DIN = 9808
DFF = 2816
EPS = 1e-6
NEG = -1.0e30
O_MQ, O_MK, O_MV, O_MO, O_MI, O_MF = 0, 1024, 2048, 3072, 4096, 4100
O_AQ, O_AK, O_AV, O_IQ, O_IK, O_IW, O_GA, O_GB = 4104, 5128, 6152, 7176, 7688, 7752, 7760, 8784
NCH = 130
NBIS = 18
import os
SUB = int(os.environ.get('KSUB', '99'))


class Stop(Exception):
    pass


CUT = 99


def cut(n):
    if CUT == n:
        raise Stop()


class Buf:
    __slots__ = ("name", "w", "r", "dsem", "dcnt")

    def __init__(self, name):
        self.name = name
        self.w = []
        self.r = []
        self.dsem = None
        self.dcnt = 0


class KB:
    def __init__(self):
        self.nc = bass.Bass("TRN2", target_bir_lowering=False)
        nc = self.nc
        self.eng = {"pe": nc.tensor, "act": nc.scalar, "dve": nc.vector,
                    "pool": nc.gpsimd, "sp": nc.sync}
        self.sem = {e: nc.alloc_semaphore("s_" + e) for e in self.eng}
        self.cnt = {e: 0 for e in self.eng}
        self.seen = {e: {} for e in self.eng}
        self.nsem = 0
        self.out_events = {}
        self.dma_bufs = []
        self.sem_pool = []

    def sb(self, es, name, shape, dt):
        t = es.enter_context(self.nc.sbuf_tensor(name, shape, dt))
        return t, Buf(name)

    def ps(self, es, name, shape, dt):
        t = es.enter_context(self.nc.psum_tensor(name, shape, dt))
        return t, Buf(name)

    def dram(self, name, shape, dt, kind):
        return self.nc.dram_tensor(name, shape, dt, kind=kind).ap()

    def _wait(self, e, ev, war=False):
        sem, val, src = ev
        if src == e and (e == "pe" or war):
            return
        key = id(sem)
        if self.seen[e].get(key, 0) >= val:
            return
        self.eng[e].wait_ge(sem, val)
        self.seen[e][key] = val

    def _deps(self, e, reads, writes):
        for b in reads:
            for ev in b.w:
                self._wait(e, ev)
        for b in writes:
            for ev in b.w:
                self._wait(e, ev)
            for ev in b.r:
                self._wait(e, ev, war=True)

    @staticmethod
    def _addr(b, ev):
        b.r = [x for x in b.r if x[0] is not ev[0]] + [ev]

    def op(self, e, fn, reads=(), writes=()):
        self._deps(e, reads, writes)
        ins = fn(self.eng[e])
        self.cnt[e] += 1
        ins.then_inc(self.sem[e], 1)
        ev = (self.sem[e], self.cnt[e], e)
        for b in writes:
            b.w = [ev]
            b.r = []
        for b in reads:
            self._addr(b, ev)
        return ev

    def dma(self, q, out, in_, sbuf, reads=(), writes=(), is_output=False, **kw):
        self._deps(q, reads, writes)
        if sbuf.dsem is None:
            if self.sem_pool:
                sbuf.dsem, sbuf.dcnt = self.sem_pool.pop()
            else:
                sbuf.dsem = self.nc.alloc_semaphore("d_%d" % self.nsem)
                self.nsem += 1
                sbuf.dcnt = 0
            self.dma_bufs.append(sbuf)
        ins = self.eng[q].dma_start(out=out, in_=in_, **kw)
        sbuf.dcnt += 16
        ins.then_inc(sbuf.dsem, 16)
        ev = (sbuf.dsem, sbuf.dcnt, None)
        for b in writes:
            b.w = [ev]
            b.r = []
        for b in reads:
            self._addr(b, ev)
        if is_output:
            self.out_events[id(sbuf.dsem)] = ev
        return ev

    def barrier(self):
        evs = [(self.sem[e], self.cnt[e], e) for e in self.eng if self.cnt[e] > 0]
        evs += [(b.dsem, b.dcnt, None) for b in self.dma_bufs]
        for e in self.eng:
            for ev in evs:
                if ev[2] == e and e == "pe":
                    continue
                self._wait(e, ev)
        for b in self.dma_bufs:
            self.sem_pool.append((b.dsem, b.dcnt))
            b.dsem = None
        self.dma_bufs = []

    def finish(self):
        for ev in self.out_events.values():
            self._wait("sp", ev)


def mk_ident(k, es, name, dt):
    t, b = k.sb(es, name, [128, 128], dt)
    k.op("pool", lambda e: e.memset(t[:, :], 1.0), writes=[b])
    k.op("pool", lambda e: e.affine_select(
        out=t[:, :], in_=t[:, :], pattern=[[-1, 128]], compare_op=ALU.is_equal,
        fill=0.0, base=0, channel_multiplier=1), reads=[b], writes=[b])
    return t, b


class WStream:
    def __init__(self, k, es, name, nslots, maxcols):
        self.k = k
        self.stg = [k.sb(es, "%s_stg%d" % (name, i), [128, 8, 128], F32) for i in range(2)]
        self.slots = [k.sb(es, "%s_w%d" % (name, i), [128, 8, maxcols], BF16) for i in range(nslots)]
        self.si = 0
        self.ci = 0

    def load(self, wdram, col0, ncols):
        k = self.k
        wt, wB = self.slots[self.si % len(self.slots)]
        self.si += 1
        wv = wdram.rearrange("(kc p) c -> p kc c", p=128)
        off = 0
        while off < ncols:
            m = min(128, ncols - off)
            st, sB = self.stg[self.ci % 2]
            k.dma("sp", st[:, :, 0:m], wv[:, :, col0 + off:col0 + off + m], sB, writes=[sB])
            eng = "dve" if self.ci % 2 == 0 else "pool"
            k.op(eng, lambda e, st=st, m=m, off=off: e.tensor_copy(
                out=wt[:, :, off:off + m], in_=st[:, :, 0:m]), reads=[sB], writes=[wB])
            off += m
            self.ci += 1
        return wt, wB


def head_norm(k, ft, fB, stt, stB, sq, sqB, gain3, gainB, neghalf, neghalfB, nh=8, P=128):
    hd = D // nh
    v3 = lambda t: t[0:P, :].rearrange("p (h d) -> p h d", h=nh)
    k.op("dve", lambda e: e.tensor_tensor(out=sq[0:P, :], in0=ft[0:P, :], in1=ft[0:P, :], op=ALU.mult),
         reads=[fB], writes=[sqB])
    k.op("dve", lambda e: e.tensor_reduce(out=stt[0:P, 0:nh], in_=v3(sq), axis=AX.X, op=ALU.add),
         reads=[sqB], writes=[stB])
    k.op("dve", lambda e: e.tensor_scalar(out=stt[0:P, 0:nh], in0=stt[0:P, 0:nh], scalar1=1.0 / hd,
                                          scalar2=EPS, op0=ALU.mult, op1=ALU.add), reads=[stB], writes=[stB])
    k.op("pool", lambda e: e.tensor_tensor(out=stt[0:P, 8:8 + nh], in0=stt[0:P, 0:nh], in1=neghalf[0:P, 0:nh],
                                           op=ALU.pow), reads=[stB, neghalfB], writes=[stB])
    k.op("dve", lambda e: e.tensor_tensor(out=v3(ft), in0=v3(ft),
                                          in1=stt[0:P, 8:8 + nh].unsqueeze(2).to_broadcast([P, nh, hd]),
                                          op=ALU.mult), reads=[fB, stB], writes=[fB])
    k.op("dve", lambda e: e.tensor_tensor(out=v3(ft), in0=v3(ft), in1=gain3, op=ALU.mult),
         reads=[fB, gainB], writes=[fB])


def phaseA(k, T):
    es = ExitStack()
    xb, w_in = T["xb"], T["w_in"]
    ident, identB = mk_ident(k, es, "identA", BF16)
    identf, identfB = mk_ident(k, es, "identfA", F32)
    neghalf, neghalfB = k.sb(es, "neghalfA", [128, 8], F32)
    k.op("pool", lambda e: e.memset(neghalf[:, :], -0.5), writes=[neghalfB])
    g1b, g1bB = k.sb(es, "g1b", [128, D], F32)
    k.dma("sp", g1b[:, :], T["g_norm1"][0:1, :].to_broadcast([128, D]), g1bB, writes=[g1bB])
    gkb, gkbB = k.sb(es, "gkb", [128, 128], F32)
    k.dma("sp", gkb[:, :], T["g_k"][0:1, :].to_broadcast([128, 128]), gkbB, writes=[gkbB])
    wck, wckB = k.sb(es, "wck", [128, 8, 4], F32)
    for j in range(4):
        k.dma("sp", wck[:, :, j], T["w_conv"][j:j + 1, 1024:2048].rearrange("o (g p) -> p (o g)", p=128),
              wckB, writes=[wckB], allow_slow_non_contiguous=True)
    bck, bckB = k.sb(es, "bck", [128, 8], F32)
    k.dma("sp", bck[:, :], T["b_conv"][0:1, 1024:2048].rearrange("o (g p) -> p (o g)", p=128),
          bckB, writes=[bckB], allow_slow_non_contiguous=True)
    dgk, dgkB = k.sb(es, "dgk", [128, 8, 4, 128], BF16)
    for g in range(8):
        for j in range(4):
            k.op("dve", lambda e, g=g, j=j: e.tensor_scalar(
                out=dgk[:, g, j, :], in0=identf[:, :], scalar1=wck[:, g, j:j + 1], scalar2=None,
                op0=ALU.mult), reads=[identfB, wckB], writes=[dgkB])

    NWA = 4168
    WA, WAB = k.sb(es, "WA", [128, 8, NWA], BF16)
    xs = [k.sb(es, "xs%d" % i, [128, D], F32) for i in range(3)]
    wa_src = [(O_MK, 1024), (O_MV, 1024), (O_AK, 1024), (O_AV, 1024), (O_MI, 8), (O_IK, 64)]
    w_view = w_in.rearrange("(kc p) c -> p kc c", p=128)
    dst = 0
    ci = 0
    WAparts = []
    for (src, n) in wa_src:
        off = 0
        while off < n:
            m = min(128, n - off)
            xt, xB = xs[ci % 2]
            stg = xt[:, 0:8 * 128].rearrange("p (k n) -> p k n", k=8)
            k.dma("sp", stg[:, :, 0:m], w_view[:, :, src + off:src + off + m], xB, writes=[xB])
            pb = Buf("WAp%d" % ci)
            eng = "dve" if ci % 2 == 0 else "pool"
            k.op(eng, lambda e, stg=stg, m=m, dst=dst: e.tensor_copy(
                out=WA[:, :, dst:dst + m], in_=stg[:, :, 0:m]), reads=[xB], writes=[pb])
            WAparts.append(pb)
            dst += m
            off += m
            ci += 1
    C_MK, C_MV, C_AK, C_AV, C_G = 0, 1024, 2048, 3072, 4096

    hb = [k.sb(es, "hb%d" % i, [128, D], BF16) for i in range(2)]
    ss = [k.sb(es, "ss%d" % i, [128, 4], F32) for i in range(2)]
    hT = [k.sb(es, "hT%d" % i, [128, 8, 512], BF16) for i in range(2)]
    kf = [k.sb(es, "kf%d" % i, [128, D], F32) for i in range(2)]
    vf = [k.sb(es, "vf%d" % i, [128, D], F32) for i in range(2)]
    sq, sqB = k.sb(es, "sqA", [128, D], F32)
    kn = [k.sb(es, "kn%d" % i, [128, D], BF16) for i in range(2)]
    vb = [k.sb(es, "vb%d" % i, [128, D], BF16) for i in range(2)]
    mvb = [k.sb(es, "mvb%d" % i, [128, D], BF16) for i in range(2)]
    sm = [k.sb(es, "sm%d" % i, [128, 72], F32) for i in range(2)]
    kid = [k.sb(es, "kid%d" % i, [128, 128], BF16) for i in range(2)]
    kst = [k.sb(es, "kst%d" % i, [128, 16], F32) for i in range(2)]
    KTs, KTsB = k.sb(es, "KTs", [128, 8, 512], BF16)
    kiTs, kiTsB = k.sb(es, "kiTs", [128, 512], BF16)
    pre, preB = k.sb(es, "pre", [128, 8, 3 + 512], BF16)
    pres, presB = k.sb(es, "pres", [128, 8, 2, 67], BF16)
    kTs, kTsB = k.sb(es, "kTs", [128, 8, 512], BF16)
    hstg, hstgB = k.sb(es, "hstg", [128, 8, 2, 3], F32)
    cvo, cvoB = k.sb(es, "cvo", [128, 3, 8, 3], F32)
    psT = [k.ps(es, "psT%d" % i, [128, 1024], BF16) for i in range(2)]
    psM = [k.ps(es, "psM%d" % i, [128, 512], F32) for i in range(6)]
    pmi = [0]

    def next_ps():
        p = psM[pmi[0] % 6]
        pmi[0] += 1
        return p

    k.op("pool", lambda e: e.memset(pre[:, :, 0:3], 0.0), writes=[preB])
    for s_ in range(2):
        for r_ in range(3):
            k.dma("sp", hstg[:, :, s_, r_],
                  T["state_conv"][s_, r_:r_ + 1, 1024:2048].rearrange("o (g p) -> p (o g)", p=128),
                  hstgB, writes=[hstgB], allow_slow_non_contiguous=True)
    k.op("dve", lambda e: e.tensor_copy(out=pres[:, :, :, 0:3], in_=hstg[:, :, :, :]),
         reads=[hstgB], writes=[presB])

    tiles = [(bi, ti) for bi in range(17) for ti in range(4 if bi < 16 else 1)]

    def ctx(idx):
        bi, ti = tiles[idx]
        Tt = bi * 4 + ti
        return bi, ti, Tt, xs[idx % 3], hb[idx % 2], ss[idx % 2], hT[bi % 2]

    def load_x(idx):
        bi, ti, Tt, (xt, xB), _, _, _ = ctx(idx)
        k.dma("sp", xt[:, :], xb[Tt * 128:(Tt + 1) * 128, :], xB, writes=[xB])

    def prologue_a(idx):
        bi, ti, Tt, (xt, xB), (hbt, hbB), (sst, ssB), (hTt, hTB) = ctx(idx)
        k.op("act", lambda e: e.activation(out=hbt[:, :], in_=xt[:, :], func=AF.Square,
                                           accum_out=sst[:, 0:1]), reads=[xB], writes=[hbB, ssB])
        k.op("dve", lambda e: e.tensor_scalar(out=sst[:, 1:2], in0=sst[:, 0:1], scalar1=1.0 / D,
                                              scalar2=EPS, op0=ALU.mult, op1=ALU.add),
             reads=[ssB], writes=[ssB])
        k.op("pool", lambda e: e.tensor_tensor(out=sst[:, 2:3], in0=sst[:, 1:2], in1=neghalf[:, 0:1],
                                               op=ALU.pow), reads=[ssB, neghalfB], writes=[ssB])
        k.op("dve", lambda e: e.scalar_tensor_tensor(out=hbt[:, :], in0=xt[:, :], scalar=sst[:, 2:3],
                                                     in1=g1b[:, :], op0=ALU.mult, op1=ALU.mult),
             reads=[xB, ssB, g1bB], writes=[hbB])

    def prologue_b(idx):
        bi, ti, Tt, (xt, xB), (hbt, hbB), (sst, ssB), (hTt, hTB) = ctx(idx)
        pT, pTB = psT[0]
        for kc in range(8):
            k.op("pe", lambda e, kc=kc: e.transpose(out=pT[:, kc * 128:(kc + 1) * 128],
                                                    in_=hbt[:, kc * 128:(kc + 1) * 128],
                                                    identity=ident[:, :]),
                 reads=[hbB, identB], writes=[pTB])
        k.op("act", lambda e: e.activation(
            out=hTt[:, :, ti * 128:(ti + 1) * 128],
            in_=pT[:, :].rearrange("p (k n) -> p k n", k=8), func=AF.Copy),
            reads=[pTB], writes=[hTB])


    def tile_body(idx):
        bi, ti, Tt, (xt, xB), (hbt, hbB), (sst, ssB), (hTt, hTB) = ctx(idx)
        s = Tt % 2
        def tokmm(c0, n):
            pm, pmB = next_ps()
            for kc in range(8):
                k.op("pe", lambda e, kc=kc: e.matmul(
                    pm[:, 0:n], lhsT=hTt[:, kc, ti * 128:(ti + 1) * 128],
                    rhs=WA[:, kc, c0:c0 + n], start=(kc == 0), stop=(kc == 7)),
                    reads=[hTB] + (WAparts if kc == 0 else []), writes=[pmB])
            return pm, pmB

        mvt, mvB = mvb[s]
        for hh in range(2):
            pm, pmB = tokmm(C_MV + hh * 512, 512)
            k.op("act", lambda e, hh=hh, pm=pm: e.activation(
                out=mvt[:, hh * 512:(hh + 1) * 512], in_=pm[:, :], func=AF.Copy),
                reads=[pmB], writes=[mvB])
        k.dma("sp", T["mv_d"][Tt * 128:(Tt + 1) * 128, :], mvt[:, :], mvB, reads=[mvB])
        if idx > 0 and tiles[idx - 1][0] == bi:
            tile_tail(idx - 1)
        kft, kfB = kf[s]
        kstt, kstB = kst[s]
        for hh in range(2):
            pm, pmB = tokmm(C_AK + hh * 512, 512)
            k.op("act", lambda e, hh=hh, pm=pm: e.activation(
                out=kft[:, hh * 512:(hh + 1) * 512], in_=pm[:, :], func=AF.Copy),
                reads=[pmB], writes=[kfB])
        head_norm(k, kft, kfB, kstt, kstB, sq, sqB, gkb[:, :].unsqueeze(1).to_broadcast([128, 8, 128]), gkbB, neghalf, neghalfB)
        k.dma("sp", T["k_out"][Tt * 128:(Tt + 1) * 128, :], kft[:, :], kfB, reads=[kfB], is_output=True)
        knt, knB = kn[s]
        k.op("pool", lambda e: e.tensor_copy(out=knt[:, :], in_=kft[:, :]), reads=[kfB], writes=[knB])
        if idx + 1 < len(tiles):
            prologue_b(idx + 1)
        vft, vfB = vf[s]
        for hh in range(2):
            pm, pmB = tokmm(C_AV + hh * 512, 512)
            k.op("act", lambda e, hh=hh, pm=pm: e.activation(
                out=vft[:, hh * 512:(hh + 1) * 512], in_=pm[:, :], func=AF.Copy),
                reads=[pmB], writes=[vfB])
        k.dma("sp", T["v_out"][Tt * 128:(Tt + 1) * 128, :], vft[:, :], vfB, reads=[vfB], is_output=True)
        vbt, vbB = vb[s]
        k.op("pool", lambda e: e.tensor_copy(out=vbt[:, :], in_=vft[:, :]), reads=[vfB], writes=[vbB])
        k.dma("sp", T["V_d"][Tt * 128:(Tt + 1) * 128, :], vbt[:, :], vbB, reads=[vbB])
        smt, smB = sm[s]
        pm, pmB = tokmm(C_G, 72)
        k.op("act", lambda e, pm=pm: e.activation(out=smt[:, :], in_=pm[:, 0:72], func=AF.Copy),
             reads=[pmB], writes=[smB])
        k.dma("sp", T["g_d"][Tt * 128:(Tt + 1) * 128, :], smt[:, 0:8], smB, reads=[smB])
        k.dma("sp", T["ki_out"][Tt * 128:(Tt + 1) * 128, :], smt[:, 8:72], smB, reads=[smB],
              is_output=True)
        kidt, kidB = kid[s]
        k.op("dve", lambda e: e.tensor_copy(
            out=kidt[:, :].rearrange("p (a d) -> p a d", a=2),
            in_=smt[:, 8:72].unsqueeze(1).to_broadcast([128, 2, 64])), reads=[smB], writes=[kidB])

    def tile_tail(idx):
        bi, ti, Tt, (xt, xB), (hbt, hbB), (sst, ssB), (hTt, hTB) = ctx(idx)
        s = Tt % 2
        knt, knB = kn[s]
        kidt, kidB = kid[s]
        pT2, pT2B = psT[1]
        for h in range(8):
            k.op("pe", lambda e, h=h: e.transpose(out=pT2[:, h * 128:(h + 1) * 128],
                                                  in_=knt[:, h * 128:(h + 1) * 128],
                                                  identity=ident[:, :]),
                 reads=[knB, identB], writes=[pT2B])
        k.op("act", lambda e: e.activation(
            out=KTs[:, :, ti * 128:(ti + 1) * 128],
            in_=pT2[:, :].rearrange("p (k n) -> p k n", k=8), func=AF.Copy),
            reads=[pT2B], writes=[KTsB])
        pT2, pT2B = psT[1]
        k.op("pe", lambda e: e.transpose(out=pT2[:, 0:128], in_=kidt[:, :], identity=ident[:, :]),
             reads=[kidB, identB], writes=[pT2B])
        k.op("act", lambda e: e.activation(out=kiTs[:, ti * 128:(ti + 1) * 128], in_=pT2[:, 0:128],
                                           func=AF.Copy), reads=[pT2B], writes=[kiTsB])


    def block_epilogue(bi):
        ntile = 4 if bi < 16 else 1
        ntok = ntile * 128
        tok0 = bi * 512
        hTt, hTB = hT[bi % 2]
        if bi < 16:
            k.dma("sp", T["hTo_d"][:, :, bi * 131:(bi + 1) * 131], hTt[:, :, 381:512], hTB, reads=[hTB])
        else:
            k.dma("sp", T["hTo_d"][:, :, 16 * 131 + 3:17 * 131], hTt[:, :, 0:128], hTB, reads=[hTB])
            k.op("pool", lambda e: e.memset(pre[:, :, 0:3], 0.0), reads=[preB], writes=[preB])
            k.dma("sp", T["hTo_d"][:, :, 16 * 131:16 * 131 + 3], pre[:, :, 0:3], preB, reads=[preB])
        k.dma("sp", T["KT_d"][:, :, tok0:tok0 + ntok].rearrange("h d t -> d h t"), KTs[:, :, 0:ntok], KTsB,
              reads=[KTsB])
        k.dma("sp", T["kiT_d"][:, tok0:tok0 + ntok], kiTs[:, 0:ntok], kiTsB, reads=[kiTsB])
        for cg in range(8):
            pm, pmB = next_ps()
            for kc in range(8):
                k.op("pe", lambda e, kc=kc: e.matmul(
                    pm[:, 0:ntok], lhsT=WA[:, kc, C_MK + cg * 128:C_MK + (cg + 1) * 128],
                    rhs=hTt[:, kc, 0:ntok], start=(kc == 0), stop=(kc == 7)),
                    reads=[hTB] + (WAparts if kc == 0 else []), writes=[pmB])
            if bi < 16:
                k.op("act", lambda e: e.activation(out=pre[:, cg, 3:3 + ntok], in_=pm[:, 0:ntok],
                                                   func=AF.Copy), reads=[pmB], writes=[preB])
                if bi == 15:
                    k.op("act", lambda e: e.activation(out=cvo[:, 0, cg, :], in_=pm[:, 509:512], func=AF.Copy),
                         reads=[pmB], writes=[cvoB])
            else:
                k.op("act", lambda e: e.activation(
                    out=pres[:, cg, :, 3:67], in_=pm[:, 0:128].rearrange("p (s t) -> p s t", s=2),
                    func=AF.Copy), reads=[pmB], writes=[presB])
                k.op("act", lambda e: e.activation(
                    out=cvo[:, 1:3, cg, :], in_=pm[:, 0:128].rearrange("p (s t) -> p s t", s=2)[:, :, 61:64],
                    func=AF.Copy), reads=[pmB], writes=[cvoB])
        for cg in range(8):
            pm, pmB = next_ps()
            if bi < 16:
                for j in range(4):
                    k.op("pe", lambda e, j=j: e.matmul(
                        pm[:, 0:ntok], lhsT=dgk[:, cg, j, :], rhs=pre[:, cg, j:j + ntok],
                        start=(j == 0), stop=(j == 3)), reads=[dgkB, preB], writes=[pmB])
            else:
                for sq_i in range(2):
                    for j in range(4):
                        k.op("pe", lambda e, j=j, sq_i=sq_i: e.matmul(
                            pm[:, sq_i * 64:(sq_i + 1) * 64], lhsT=dgk[:, cg, j, :],
                            rhs=pres[:, cg, sq_i, j:j + 64], start=(j == 0), stop=(j == 3)),
                            reads=[dgkB, presB], writes=[pmB])
            k.op("act", lambda e: e.activation(out=kTs[:, cg, 0:ntok], in_=pm[:, 0:ntok], func=AF.Silu,
                                               bias=bck[:, cg:cg + 1]), reads=[pmB, bckB], writes=[kTsB])
        k.dma("sp", T["kT_d"][:, :, tok0:tok0 + ntok].rearrange("g p t -> p g t"), kTs[:, :, 0:ntok], kTsB,
              reads=[kTsB])
        if bi < 15:
            k.op("dve", lambda e: e.tensor_copy(out=pre[:, :, 0:3], in_=pre[:, :, 512:515]),
                 reads=[preB], writes=[preB])

    load_x(0)
    load_x(1)
    prologue_a(0)
    prologue_b(0)
    for idx in range(len(tiles)):
        if idx + 2 < len(tiles):
            load_x(idx + 2)
        if idx + 1 < len(tiles):
            prologue_a(idx + 1)
        tile_body(idx)
        bi, ti = tiles[idx]
        if idx + 1 == len(tiles) or tiles[idx + 1][0] != bi:
            tile_tail(idx)
            block_epilogue(bi)
    for a_ in range(3):
        k.dma("pool", T["cvk_out"][a_].rearrange("(g p) r -> p g r", p=128), cvo[:, a_, :, :], cvoB,
              reads=[cvoB], is_output=True, allow_slow_non_contiguous=True)
    k.barrier()
    es.close()


def phaseB(k, T):
    es = ExitStack()
    w_in = T["w_in"]
    ident, identB = mk_ident(k, es, "identB", BF16)
    identf, identfB = mk_ident(k, es, "identfB", F32)
    neghalf, neghalfB = k.sb(es, "neghalfB", [128, 8], F32)
    k.op("pool", lambda e: e.memset(neghalf[:, :], -0.5), writes=[neghalfB])
    hTo, hToB = k.sb(es, "hTo", [128, 8, NH], BF16)
    k.dma("sp", hTo[:, :, :], T["hTo_d"][:, :, :], hToB, writes=[hToB])
    ws = WStream(k, es, "wsB", 2, 1024)
    big, bigB = k.sb(es, "bigB", [128, 8, NQ], BF16)
    psM = [k.ps(es, "psMB%d" % i, [128, 512], F32) for i in range(6)]
    psT = [k.ps(es, "psTB%d" % i, [128, 1024], BF16) for i in range(2)]
    pmi = [0]

    def next_ps():
        p = psM[pmi[0] % 6]
        pmi[0] += 1
        return p

    wqk, wqkB = k.sb(es, "wqk", [128, 8, 4], F32)
    for j in range(4):
        k.dma("sp", wqk[:, :, j], T["w_conv"][j:j + 1, 0:1024].rearrange("o (g p) -> p (o g)", p=128),
              wqkB, writes=[wqkB], allow_slow_non_contiguous=True)
    bcq, bcqB = k.sb(es, "bcq", [128, 8], F32)
    k.dma("sp", bcq[:, :], T["b_conv"][0:1, 0:1024].rearrange("o (g p) -> p (o g)", p=128),
          bcqB, writes=[bcqB], allow_slow_non_contiguous=True)
    dgq, dgqB = k.sb(es, "dgq", [128, 8, 4, 128], BF16)
    for g in range(8):
        for j in range(4):
            k.op("dve", lambda e, g=g, j=j: e.tensor_scalar(
                out=dgq[:, g, j, :], in0=identf[:, :], scalar1=wqk[:, g, j:j + 1], scalar2=None,
                op0=ALU.mult), reads=[identfB, wqkB], writes=[dgqB])
    hq, hqB = k.sb(es, "hq", [128, 8, 2, 3], F32)
    for s_ in range(2):
        for r_ in range(3):
            k.dma("sp", hq[:, :, s_, r_],
                  T["state_conv"][s_, r_:r_ + 1, 0:1024].rearrange("o (g p) -> p (o g)", p=128),
                  hqB, writes=[hqB], allow_slow_non_contiguous=True)
    preq = [k.sb(es, "preq%d" % i, [128, 2240], BF16) for i in range(2)]
    prb = [k.sb(es, "prb%d" % i, [128, 96], BF16) for i in range(2)]
    cvq, cvqB = k.sb(es, "cvq", [128, 3, 8, 3], F32)
    cut(1)
    W, WB = ws.load(w_in, O_MQ, 1024)
    cut(2)
    for cg in range(8):
        pq, pqB = preq[cg % 2]
        pb_, pbB = prb[cg % 2]
        for grp in range(5):
            n = min(512, NH - grp * 512)
            pm, pmB = next_ps()
            for kc in range(8):
                k.op("pe", lambda e, kc=kc: e.matmul(
                    pm[:, 0:n], lhsT=W[:, kc, cg * 128:(cg + 1) * 128],
                    rhs=hTo[:, kc, grp * 512:grp * 512 + n], start=(kc == 0), stop=(kc == 7)),
                    reads=[hToB, WB], writes=[pmB])
            if cg == 0 and grp == 0:
                cut(31)
            k.op("act", lambda e: e.activation(out=pq[:, grp * 512:grp * 512 + n], in_=pm[:, 0:n],
                                               func=AF.Copy), reads=[pmB], writes=[pqB])
            if cg == 0 and grp == 0:
                cut(32)
            if cg == 0 and grp == 3:
                cut(33)
            if cg == 0 and grp == 4:
                cut(34)
            if grp == 4:
                for a_, o_ in ((0, 15 * 131 + 128 - 2048), (1, 16 * 131 + 3 + 61 - 2048),
                               (2, 16 * 131 + 3 + 125 - 2048))[0:int(os.environ.get("NCVQ", "3"))]:
                    k.op("act", lambda e, a_=a_, o_=o_: e.activation(out=cvq[:, a_, cg, :],
                                                                     in_=pm[:, o_:o_ + 3], func=AF.Copy),
                         reads=[pmB], writes=[cvqB])
        if cg == 0:
            cut(3)
        k.op("dve", lambda e: e.tensor_copy(out=pq[:, 16 * 131:16 * 131 + 3], in_=hq[:, cg, 0, :]),
             reads=[hqB], writes=[pqB])
        k.op("dve", lambda e: e.tensor_copy(out=pb_[:, 0:3], in_=hq[:, cg, 1, :]), reads=[hqB], writes=[pbB])
        k.op("dve", lambda e: e.tensor_copy(out=pb_[:, 3:67], in_=pq[:, 16 * 131 + 67:16 * 131 + 131]),
             reads=[pqB], writes=[pbB])
        if cg == 0:
            cut(4)
        for g0 in range(0, 17, 4):
            ng = min(4, 17 - g0)
            pm, pmB = next_ps()
            for gi in range(g0, g0 + ng):
                if gi < 16:
                    for j in range(4):
                        k.op("pe", lambda e, j=j, gi=gi: e.matmul(
                            pm[:, (gi - g0) * 128:(gi - g0 + 1) * 128], lhsT=dgq[:, cg, j, :],
                            rhs=pq[:, gi * 131 + j:gi * 131 + j + 128], start=(j == 0), stop=(j == 3)),
                            reads=[dgqB, pqB], writes=[pmB])
                else:
                    for j in range(4):
                        k.op("pe", lambda e, j=j: e.matmul(
                            pm[:, 0:64], lhsT=dgq[:, cg, j, :], rhs=pq[:, 16 * 131 + j:16 * 131 + j + 64],
                            start=(j == 0), stop=(j == 3)), reads=[dgqB, pqB], writes=[pmB])
                    for j in range(4):
                        k.op("pe", lambda e, j=j: e.matmul(
                            pm[:, 64:128], lhsT=dgq[:, cg, j, :], rhs=pb_[:, j:j + 64],
                            start=(j == 0), stop=(j == 3)), reads=[dgqB, pbB], writes=[pmB])
            k.op("act", lambda e: e.activation(out=big[:, cg, g0 * 128:(g0 + ng) * 128], in_=pm[:, 0:ng * 128],
                                               func=AF.Silu, bias=bcq[:, cg:cg + 1]),
                 reads=[pmB, bcqB], writes=[bigB])
    cut(5)
    k.dma("pool", T["qT_d"][:, :, :], big[:, :, :], bigB, reads=[bigB])
    cut(6)
    for a_ in range(3):
        k.dma("pool", T["cvq_out"][a_].rearrange("(g p) r -> p g r", p=128), cvq[:, a_, :, :], cvqB,
              reads=[cvqB], is_output=True, allow_slow_non_contiguous=True)

    if SUB <= 1:
        k.barrier()
        es.close()
        return
    sob = [k.sb(es, "sob%d" % i, [128, D], BF16) for i in range(2)]
    W, WB = ws.load(w_in, O_MO, 1024)
    for gi in range(NOWN):
        st, sB = sob[gi % 2]
        for hh in range(2):
            pm, pmB = next_ps()
            for kc in range(8):
                k.op("pe", lambda e, kc=kc: e.matmul(
                    pm[:, :], lhsT=hTo[:, kc, gi * 131 + 3:gi * 131 + 131],
                    rhs=W[:, kc, hh * 512:(hh + 1) * 512], start=(kc == 0), stop=(kc == 7)),
                    reads=[hToB, WB], writes=[pmB])
            k.op("act", lambda e: e.activation(out=st[:, hh * 512:(hh + 1) * 512], in_=pm[:, :],
                                               func=AF.Sigmoid), reads=[pmB], writes=[sB])
        k.dma("pool", T["so_d"][gi * 128:(gi + 1) * 128, :], st[:, :], sB, reads=[sB])

    if SUB <= 2:
        k.barrier()
        es.close()
        return
    gqb, gqbB = k.sb(es, "gqb", [128, 128], F32)
    k.dma("sp", gqb[:, :], T["g_q"][0:1, :].to_broadcast([128, 128]), gqbB, writes=[gqbB])
    k.op("dve", lambda e: e.tensor_scalar(out=gqb[:, :], in0=gqb[:, :], scalar1=128.0 ** -0.5, scalar2=None,
                                          op0=ALU.mult), reads=[gqbB], writes=[gqbB])
    qf = [k.sb(es, "qf%d" % i, [128, D], F32) for i in range(2)]
    qst = [k.sb(es, "qst%d" % i, [128, 16], F32) for i in range(2)]
    sq, sqB = k.sb(es, "sqB", [128, D], F32)
    qn = [k.sb(es, "qn%d" % i, [128, D], BF16) for i in range(2)]
    W, WB = ws.load(w_in, O_AQ, 1024)
    for gi in range(NOWN):
        qt, qB = qf[gi % 2]
        stt, stB = qst[gi % 2]
        for hh in range(2):
            pm, pmB = next_ps()
            for kc in range(8):
                k.op("pe", lambda e, kc=kc: e.matmul(
                    pm[:, :], lhsT=hTo[:, kc, gi * 131 + 3:gi * 131 + 131],
                    rhs=W[:, kc, hh * 512:(hh + 1) * 512], start=(kc == 0), stop=(kc == 7)),
                    reads=[hToB, WB], writes=[pmB])
            k.op("act", lambda e: e.activation(out=qt[:, hh * 512:(hh + 1) * 512], in_=pm[:, :],
                                               func=AF.Copy), reads=[pmB], writes=[qB])
        head_norm(k, qt, qB, stt, stB, sq, sqB, gqb[:, :].unsqueeze(1).to_broadcast([128, 8, 128]), gqbB, neghalf, neghalfB)
        qnt, qnB = qn[gi % 2]
        k.op("pool", lambda e: e.tensor_copy(out=qnt[:, :], in_=qt[:, :]), reads=[qB], writes=[qnB])
        pT, pTB = psT[gi % 2]
        for h in range(8):
            k.op("pe", lambda e, h=h: e.transpose(out=pT[:, h * 128:(h + 1) * 128],
                                                  in_=qnt[:, h * 128:(h + 1) * 128], identity=ident[:, :]),
                 reads=[qnB, identB], writes=[pTB])
        k.op("act", lambda e: e.activation(out=big[:, :, gi * 128:(gi + 1) * 128],
                                           in_=pT[:, :].rearrange("p (k n) -> p k n", k=8), func=AF.Copy),
             reads=[pTB], writes=[bigB])
    k.dma("pool", T["QT_d"][:, :, :], big[:, :, :], bigB, reads=[bigB])

    if SUB <= 3:
        k.barrier()
        es.close()
        return
    def feat_major(W, WB, ncg, func, scale=1.0):
        for cg in range(ncg):
            for g0 in range(0, 17, 4):
                ng = min(4, 17 - g0)
                pm, pmB = next_ps()
                for gi in range(g0, g0 + ng):
                    for kc in range(8):
                        k.op("pe", lambda e, kc=kc, gi=gi: e.matmul(
                            pm[:, (gi - g0) * 128:(gi - g0 + 1) * 128],
                            lhsT=W[:, kc, cg * 128:(cg + 1) * 128],
                            rhs=hTo[:, kc, gi * 131 + 3:gi * 131 + 131], start=(kc == 0), stop=(kc == 7)),
                            reads=[hToB, WB], writes=[pmB])
                k.op("act", lambda e: e.activation(out=big[:, cg, g0 * 128:(g0 + ng) * 128],
                                                   in_=pm[:, 0:ng * 128], func=func),
                     reads=[pmB], writes=[bigB])

    W, WB = ws.load(w_in, O_IQ, 512)
    feat_major(W, WB, 4, AF.Copy)
    k.dma("pool", T["qiT_d"][:, :, :], big[:, 0:4, :], bigB, reads=[bigB])

    if SUB <= 4:
        k.barrier()
        es.close()
        return
    iwa, iwaB = k.sb(es, "iwa", [128, NOWN, 8], F32)
    W, WB = ws.load(w_in, O_IW, 8)
    for gi in range(NOWN):
        pm, pmB = next_ps()
        for kc in range(8):
            k.op("pe", lambda e, kc=kc: e.matmul(
                pm[:, 0:8], lhsT=hTo[:, kc, gi * 131 + 3:gi * 131 + 131], rhs=W[:, kc, 0:8],
                start=(kc == 0), stop=(kc == 7)), reads=[hToB, WB], writes=[pmB])
        k.op("act", lambda e: e.activation(out=iwa[:, gi, :], in_=pm[:, 0:8], func=AF.Copy,
                                           scale=512.0 ** -0.5), reads=[pmB], writes=[iwaB])
    k.dma("pool", T["iw_d"][:, :], iwa[:, :, :].rearrange("p g h -> p (g h)"), iwaB, reads=[iwaB])

    if SUB <= 5:
        k.barrier()
        es.close()
        return
    for (o_, dname) in ((O_GA, "gaT_d"), (O_GB, "gbT_d")):
        W, WB = ws.load(w_in, o_, 1024)
        feat_major(W, WB, 8, AF.Sigmoid)
        k.dma("pool", T[dname][:, :, :], big[:, :, :], bigB, reads=[bigB])
    k.barrier()
    es.close()


def phaseC(k, T):
    es = ExitStack()
    ident, identB = mk_ident(k, es, "identC", BF16)
    identf, identfB = mk_ident(k, es, "identfC", F32)
    neghalf, neghalfB = k.sb(es, "neghalfC", [128, 8], F32)
    k.op("pool", lambda e: e.memset(neghalf[:, :], -0.5), writes=[neghalfB])
    ones64, ones64B = k.sb(es, "ones64", [128, 64], F32)
    k.op("pool", lambda e: e.memset(ones64[:, :], 1.0), writes=[ones64B])
    tri, triB = k.sb(es, "tri16", [64, 64], F32)
    k.op("pool", lambda e: e.memset(tri[:, :], 1.0 / 16.0), writes=[triB])
    k.op("pool", lambda e: e.affine_select(out=tri[:, :], in_=tri[:, :], pattern=[[1, 64]],
                                           compare_op=ALU.is_ge, fill=0.0, base=0, channel_multiplier=-1),
         reads=[triB], writes=[triB])
    bifb, bifbB = k.sb(es, "bifb", [128, 8], F32)
    k.dma("sp", bifb[:, :], T["b_if"][0:1, :].to_broadcast([128, 8]), bifbB, writes=[bifbB])
    gmnb, gmnbB = k.sb(es, "gmnb", [64, D], F32)
    k.dma("sp", gmnb[:, :], T["g_mnorm"][0:1, :].to_broadcast([64, D]), gmnbB, writes=[gmnbB])

    pX = k.ps(es, "pXC", [128, 512], F32)
    pK = k.ps(es, "pKC", [128, 1024], BF16)
    pC = [k.ps(es, "pCC%d" % i, [128, 2, 256], F32) for i in range(2)]
    pN = pX
    pS = k.ps(es, "pSC", [64, 256], F32)
    pI = k.ps(es, "pIC", [64, 2, 512], F32)
    pA = k.ps(es, "pAC", [64, 512], F32)

    def gate_prep(P, tag, g_rows, vm_rows, mp_tile, mp_B, row0):
        G, GB = k.sb(es, "G" + tag, [P, 64, 8], F32)
        k.dma("sp", G[:, :, :], T["g_d"][g_rows[0]:g_rows[1], :].rearrange("(c l) q -> c l q", l=64), GB,
              writes=[GB])
        vm, vmB = k.sb(es, "vm" + tag, [P, 64], F32)
        k.dma("sp", vm[:, :], T["tokvalid"][vm_rows[0]:vm_rows[1]].rearrange("(c l) -> c l", l=64), vmB,
              writes=[vmB])
        pen, penB = k.sb(es, "pen" + tag, [P, 64], F32)
        k.op("dve", lambda e: e.tensor_scalar(out=pen[:, :], in0=vm[:, :], scalar1=-1.0, scalar2=1.0e30,
                                              op0=ALU.add, op1=ALU.mult), reads=[vmB], writes=[penB])
        k.op("dve", lambda e: e.tensor_tensor(out=G[:, :, :], in0=G[:, :, :],
                                              in1=bifb[0:P, :].unsqueeze(1).to_broadcast([P, 64, 8]),
                                              op=ALU.add), reads=[GB, bifbB], writes=[GB])
        E, EB = k.sb(es, "E" + tag, [P, 64, 4], F32)
        k.op("act", lambda e: e.activation(out=E[:, :, :], in_=G[:, :, 4:8], func=AF.Exp, scale=-1.0),
             reads=[GB], writes=[EB])
        k.op("act", lambda e: e.activation(out=E[:, :, :], in_=E[:, :, :], func=AF.Ln, bias=1.0, scale=1.0),
             reads=[EB], writes=[EB])
        vm3 = vm[:, :].unsqueeze(2).to_broadcast([P, 64, 4])
        LF, LFB = k.sb(es, "LF" + tag, [P, 64, 4], F32)
        k.op("dve", lambda e: e.scalar_tensor_tensor(out=LF[:, :, :], in0=E[:, :, :], scalar=-1.0, in1=vm3,
                                                     op0=ALU.mult, op1=ALU.mult), reads=[EB, vmB], writes=[LFB])
        LI, LIB = k.sb(es, "LI" + tag, [P, 64, 4], F32)
        k.op("dve", lambda e: e.tensor_tensor(out=LI[:, :, :], in0=G[:, :, 0:4], in1=vm3, op=ALU.mult),
             reads=[GB, vmB], writes=[LIB])
        k.op("dve", lambda e: e.tensor_tensor(out=LI[:, :, :], in0=LI[:, :, :],
                                              in1=pen[:, :].unsqueeze(2).to_broadcast([P, 64, 4]), op=ALU.add),
             reads=[LIB, penB], writes=[LIB])
        Bc, BcB = k.sb(es, "Bc" + tag, [P, 64, 4], F32)
        for h in range(4):
            k.op("dve", lambda e, h=h: e.tensor_tensor_scan(out=Bc[:, :, h], data0=ones64[0:P, :],
                                                            data1=LF[:, :, h], initial=0.0, op0=ALU.mult,
                                                            op1=ALU.add), reads=[LFB, ones64B], writes=[BcB])
        Cc, CcB = k.sb(es, "Cc" + tag, [P, 64, 4], F32)
        k.op("dve", lambda e: e.tensor_tensor(out=Cc[:, :, :], in0=LI[:, :, :], in1=Bc[:, :, :],
                                              op=ALU.subtract), reads=[LIB, BcB], writes=[CcB])
        CM, CMB = k.sb(es, "CM" + tag, [P, 64, 4], F32)
        for h in range(4):
            k.op("dve", lambda e, h=h: e.tensor_tensor_scan(out=CM[:, :, h], data0=ones64[0:P, :],
                                                            data1=Cc[:, :, h], initial=-3.0e38, op0=ALU.mult,
                                                            op1=ALU.max), reads=[CcB, ones64B], writes=[CMB])
        BX, BXB = k.sb(es, "BX" + tag, [P, 8], F32)
        k.op("dve", lambda e: e.tensor_copy(out=BX[:, 0:4], in_=Bc[:, 63, :]), reads=[BcB], writes=[BXB])
        k.op("dve", lambda e: e.tensor_copy(out=BX[:, 4:8], in_=CM[:, 63, :]), reads=[CMB], writes=[BXB])
        mnew = None
        if mp_tile is None:
            BT, BTB = k.sb(es, "BT" + tag, [4, 128], F32)
            XT, XTB = k.sb(es, "XT" + tag, [4, 128], F32)
            bxd = Buf("bx_d")
            k.dma("pool", T["bx_d"][:, :], BX[:, :], BXB, reads=[BXB], writes=[bxd])
            k.dma("sp", BT[:, :], T["bx_d"][:, 0:4].rearrange("c q -> q c"), BTB, reads=[bxd], writes=[BTB],
                  allow_slow_non_contiguous=True)
            k.dma("sp", XT[:, :], T["bx_d"][:, 4:8].rearrange("c q -> q c"), XTB, reads=[bxd], writes=[XTB],
                  allow_slow_non_contiguous=True)
            MN, MNB = k.sb(es, "MN" + tag, [4, 128], F32)
            k.op("dve", lambda e: e.tensor_tensor_scan(out=MN[:, :], data0=XT[:, :], data1=BT[:, :],
                                                       initial=0.0, op0=ALU.max, op1=ALU.add),
                 reads=[XTB, BTB], writes=[MNB])
            MP, MPB = k.sb(es, "MP" + tag, [4, 128], F32)
            k.op("dve", lambda e: e.memset(MP[:, 0:1], 0.0), writes=[MPB])
            k.op("dve", lambda e: e.tensor_copy(out=MP[:, 1:128], in_=MN[:, 0:127]), reads=[MNB], writes=[MPB])
            mpd = Buf("mp_d")
            k.dma("pool", T["mp_d"][:, :], MP[:, :], MPB, reads=[MPB], writes=[mpd])
            mp, mpB = k.sb(es, "mp" + tag, [P, 4], F32)
            k.dma("sp", mp[:, :], T["mp_d"].rearrange("q c -> c q"), mpB, reads=[mpd], writes=[mpB],
                  allow_slow_non_contiguous=True)
            k.dma("pool", T["m_out"][0:1, :].rearrange("o h -> h o"), MN[:, 127:128], MNB, reads=[MNB],
                  is_output=True, allow_slow_non_contiguous=True)
        else:
            mp, mpB = mp_tile, mp_B
            mnew, mnewB = k.sb(es, "mnew" + tag, [P, 4], F32)
            k.op("dve", lambda e: e.tensor_tensor(out=mnew[:, :], in0=mp[:, :], in1=BX[:, 4:8], op=ALU.max),
                 reads=[mpB, BXB], writes=[mnewB])
            k.op("dve", lambda e: e.tensor_tensor(out=mnew[:, :], in0=mnew[:, :], in1=BX[:, 0:4], op=ALU.add),
                 reads=[mnewB, BXB], writes=[mnewB])
            k.dma("pool", T["m_out"][1:3, :], mnew[:, :], mnewB, reads=[mnewB], is_output=True)
        mp3 = mp[:, :].unsqueeze(1).to_broadcast([P, 64, 4])
        MM, MMB = k.sb(es, "MM" + tag, [P, 64, 4], F32)
        k.op("dve", lambda e: e.tensor_tensor(out=MM[:, :, :], in0=CM[:, :, :], in1=mp3, op=ALU.max),
             reads=[CMB, mpB], writes=[MMB])
        U, UB = k.sb(es, "U" + tag, [P, 64, 4], F32)
        k.op("dve", lambda e: e.tensor_tensor(out=U[:, :, :], in0=MM[:, :, :], in1=mp3, op=ALU.subtract),
             reads=[MMB, mpB], writes=[UB])
        k.op("act", lambda e: e.activation(out=U[:, :, :], in_=U[:, :, :], func=AF.Exp, scale=-1.0),
             reads=[UB], writes=[UB])
        GX, GXB = k.sb(es, "GX" + tag, [P, 64, 4], F32)
        k.op("dve", lambda e: e.tensor_tensor(out=GX[:, :, :], in0=Cc[:, :, :],
                                              in1=MM[:, 63, :].unsqueeze(1).to_broadcast([P, 64, 4]),
                                              op=ALU.subtract), reads=[CcB, MMB], writes=[GXB])
        k.op("dve", lambda e: e.tensor_scalar(out=GX[:, :, :], in0=GX[:, :, :], scalar1=0.0, scalar2=-80.0,
                                              op0=ALU.min, op1=ALU.max), reads=[GXB], writes=[GXB])
        k.op("act", lambda e: e.activation(out=GX[:, :, :], in_=GX[:, :, :], func=AF.Exp,
                                           bias=-math.log(16.0), scale=1.0), reads=[GXB], writes=[GXB])
        FL, FLB = k.sb(es, "FL" + tag, [P, 64, 4], F32)
        k.op("dve", lambda e: e.tensor_tensor(out=FL[:, :, :], in0=Bc[:, :, :], in1=MM[:, :, :], op=ALU.add),
             reads=[BcB, MMB], writes=[FLB])
        k.op("act", lambda e: e.activation(out=FL[:, :, :], in_=FL[:, :, :], func=AF.Exp, scale=-1.0),
             reads=[FLB], writes=[FLB])
        DEC, DECB = k.sb(es, "DEC" + tag, [P, 4], F32)
        k.op("dve", lambda e: e.tensor_copy(out=DEC[:, :], in_=U[:, 63, :]), reads=[UB], writes=[DECB])
        r0, r1 = row0, row0 + P
        k.dma("pool", T["cc_d"][r0:r1], Cc[:, :, :], CcB, reads=[CcB], writes=[T["B_cc"]])
        k.dma("pool", T["mm_d"][r0:r1], MM[:, :, :], MMB, reads=[MMB], writes=[T["B_mm"]])
        k.dma("pool", T["u_d"][r0:r1], U[:, :, :], UB, reads=[UB], writes=[T["B_u"]])
        k.dma("pool", T["gx_d"][r0:r1], GX[:, :, :], GXB, reads=[GXB], writes=[T["B_gx"]])
        k.dma("pool", T["fl_d"][r0:r1], FL[:, :, :], FLB, reads=[FLB], writes=[T["B_fl"]])
        k.dma("pool", T["dec_d"][r0:r1], DEC[:, :], DECB, reads=[DECB], writes=[T["B_dec"]])

    for nm in ("cc", "mm", "u", "gx", "fl", "dec"):
        T["B_" + nm] = Buf("B_" + nm)
    gate_prep(128, "p", (0, SEQ), (0, SEQ), None, None, 0)
    mps, mpsB = k.sb(es, "mps", [2, 4], F32)
    k.dma("sp", mps[:, :], T["state_m"][:, :], mpsB, writes=[mpsB])
    gate_prep(2, "s", (SEQ, NTOK), (SEQ, NTOK), mps, mpsB, 128)

    gxa, gxaB = k.sb(es, "gxa", [64, NCH, 4], F32)
    gxparts = []
    for c0 in range(0, NCH, 16):
        c1 = min(NCH, c0 + 16)
        pb_ = Buf("gxa%d" % c0)
        k.dma("sp", gxa[:, c0:c1, :], T["gx_d"][c0:c1].rearrange("c l h -> l c h"), pb_, reads=[T["B_gx"]],
              writes=[pb_], allow_slow_non_contiguous=True)
        gxparts.append(pb_)
    deca, decaB = k.sb(es, "deca", [128, NCH * 4], F32)
    k.dma("sp", deca[:, :], T["dec_d"].rearrange("c h -> (c h)").unsqueeze(0).to_broadcast([128, NCH * 4]),
          decaB, reads=[T["B_dec"]], writes=[decaB])

    CT, CTB = k.sb(es, "CT", [128, 8, 256], F32)
    CTh = [Buf("CTh%d" % i) for i in range(4)]
    nT, nTB = k.sb(es, "nT", [128, 8], F32)
    CTb, CTbB = k.sb(es, "CTb", [128, 8, 257], BF16)
    kTb = [k.sb(es, "kTb%d" % i, [128, 8, 512], BF16) for i in range(2)]
    vx = [k.sb(es, "vx%d" % i, [64, 8, 4, 257], BF16) for i in range(2)]
    for (t, b) in vx:
        k.op("pool", lambda e, t=t: e.memset(t[:, :, :, 256:257], 1.0), writes=[b])
    kM = [k.sb(es, "kM%d" % i, [64, D], BF16) for i in range(2)]
    gv = [k.sb(es, "gv%d" % i, [64, 4, 257], BF16) for i in range(2)]
    qTt = [k.sb(es, "qTt%d" % i, [128, 8, 128], BF16) for i in range(2)]
    sot = [k.sb(es, "sot%d" % i, [64, 2, D], BF16) for i in range(2)]
    cs2 = [k.sb(es, "cs2%d" % i, [64, 2, 4], F32) for i in range(2)]
    u2 = [k.sb(es, "u2%d" % i, [64, 2, 4], F32) for i in range(2)]
    fl2 = [k.sb(es, "fl2%d" % i, [64, 2, 4], F32) for i in range(2)]
    Mb = [k.sb(es, "Mb%d" % i, [64, 2, 64, 4], F32) for i in range(2)]
    EA, EAB = k.sb(es, "EA", [64, 4, 64], F32)
    PT, PTB = k.sb(es, "PT", [64, 4, 64], BF16)
    ta, taB = k.sb(es, "ta", [64, 2, 257], F32)
    hn, hnB = k.sb(es, "hn", [64, 4, 257], F32)
    dn, dnB = k.sb(es, "dn", [64, 8], F32)
    hs, hsB = k.sb(es, "hs", [64, D], F32)
    sqc, sqcB = k.sb(es, "sqc", [64, D], F32)
    stc, stcB = k.sb(es, "stc", [64, 16], F32)
    yab, yabB = k.sb(es, "yab", [64, 2, D], BF16)
    yaT, yaTB = k.sb(es, "yaT", [128, 8, 128], BF16)
    CO, COB = k.sb(es, "CO", [128, 8, 256], F32)

    def zero_state():
        k.op("pool", lambda e: e.memset(CT[:, :, :], 0.0), writes=CTh)
        k.op("pool", lambda e: e.memset(nT[:, :], 0.0), writes=[nTB])

    def store_state(cdst, ndst):
        for hb_ in range(8):
            h, eb = hb_ // 2, hb_ % 2
            for db in range(2):
                k.op("pe", lambda e, db=db: e.transpose(out=pX[0][:, db * 128:(db + 1) * 128],
                                                        in_=CT[:, 2 * h + db, eb * 128:(eb + 1) * 128],
                                                        identity=identf[:, :]),
                     reads=CTh + [identfB], writes=[pX[1]])
            k.op("act", lambda e: e.activation(out=CO[:, hb_, :], in_=pX[0][:, 0:256], func=AF.Copy),
                 reads=[pX[1]], writes=[COB])
        k.dma("pool", cdst.rearrange("h (eb e) d -> e (h eb) d", e=128), CO[:, :, :], COB, reads=[COB],
              is_output=True)
        k.dma("pool", ndst.rearrange("h (db p) -> p (h db)", p=128), nT[:, :], nTB, reads=[nTB],
              is_output=True, allow_slow_non_contiguous=True)

    def load_state(csrc, nsrc):
        k.dma("sp", CO[:, :, :], csrc.rearrange("h (eb e) d -> e (h eb) d", e=128), COB, writes=[COB])
        for hd_ in range(8):
            h, db = hd_ // 2, hd_ % 2
            for eb in range(2):
                k.op("pe", lambda e, eb=eb: e.transpose(out=pX[0][:, eb * 128:(eb + 1) * 128],
                                                        in_=CO[:, 2 * h + eb, db * 128:(db + 1) * 128],
                                                        identity=identf[:, :]),
                     reads=[COB, identfB], writes=[pX[1]])
            k.op("act", lambda e: e.activation(out=CT[:, hd_, :], in_=pX[0][:, 0:256], func=AF.Copy),
                 reads=[pX[1]], writes=CTh)
        k.dma("sp", nT[:, :], nsrc.rearrange("h (db p) -> p (h db)", p=128), nTB, writes=[nTB],
              allow_slow_non_contiguous=True)

    state = {"blk": -1}

    def load_block(blk, nchunks):
        kt, ktB = kTb[blk % 2]
        vt, vtB = vx[blk % 2]
        ntok = nchunks * 64
        k.dma("sp", kt[:, :, 0:ntok], T["kT_d"][:, :, blk * 512:blk * 512 + ntok].rearrange("g p t -> p g t"),
              ktB, writes=[ktB])
        for c in range(nchunks):
            r0 = blk * 512 + c * 64
            k.dma("sp", vt[:, c, :, 0:256], T["mv_d"][r0:r0 + 64, :].rearrange("l (h e) -> l h e", h=4), vtB,
                  writes=[vtB])
        return kt, ktB, vt, vtB

    def chunk_common(ch, kt, ktB, vt, vtB, cL):
        kMt, kMB = kM[ch % 2]
        for cg in range(8):
            k.op("pe", lambda e, cg=cg: e.transpose(out=pK[0][0:64, cg * 128:(cg + 1) * 128],
                                                    in_=kt[:, cg, cL * 64:(cL + 1) * 64], identity=ident[:, :]),
                 reads=[ktB, identB], writes=[pK[1]])
        k.op("act", lambda e: e.activation(out=kMt[:, :], in_=pK[0][0:64, :], func=AF.Copy),
             reads=[pK[1]], writes=[kMB])
        gvt, gvB = gv[ch % 2]
        k.op("dve", lambda e: e.tensor_tensor(out=gvt[:, :, :], in0=vt[:, cL, :, :],
                                              in1=gxa[:, ch, :].unsqueeze(2).to_broadcast([64, 4, 257]),
                                              op=ALU.mult), reads=[vtB] + gxparts, writes=[gvB])
        return kMt, kMB, gvt, gvB

    dnT, dnTB = k.sb(es, "dnT", [128, 8], F32)
    evc = [k.sb(es, "evc%d" % i, [128, 2, 256], F32) for i in range(2)]

    def state_update(ch, kMt, kMB, gvt, gvB):
        for h in range(4):
            for db in range(2):
                k.op("pe", lambda e, db=db: e.matmul(pN[0][:, 2 * h + db:2 * h + db + 1],
                                                     lhsT=kMt[:, h * 256 + db * 128:h * 256 + (db + 1) * 128],
                                                     rhs=gvt[:, h, 256:257], start=True, stop=True),
                     reads=[kMB, gvB], writes=[pN[1]])
        k.op("act", lambda e: e.activation(out=dnT[:, :], in_=pN[0][:, 0:8], func=AF.Copy),
             reads=[pN[1]], writes=[dnTB])
        for h in range(4):
            pc, pcB = pC[h % 2]
            for db in range(2):
                k.op("pe", lambda e, db=db: e.matmul(pc[:, db, :],
                                                     lhsT=kMt[:, h * 256 + db * 128:h * 256 + (db + 1) * 128],
                                                     rhs=gvt[:, h, 0:256], start=True, stop=True),
                     reads=[kMB, gvB], writes=[pcB])
            dsc = deca[:, ch * 4 + h:ch * 4 + h + 1]
            if h < 2:
                k.op("dve", lambda e: e.scalar_tensor_tensor(out=CT[:, 2 * h:2 * h + 2, :],
                                                             in0=CT[:, 2 * h:2 * h + 2, :], scalar=dsc,
                                                             in1=pc[:, :, :], op0=ALU.mult, op1=ALU.add),
                     reads=[CTh[h], decaB, pcB], writes=[CTh[h]])
            else:
                et, eB = evc[h % 2]
                k.op("act", lambda e: e.activation(out=et[:, :, :], in_=pc[:, :, :], func=AF.Copy),
                     reads=[pcB], writes=[eB])
                k.op("pool", lambda e: e.tensor_scalar(out=CT[:, 2 * h:2 * h + 2, :],
                                                       in0=CT[:, 2 * h:2 * h + 2, :], scalar1=dsc, scalar2=None,
                                                       op0=ALU.mult), reads=[CTh[h], decaB], writes=[CTh[h]])
                k.op("pool", lambda e: e.tensor_tensor(out=CT[:, 2 * h:2 * h + 2, :],
                                                       in0=CT[:, 2 * h:2 * h + 2, :], in1=et[:, :, :],
                                                       op=ALU.add), reads=[CTh[h], eB], writes=[CTh[h]])
            k.op("dve", lambda e: e.scalar_tensor_tensor(out=nT[:, 2 * h:2 * h + 2], in0=nT[:, 2 * h:2 * h + 2],
                                                         scalar=dsc, in1=dnT[:, 2 * h:2 * h + 2],
                                                         op0=ALU.mult, op1=ALU.add),
                 reads=[nTB, decaB, dnTB], writes=[nTB])

    def tile_loads(gi, ch0, slot):
        qt, qB = qTt[slot]
        k.dma("sp", qt[:, :, :], T["qT_d"][:, :, gi * 128:(gi + 1) * 128], qB, writes=[qB])
        st, sB = sot[slot]
        k.dma("sp", st[:, :, :], T["so_d"][gi * 128:(gi + 1) * 128, :].rearrange("(c l) f -> l c f", l=64), sB,
              writes=[sB])
        outs = []
        for (tl, dn_, bname) in ((cs2, "cc_d", "B_cc"), (u2, "u_d", "B_u"), (fl2, "fl_d", "B_fl")):
            t_, b_ = tl[slot]
            k.dma("sp", t_[:, :, :], T[dn_][ch0:ch0 + 2].rearrange("c l h -> l c h"), b_, reads=[T[bname]],
                  writes=[b_], allow_slow_non_contiguous=True)
            outs.append((t_, b_))
        mt, mB = Mb[slot]
        k.dma("sp", mt[:, :, :, :].rearrange("p a l h -> p (a l h)"),
              T["mm_d"][ch0:ch0 + 2].rearrange("c l h -> (c l h)").unsqueeze(0).to_broadcast([64, 512]),
              mB, reads=[T["B_mm"]], writes=[mB])
        return (qt, qB, st, sB) + tuple(outs) + ((mt, mB),)

    def chunk_output(cc, tl, kt, ktB, cL, vt, vtB):
        qt, qB, st, sB, (cst, csB), (ut, uB), (flt, flB), (mt, mB) = tl
        k.op("act", lambda e: e.activation(out=CTb[:, :, 0:256], in_=CT[:, :, :], func=AF.Copy),
             reads=CTh, writes=[CTbB])
        k.op("dve", lambda e: e.tensor_copy(out=CTb[:, :, 256], in_=nT[:, :]), reads=[nTB], writes=[CTbB])
        for h in range(4):
            for db in range(2):
                k.op("pe", lambda e, db=db: e.matmul(pS[0][:, h * 64:(h + 1) * 64],
                                                     lhsT=kt[:, 2 * h + db, cL * 64:(cL + 1) * 64],
                                                     rhs=qt[:, 2 * h + db, cc * 64:(cc + 1) * 64],
                                                     start=(db == 0), stop=(db == 1)),
                     reads=[ktB, qB], writes=[pS[1]])
            k.op("dve", lambda e: e.tensor_scalar(out=EA[:, h, :], in0=mt[:, cc, :, h],
                                                  scalar1=cst[:, cc, h:h + 1], scalar2=0.0, op0=ALU.subtract,
                                                  op1=ALU.max), reads=[mB, csB], writes=[EAB])
        k.op("act", lambda e: e.activation(out=EA[:, :, :], in_=EA[:, :, :], func=AF.Exp, scale=-1.0),
             reads=[EAB], writes=[EAB])
        k.op("dve", lambda e: e.tensor_tensor(out=EA[:, :, :], in0=EA[:, :, :],
                                              in1=tri[:, :].unsqueeze(1).to_broadcast([64, 4, 64]), op=ALU.mult),
             reads=[EAB, triB], writes=[EAB])
        k.op("dve", lambda e: e.tensor_tensor(out=PT[:, :, :].rearrange("p h l -> p (h l)"), in0=pS[0][:, :],
                                              in1=EA[:, :, :].rearrange("p h l -> p (h l)"), op=ALU.mult),
             reads=[pS[1], EAB], writes=[PTB])
        for hp in range(2):
            for hh in range(2):
                h = hp * 2 + hh
                for db in range(2):
                    k.op("pe", lambda e, db=db: e.matmul(pI[0][:, hh, 0:257],
                                                         lhsT=qt[:, 2 * h + db, cc * 64:(cc + 1) * 64],
                                                         rhs=CTb[:, 2 * h + db, :], start=(db == 0),
                                                         stop=(db == 1)),
                         reads=[qB, CTbB], writes=[pI[1]])
            for hh in range(2):
                h = hp * 2 + hh
                k.op("pe", lambda e: e.matmul(pA[0][:, 0:257], lhsT=PT[:, h, :], rhs=vt[:, cL, h, :],
                                              start=True, stop=True), reads=[PTB, vtB], writes=[pA[1]])
                k.op("act", lambda e: e.activation(out=ta[:, hh, :], in_=pA[0][:, 0:257], func=AF.Copy),
                     reads=[pA[1]], writes=[taB])
                k.op("dve", lambda e: e.scalar_tensor_tensor(out=hn[:, h, :], in0=pI[0][:, hh, 0:257],
                                                             scalar=ut[:, cc, h:h + 1], in1=ta[:, hh, :],
                                                             op0=ALU.mult, op1=ALU.add),
                     reads=[pI[1], uB, taB], writes=[hnB])
        k.op("dve", lambda e: e.tensor_scalar(out=dn[:, 0:4], in0=hn[:, :, 256], scalar1=-1.0, scalar2=None,
                                              op0=ALU.mult), reads=[hnB], writes=[dnB])
        k.op("dve", lambda e: e.tensor_tensor(out=dn[:, 0:4], in0=dn[:, 0:4], in1=hn[:, :, 256], op=ALU.max),
             reads=[hnB, dnB], writes=[dnB])
        k.op("dve", lambda e: e.tensor_tensor(out=dn[:, 0:4], in0=dn[:, 0:4], in1=flt[:, cc, :], op=ALU.max),
             reads=[dnB, flB], writes=[dnB])
        k.op("dve", lambda e: e.reciprocal(out=dn[:, 4:8], in_=dn[:, 0:4]), reads=[dnB], writes=[dnB])
        for h in range(4):
            k.op("dve", lambda e: e.tensor_scalar(out=hs[:, h * 256:(h + 1) * 256], in0=hn[:, h, 0:256],
                                                  scalar1=dn[:, 4 + h:5 + h], scalar2=None, op0=ALU.mult),
                 reads=[hnB, dnB], writes=[hsB])
        head_norm(k, hs, hsB, stc, stcB, sqc, sqcB, gmnb[:, :].rearrange("p (h d) -> p h d", h=4), gmnbB, neghalf, neghalfB, nh=4, P=64)
        k.op("dve", lambda e: e.tensor_tensor(out=yab[:, cc, :], in0=hs[:, :], in1=st[:, cc, :], op=ALU.mult),
             reads=[hsB, sB], writes=[yabB])

    def tile_finish(gi):
        for cg in range(8):
            for cc in range(2):
                k.op("pe", lambda e, cg=cg, cc=cc: e.transpose(
                    out=pK[0][:, cg * 128 + cc * 64:cg * 128 + (cc + 1) * 64],
                    in_=yab[:, cc, cg * 128:(cg + 1) * 128], identity=ident[0:64, 0:64]),
                    reads=[yabB, identB], writes=[pK[1]])
        k.op("act", lambda e: e.activation(out=yaT[:, :, :], in_=pK[0][:, :].rearrange("p (g t) -> p g t", g=8),
                                           func=AF.Copy), reads=[pK[1]], writes=[yaTB])
        k.dma("pool", T["yaT_d"][:, :, gi * 128:(gi + 1) * 128], yaT[:, :, :], yaTB, reads=[yaTB])

    zero_state()
    chunks = [(blk, cL) for blk in range(16) for cL in range(8)]
    blkres = {}

    def get_common(i):
        blk, cL = chunks[i]
        if cL == 0:
            blkres[blk] = (load_block(blk, 8), tile_loads(blk, blk * 8 + 6, blk % 2))
        (kt, ktB, vt, vtB), tl = blkres[blk]
        return chunk_common(blk * 8 + cL, kt, ktB, vt, vtB, cL)

    com = get_common(0)
    for i, (blk, cL) in enumerate(chunks):
        nxt = get_common(i + 1) if i + 1 < len(chunks) else None
        (kt, ktB, vt, vtB), tl = blkres[blk]
        ch = blk * 8 + cL
        if cL >= 6:
            chunk_output(cL - 6, tl, kt, ktB, cL, vt, vtB)
        state_update(ch, *com)
        if cL == 7:
            tile_finish(blk)
        com = nxt
    store_state(T["C_out"][0], T["n_out"][0])
    kt, ktB, vt, vtB = load_block(16, 2)
    tl = tile_loads(16, 128, 0)
    for sq_ in range(2):
        load_state(T["state_C"][sq_], T["state_n"][sq_])
        ch = 128 + sq_
        kMt, kMB, gvt, gvB = chunk_common(ch, kt, ktB, vt, vtB, sq_)
        chunk_output(sq_, tl, kt, ktB, sq_, vt, vtB)
        state_update(ch, kMt, kMB, gvt, gvB)
        store_state(T["C_out"][1 + sq_], T["n_out"][1 + sq_])
    tile_finish(16)
    k.barrier()
    es.close()


def phaseD1(k, T):
    es = ExitStack()
    ident, identB = mk_ident(k, es, "identD", BF16)
    identf, identfB = mk_ident(k, es, "identfD", F32)
    kiT, kiTB = k.sb(es, "kiT", [128, NTOK], BF16)
    k.dma("sp", kiT[:, :], T["kiT_d"][:, :], kiTB, writes=[kiTB])
    qiT, qiTB = k.sb(es, "qiT", [128, 4, NQ], BF16)
    k.dma("sp", qiT[:, :, :], T["qiT_d"][:, :, :], qiTB, writes=[qiTB])
    iwa, iwaB = k.sb(es, "iwaD", [128, NOWN * 8], F32)
    k.dma("sp", iwa[:, :], T["iw_d"][:, :], iwaB, writes=[iwaB])
    pk, pkB = k.sb(es, "pk", [128, 512], F32)
    k.dma("sp", pk[:, :], T["tokvalid"][0:512].unsqueeze(0).to_broadcast([128, 512]), pkB, writes=[pkB])
    k.op("dve", lambda e: e.tensor_scalar(out=pk[:, :], in0=pk[:, :], scalar1=-1.0, scalar2=1.0e30,
                                          op0=ALU.add, op1=ALU.mult), reads=[pkB], writes=[pkB])
    p2, p2B = k.sb(es, "p2", [128, 32], F32)
    for i in range(NBIS):
        k.op("pool", lambda e, i=i: e.memset(p2[:, i:i + 1], 2.0 ** (-i)), writes=[p2B])
    SCs = [k.sb(es, "SC%d" % i, [128, SEQ], F32) for i in range(2)]
    MKs = [k.sb(es, "MK%d" % i, [128, SEQ], BF16) for i in range(2)]
    mTs, mTsB = k.sb(es, "mTs", [128, 64, 128], BF16)
    R = [k.sb(es, "R%d" % i, [128, 512], BF16) for i in range(6)]
    dws = [k.sb(es, "dw%d" % i, [128, 8, 128], BF16) for i in range(2)]
    sv, svB = k.sb(es, "sv", [128, 8], F32)
    svm, svmB = k.sb(es, "svm", [128, 8], F32)
    svc, svcB = k.sb(es, "svc", [128, 8], F32)
    svs, svsB = k.sb(es, "svs", [128, 8], F32)
    Wt, WtB = k.sb(es, "Wt", [128, 32], F32)
    kis, kisB = k.sb(es, "kis", [128, NQ], BF16)
    ck, ckB = k.sb(es, "ck", [128, 8, 64], F32)
    kd, kdB = k.sb(es, "kd", [128, 8, 2, 64], BF16)
    pr = [k.ps(es, "prD%d" % i, [128, 512], F32) for i in range(4)]
    pacc = [k.ps(es, "paccD%d" % i, [128, 512], F32) for i in range(2)]
    pT = [k.ps(es, "pTD%d" % i, [128, 1024], BF16) for i in range(2)]
    cnts = {"r": 0, "pr": 0, "acc": 0, "pt": 0}

    for sq_ in range(2):
        k.dma("sp", ck[:, :, :], T["cache_kidx"][sq_].rearrange("(t p) d -> p t d", p=128), ckB, writes=[ckB])
        k.op("dve", lambda e: e.tensor_copy(out=kd[:, :, :, :],
                                            in_=ck[:, :, :].unsqueeze(2).to_broadcast([128, 8, 2, 64])),
             reads=[ckB], writes=[kdB])
        pt, ptB = pT[sq_]
        for st in range(8):
            k.op("pe", lambda e, st=st: e.transpose(out=pt[:, st * 128:(st + 1) * 128],
                                                    in_=kd[:, st, :, :].rearrange("p a d -> p (a d)"),
                                                    identity=ident[:, :]), reads=[kdB, identB], writes=[ptB])
        k.op("act", lambda e: e.activation(out=kis[:, sq_ * 1024:(sq_ + 1) * 1024], in_=pt[:, :], func=AF.Copy),
             reads=[ptB], writes=[kisB])
    k.op("dve", lambda e: e.tensor_copy(out=kis[:, 2048:NQ], in_=kiT[:, SEQ:NTOK]), reads=[kiTB], writes=[kisB])

    def scores(gi, keyT, keyB, S):
        SC, SCB = SCs[gi % 2]
        dw, dwB = dws[gi % 2]
        blocks = [(c0, min(512, S - c0)) for c0 in range(0, S, 512)]
        for h in range(8):
            k.op("pool", lambda e, h=h: e.tensor_scalar(out=dw[:, h, :], in0=identf[:, :],
                                                        scalar1=iwa[:, gi * 8 + h:gi * 8 + h + 1], scalar2=None,
                                                        op0=ALU.mult), reads=[identfB, iwaB], writes=[dwB])
        for (c0, n) in blocks:
            pa, paB = pacc[cnts["acc"] % 2]
            cnts["acc"] += 1
            pps = {}

            def sm(h):
                hp, base = h // 2, (h % 2) * 64
                pp, ppB = pr[cnts["pr"] % 4]
                cnts["pr"] += 1
                k.op("pe", lambda e: e.matmul(pp[:, 0:n], lhsT=qiT[base:base + 64, hp, gi * 128:(gi + 1) * 128],
                                              rhs=keyT[base:base + 64, c0:c0 + n], start=True, stop=True),
                     reads=[qiTB, keyB], writes=[ppB])
                pps[h] = (pp, ppB)
            sm(0)
            sm(1)
            sm(2)
            for h in range(8):
                pp, ppB = pps[h]
                rt, rB = R[cnts["r"] % 6]
                cnts["r"] += 1
                k.op("act", lambda e: e.activation(out=rt[:, 0:n], in_=pp[:, 0:n], func=AF.Relu),
                     reads=[ppB], writes=[rB])
                if h + 3 < 8:
                    sm(h + 3)
                k.op("pe", lambda e: e.matmul(pa[:, 0:n], lhsT=dw[:, h, :], rhs=rt[:, 0:n], start=(h == 0),
                                              stop=(h == 7)), reads=[dwB, rB], writes=[paB])
            k.op("act", lambda e: e.activation(out=SC[:, c0:c0 + n], in_=pa[:, 0:n], func=AF.Copy),
                 reads=[paB], writes=[SCB])
            yield

    def select(gi, S, sample):
        SC, SCB = SCs[gi % 2]
        MK, MKB = MKs[gi % 2]
        k.op("dve", lambda e: e.tensor_reduce(out=sv[:, 0:1], in_=SC[:, 0:S], axis=AX.X, op=ALU.max,
                                              apply_absolute_value=True), reads=[SCB], writes=[svB])
        if not sample:
            k.op("dve", lambda e: e.tensor_tensor(out=SC[:, 0:512], in0=SC[:, 0:512], in1=pk[:, :], op=ALU.add),
                 reads=[SCB, pkB], writes=[SCB])
            k.op("pool", lambda e: e.memset(SC[0:64, S - 64:S], NEG), reads=[SCB], writes=[SCB])
        else:
            k.op("pool", lambda e: e.memset(SC[0:64, 1024:2048], NEG), reads=[SCB], writes=[SCB])
            k.op("pool", lambda e: e.memset(SC[0:64, 2112:2176], NEG), reads=[SCB], writes=[SCB])
            k.op("pool", lambda e: e.memset(SC[64:128, 0:1024], NEG), reads=[SCB], writes=[SCB])
            k.op("pool", lambda e: e.memset(SC[64:128, 2048:2112], NEG), reads=[SCB], writes=[SCB])
        k.op("dve", lambda e: e.tensor_tensor(out=Wt[:, 0:NBIS], in0=p2[:, 0:NBIS],
                                              in1=sv[:, 0:1].to_broadcast([128, NBIS]), op=ALU.mult),
             reads=[p2B, svB], writes=[WtB])
        k.op("dve", lambda e: e.tensor_scalar(out=sv[:, 1:2], in0=sv[:, 0:1], scalar1=-1.0, scalar2=None,
                                              op0=ALU.mult), reads=[svB], writes=[svB])
        Sd = S
        if Sd >= S:
            Sd = S
        nA = S - Sd
        mkaB = Buf("mka")
        for i in range(NBIS):
            k.op("dve", lambda e, i=i: e.tensor_tensor(out=svm[:, 2:3], in0=sv[:, 1:2], in1=Wt[:, i:i + 1],
                                                       op=ALU.add), reads=[svB, WtB], writes=[svmB])
            if nA > 0:
                k.op("act", lambda e: e.activation(out=MK[:, Sd:S], in_=SC[:, Sd:S], func=AF.Sign,
                                                   bias=svm[:, 2:3], scale=-1.0, accum_out=svs[:, 5:6]),
                     reads=[SCB, svmB], writes=[mkaB, svsB])
            k.op("dve", lambda e: e.tensor_scalar(out=MK[:, 0:Sd], in0=SC[:, 0:Sd], scalar1=svm[:, 2:3],
                                                  scalar2=None, op0=ALU.is_ge, op1=ALU.add,
                                                  accum_out=svc[:, 3:4]), reads=[SCB, svmB], writes=[MKB, svcB])
            if nA > 0:
                k.op("dve", lambda e: e.scalar_tensor_tensor(out=sv[:, 6:7], in0=svc[:, 3:4], scalar=2.0,
                                                             in1=svs[:, 5:6], op0=ALU.mult, op1=ALU.subtract),
                     reads=[svcB, svsB], writes=[svB])
                k.op("dve", lambda e: e.tensor_scalar(out=sv[:, 4:5], in0=sv[:, 6:7], scalar1=511.0 - nA,
                                                      scalar2=None, op0=ALU.is_ge), reads=[svB], writes=[svB])
            else:
                k.op("dve", lambda e: e.tensor_scalar(out=sv[:, 4:5], in0=svc[:, 3:4], scalar1=255.5,
                                                      scalar2=None, op0=ALU.is_ge), reads=[svcB], writes=[svB])
            k.op("dve", lambda e, i=i: e.scalar_tensor_tensor(out=sv[:, 1:2], in0=sv[:, 4:5],
                                                              scalar=Wt[:, i:i + 1], in1=sv[:, 1:2],
                                                              op0=ALU.mult, op1=ALU.add),
                 reads=[svB, WtB], writes=[svB])
            yield
        k.op("dve", lambda e: e.tensor_scalar(out=MK[:, 0:S], in0=SC[:, 0:S], scalar1=sv[:, 1:2], scalar2=None,
                                              op0=ALU.is_ge), reads=[SCB, svB, mkaB], writes=[MKB])
        nsb = S // 128
        for s0 in range(0, nsb, 8):
            ns = min(8, nsb - s0)
            pt, ptB = pT[cnts["pt"] % 2]
            cnts["pt"] += 1
            for sb in range(s0, s0 + ns):
                k.op("pe", lambda e, sb=sb: e.transpose(out=pt[:, (sb - s0) * 128:(sb - s0 + 1) * 128],
                                                        in_=MK[:, sb * 128:(sb + 1) * 128], identity=ident[:, :]),
                     reads=[MKB, identB], writes=[ptB])
            k.op("act", lambda e: e.activation(out=mTs[:, s0:s0 + ns, :],
                                               in_=pt[:, 0:ns * 128].rearrange("p (a q) -> p a q", q=128),
                                               func=AF.Copy), reads=[ptB], writes=[mTsB])
        k.dma("pool", T["mk_d"][gi, :, 0:nsb, :], mTs[:, 0:nsb, :], mTsB, reads=[mTsB])

    tiles = [(gi, kiT, kiTB, 512 * (gi + 1), False) for gi in range(16)] + [(16, kis, kisB, NQ, True)]

    def drain(g):
        for _ in g:
            pass

    def interleave(ga, na, gb, nb):
        ia = ib = 0
        da = db = False
        while not (da and db):
            if not da and (db or ia * nb <= ib * na):
                try:
                    next(ga)
                    ia += 1
                except StopIteration:
                    da = True
            else:
                try:
                    next(gb)
                    ib += 1
                except StopIteration:
                    db = True

    drain(scores(*tiles[0][0:4]))
    for i, (gi, kt_, ktB_, S, smp) in enumerate(tiles):
        sel = select(gi, S, smp)
        if i + 1 < len(tiles):
            nxt = tiles[i + 1]
            interleave(scores(*nxt[0:4]), (nxt[3] + 511) // 512, sel, NBIS)
        else:
            drain(sel)
    k.barrier()
    es.close()


def phaseD2(k, T):
    es = ExitStack()
    ones, onesB = k.sb(es, "onesD", [128, 128], BF16)
    k.op("pool", lambda e: e.memset(ones[:, :], 1.0), writes=[onesB])
    pS = [k.ps(es, "pSD%d" % i, [128, 512], F32) for i in range(4)]
    pO = [k.ps(es, "pOD%d" % i, [128, 512], F32) for i in range(2)]
    pD = [k.ps(es, "pDD%d" % i, [128, 512], F32) for i in range(2)]
    PTt = [k.sb(es, "PTt%d" % i, [128, 512], BF16) for i in range(6)]
    oacc, oaccB = k.sb(es, "oacc", [128, 8, 512], F32)
    dacc, daccB = k.sb(es, "dacc", [128, 8, 512], F32)
    ybs, ybsB = k.sb(es, "ybs", [128, 8, 512], BF16)
    cn = {"s": 0, "p": 0, "o": 0, "k": 0, "v": 0}

    def stage1(st):
        for f in st.get("pre", ()):
            f()
        Q_ap = st["Q"]
        nq = Q_ap.shape[-1]
        ps_, psB = pS[cn["s"] % 4]
        cn["s"] += 1
        k.op("pe", lambda e: e.matmul(ps_[:, 0:nq], lhsT=st["KT"], rhs=Q_ap, start=True, stop=True),
             reads=[st["KTB"], st["QB"]], writes=[psB])
        pt, ptB = PTt[cn["p"] % 6]
        cn["p"] += 1
        k.op("act", lambda e: e.activation(out=pt[:, 0:nq], in_=ps_[:, 0:nq], func=AF.Exp),
             reads=[psB], writes=[ptB])
        eng = "dve" if cn["p"] % 5 != 0 else "pool"
        k.op(eng, lambda e: e.tensor_tensor(out=pt[:, 0:nq].rearrange("p (a q) -> p a q", q=128),
                                            in0=pt[:, 0:nq].rearrange("p (a q) -> p a q", q=128),
                                            in1=st["m"], op=ALU.mult), reads=[ptB, st["mB"]], writes=[ptB])
        st["pt"], st["ptB"], st["nq"] = pt, ptB, nq

    def stage2(st):
        pt, ptB, nq, q0 = st["pt"], st["ptB"], st["nq"], st["q0"]
        po, poB, pd, pdB = st["po"]
        k.op("pe", lambda e: e.matmul(po[:, q0:q0 + nq], lhsT=st["V"], rhs=pt[:, 0:nq], start=st["first"],
                                      stop=st["last"]), reads=[st["VB"], ptB], writes=[poB])
        k.op("pe", lambda e: e.matmul(pd[:, q0:q0 + nq], lhsT=ones[:, :], rhs=pt[:, 0:nq], start=st["first"],
                                      stop=st["last"]), reads=[onesB, ptB], writes=[pdB])
        for f in st.get("post", ()):
            f()

    def run_steps(steps, look=4):
        n = len(steps)
        for i in range(min(look, n)):
            stage1(steps[i])
        for i in range(n):
            stage2(steps[i])
            if i + look < n:
                stage1(steps[i + look])

    es2 = ExitStack()
    QTg = [k.sb(es2, "QTg%d" % i, [128, 8, 512], BF16) for i in range(2)]
    mT = [k.sb(es2, "mT%d" % i, [128, 4, 16, 128], BF16) for i in range(2)]
    Vc = [k.sb(es2, "Vc%d" % i, [128, 16, 256], BF16) for i in range(2)]
    KTc = [k.sb(es2, "KTc%d" % i, [128, 2048], BF16) for i in range(3)]
    mi = 0
    for G in range(4):
        qg, qgB = QTg[G % 2]
        k.dma("sp", qg[:, :, :], T["QT_d"][:, :, 512 * G:512 * (G + 1)], qgB, writes=[qgB])
        steps = []
        for kc in range(G + 1):
            mt, mtB = mT[mi % 2]
            mi += 1

            def ld_mask(mt=mt, mtB=mtB, kc=kc, G=G):
                for ti in range(4):
                    nv = 16 if kc < G else 4 * (ti + 1)
                    k.dma("sp", mt[:, ti, 0:nv, :], T["mk_d"][4 * G + ti, :, kc * 16:kc * 16 + nv, :], mtB,
                          writes=[mtB])
            for hp in range(4):
                vc, vcB = Vc[cn["v"] % 2]
                cn["v"] += 1

                def ld_v(vc=vc, vcB=vcB, kc=kc, hp=hp):
                    k.dma("sp", vc[:, :, :],
                          T["V_d"][kc * 2048:(kc + 1) * 2048, hp * 256:(hp + 1) * 256].rearrange(
                              "(sb s) d -> s sb d", s=128), vcB, writes=[vcB])
                for hh in range(2):
                    h = 2 * hp + hh
                    kt, ktB = KTc[cn["k"] % 3]
                    cn["k"] += 1

                    def ld_k(kt=kt, ktB=ktB, kc=kc, h=h):
                        k.dma("sp", kt[:, :], T["KT_d"][h, :, kc * 2048:(kc + 1) * 2048], ktB, writes=[ktB])
                    pos = pO[cn["o"] % 2] + pD[cn["o"] % 2]
                    cn["o"] += 1

                    def post(kc=kc, h=h, pos=pos):
                        po, poB, pd, pdB = pos
                        if kc == 0:
                            k.op("act", lambda e: e.activation(out=oacc[:, h, :], in_=po[:, :], func=AF.Copy),
                                 reads=[poB], writes=[oaccB])
                            k.op("act", lambda e: e.activation(out=dacc[:, h, :], in_=pd[:, :], func=AF.Copy),
                                 reads=[pdB], writes=[daccB])
                        else:
                            k.op("dve", lambda e: e.tensor_tensor(out=oacc[:, h, :], in0=po[:, :],
                                                                  in1=oacc[:, h, :], op=ALU.add),
                                 reads=[poB, oaccB], writes=[oaccB])
                            k.op("dve", lambda e: e.tensor_tensor(out=dacc[:, h, :], in0=pd[:, :],
                                                                  in1=dacc[:, h, :], op=ALU.add),
                                 reads=[pdB, daccB], writes=[daccB])
                    for sb in range(16):
                        q0 = 0 if kc < G else 128 * (sb // 4)
                        st = {"KT": kt[:, sb * 128:(sb + 1) * 128], "KTB": ktB,
                              "V": vc[:, sb, hh * 128:(hh + 1) * 128], "VB": vcB,
                              "Q": qg[:, h, q0:512], "QB": qgB, "m": mt[:, q0 // 128:4, sb, :], "mB": mtB,
                              "q0": q0, "first": sb == 0, "last": sb == 15, "po": pos}
                        pre = []
                        if sb == 0:
                            if hp == 0 and hh == 0:
                                pre.append(ld_mask)
                            if hh == 0:
                                pre.append(ld_v)
                            pre.append(ld_k)
                        st["pre"] = pre
                        if sb == 15:
                            st["post"] = [post]
                        steps.append(st)
        run_steps(steps)
        k.op("dve", lambda e: e.reciprocal(out=dacc[:, :, :], in_=dacc[:, :, :]), reads=[daccB], writes=[daccB])
        k.op("dve", lambda e: e.tensor_tensor(out=ybs[:, :, :], in0=oacc[:, :, :], in1=dacc[:, :, :],
                                              op=ALU.mult), reads=[oaccB, daccB], writes=[ybsB])
        k.dma("pool", T["ybT_d"][:, :, 512 * G:512 * (G + 1)], ybs[:, :, :], ybsB, reads=[ybsB])
    k.barrier()
    es2.close()

    es3 = ExitStack()
    ident, identB = mk_ident(k, es3, "identD2", BF16)
    KTs, KTsB = k.sb(es3, "KTsS", [128, 8, NQ], BF16)
    Vs, VsB = k.sb(es3, "VsS", [128, 17, D], BF16)
    mS, mSB = k.sb(es3, "mS", [128, 17, 128], BF16)
    qs, qsB = k.sb(es3, "qsS", [128, 8, 128], BF16)
    stf = [k.sb(es3, "stf%d" % i, [128, D], F32) for i in range(2)]
    stb = [k.sb(es3, "stb%d" % i, [128, D], BF16) for i in range(2)]
    pT = pS[0]
    pTb = pS[3][0][:, :].bitcast(BF16)
    pTbB = pS[3][1]
    k.dma("sp", mS[:, :, :], T["mk_d"][16, :, 0:17, :], mSB, writes=[mSB])
    k.dma("sp", qs[:, :, :], T["QT_d"][:, :, 2048:NQ], qsB, writes=[qsB])
    vparts = []
    i = 0
    for sq_ in range(2):
        for st in range(8):
            ft, fB = stf[i % 2]
            bt, bB = stb[i % 2]
            k.dma("sp", ft[:, :], T["cache_k"][sq_, st * 128:(st + 1) * 128, :], fB, writes=[fB])
            k.op("pool", lambda e: e.tensor_copy(out=bt[:, :], in_=ft[:, :]), reads=[fB], writes=[bB])
            for h in range(8):
                k.op("pe", lambda e, h=h: e.transpose(out=pTb[:, h * 128:(h + 1) * 128],
                                                      in_=bt[:, h * 128:(h + 1) * 128], identity=ident[:, :]),
                     reads=[bB, identB], writes=[pTbB])
            c0 = sq_ * 1024 + st * 128
            k.op("act", lambda e: e.activation(out=KTs[:, :, c0:c0 + 128],
                                               in_=pTb[:, :].rearrange("p (h t) -> p h t", h=8), func=AF.Copy),
                 reads=[pTbB], writes=[KTsB])
            i += 1
            ft, fB = stf[i % 2]
            k.dma("sp", ft[:, :], T["cache_v"][sq_, st * 128:(st + 1) * 128, :], fB, writes=[fB])
            vb_ = Buf("vsp%d" % i)
            k.op("dve", lambda e: e.tensor_copy(out=Vs[:, sq_ * 8 + st, :], in_=ft[:, :]), reads=[fB],
                 writes=[vb_])
            vparts.append(vb_)
            i += 1
    nb = Buf("ktnew")
    k.dma("sp", KTs[:, :, 2048:NQ], T["KT_d"][:, :, SEQ:NTOK].rearrange("h d t -> d h t"), nb, reads=[KTsB],
          writes=[nb])
    vn = Buf("vnew")
    k.dma("sp", Vs[:, 16, :], T["V_d"][SEQ:NTOK, :], vn, writes=[vn])
    steps = []
    for h in range(8):
        pos = pO[h % 2] + pD[h % 2]

        def post(h=h, pos=pos):
            po, poB, pd, pdB = pos
            k.op("act", lambda e: e.activation(out=dacc[:, h, 0:128], in_=pd[:, 0:128], func=AF.Copy),
                 reads=[pdB], writes=[daccB])
            k.op("dve", lambda e: e.reciprocal(out=dacc[:, h, 0:128], in_=dacc[:, h, 0:128]), reads=[daccB],
                 writes=[daccB])
            k.op("dve", lambda e: e.tensor_tensor(out=ybs[:, h, 0:128], in0=po[:, 0:128], in1=dacc[:, h, 0:128],
                                                  op=ALU.mult), reads=[poB, daccB], writes=[ybsB])
        for sb in range(17):
            st = {"KT": KTs[:, h, sb * 128:(sb + 1) * 128], "KTB": KTsB if sb < 16 else nb,
                  "V": Vs[:, sb, h * 128:(h + 1) * 128], "VB": vparts[sb] if sb < 16 else vn,
                  "Q": qs[:, h, :], "QB": qsB, "m": mS[:, sb:sb + 1, :], "mB": mSB, "q0": 0,
                  "first": sb == 0, "last": sb == 16, "po": pos}
            if sb == 16:
                st["post"] = [post]
            steps.append(st)
    run_steps(steps)
    k.dma("pool", T["ybT_d"][:, :, 2048:NQ], ybs[:, :, 0:128], ybsB, reads=[ybsB])
    k.barrier()
    es3.close()
    es.close()


def phaseE(k, T):
    es = ExitStack()
    ident, identB = mk_ident(k, es, "identE", BF16)
    neghalf, neghalfB = k.sb(es, "neghalfE", [128, 8], F32)
    k.op("pool", lambda e: e.memset(neghalf[:, :], -0.5), writes=[neghalfB])
    g2b, g2bB = k.sb(es, "g2b", [128, D], F32)
    k.dma("sp", g2b[:, :], T["g_norm2"][0:1, :].to_broadcast([128, D]), g2bB, writes=[g2bB])
    pM = [k.ps(es, "pME%d" % i, [128, 512], F32) for i in range(6)]
    pTe = [k.ps(es, "pTE%d" % i, [128, 1024], BF16) for i in range(2)]
    pc = [0]

    def nps():
        p = pM[pc[0] % 6]
        pc[0] += 1
        return p

    def own_rows(gi):
        return (4 * gi + 3) * 128 if gi < 16 else SEQ

    es1 = ExitStack()
    ws = WStream(k, es1, "wsE", 3, 1024)
    Wa, WaB = ws.load(T["w_a_out"], 0, 1024)
    Wb, WbB = ws.load(T["w_b_out"], 0, 1024)
    Wo, WoB = ws.load(T["w_o"], 0, 1024)
    ins4 = [k.sb(es1, "in4_%d" % i, [128, 8, 512], BF16) for i in range(4)]
    mixT, mixTB = k.sb(es1, "mixT", [128, 8, 512], BF16)
    t1 = [k.sb(es1, "t1_%d" % i, [128, 512], F32) for i in range(2)]
    t2 = [k.sb(es1, "t2_%d" % i, [128, 512], F32) for i in range(2)]
    xs = [k.sb(es1, "xsE%d" % i, [128, D], F32) for i in range(2)]
    x1 = [k.sb(es1, "x1E%d" % i, [128, D], F32) for i in range(2)]
    h2b = [k.sb(es1, "h2b%d" % i, [128, D], BF16) for i in range(2)]
    ssE = [k.sb(es1, "ssE%d" % i, [128, 4], F32) for i in range(2)]
    h2Tt, h2TtB = k.sb(es1, "h2Tt", [128, 8, 512], BF16)
    ti_glob = 0
    for tg in range(5):
        n = 512 if tg < 4 else 128
        c0 = tg * 512
        srcs = []
        for i, nm in enumerate(("yaT_d", "ybT_d", "gaT_d", "gbT_d")):
            t_, b_ = ins4[i]
            k.dma("sp", t_[:, :, 0:n], T[nm][:, :, c0:c0 + n], b_, writes=[b_])
            srcs.append((t_, b_))
        (ya, yaB), (yb, ybB), (ga, gaB), (gb, gbB) = srcs
        for cg in range(8):
            pa, paB = nps()
            for kc in range(8):
                k.op("pe", lambda e, kc=kc: e.matmul(pa[:, 0:n], lhsT=Wa[:, kc, cg * 128:(cg + 1) * 128],
                                                     rhs=ya[:, kc, 0:n], start=(kc == 0), stop=(kc == 7)),
                     reads=[WaB, yaB], writes=[paB])
            pb, pbB = nps()
            for kc in range(8):
                k.op("pe", lambda e, kc=kc: e.matmul(pb[:, 0:n], lhsT=Wb[:, kc, cg * 128:(cg + 1) * 128],
                                                     rhs=yb[:, kc, 0:n], start=(kc == 0), stop=(kc == 7)),
                     reads=[WbB, ybB], writes=[pbB])
            a1, a1B = t1[cg % 2]
            a2, a2B = t2[cg % 2]
            k.op("dve", lambda e: e.tensor_tensor(out=a1[:, 0:n], in0=pa[:, 0:n], in1=ga[:, cg, 0:n], op=ALU.mult),
                 reads=[paB, gaB], writes=[a1B])
            k.op("dve", lambda e: e.tensor_tensor(out=a2[:, 0:n], in0=pb[:, 0:n], in1=gb[:, cg, 0:n], op=ALU.mult),
                 reads=[pbB, gbB], writes=[a2B])
            k.op("pool", lambda e: e.tensor_tensor(out=mixT[:, cg, 0:n], in0=a1[:, 0:n], in1=a2[:, 0:n],
                                                   op=ALU.add), reads=[a1B, a2B], writes=[mixTB])
        for tl in range(n // 128):
            gi = ti_glob
            ti_glob += 1
            s = gi % 2
            xt, xB = xs[s]
            x1t, x1B = x1[s]
            r0 = own_rows(gi)
            k.dma("sp", xt[:, :], T["xb"][r0:r0 + 128, :], xB, writes=[xB])
            for hh in range(2):
                po, poB = nps()
                for kc in range(8):
                    k.op("pe", lambda e, kc=kc: e.matmul(po[:, :], lhsT=mixT[:, kc, tl * 128:(tl + 1) * 128],
                                                         rhs=Wo[:, kc, hh * 512:(hh + 1) * 512],
                                                         start=(kc == 0), stop=(kc == 7)),
                         reads=[mixTB, WoB], writes=[poB])
                k.op("dve", lambda e: e.tensor_tensor(out=x1t[:, hh * 512:(hh + 1) * 512], in0=po[:, :],
                                                      in1=xt[:, hh * 512:(hh + 1) * 512], op=ALU.add),
                     reads=[poB, xB], writes=[x1B])
            k.dma("pool", T["x1_d"][gi * 128:(gi + 1) * 128, :], x1t[:, :], x1B, reads=[x1B])
            hbt, hbB = h2b[s]
            sst, ssB = ssE[s]
            k.op("act", lambda e: e.activation(out=hbt[:, :], in_=x1t[:, :], func=AF.Square,
                                               accum_out=sst[:, 0:1]), reads=[x1B], writes=[hbB, ssB])
            k.op("dve", lambda e: e.tensor_scalar(out=sst[:, 1:2], in0=sst[:, 0:1], scalar1=1.0 / D, scalar2=EPS,
                                                  op0=ALU.mult, op1=ALU.add), reads=[ssB], writes=[ssB])
            k.op("pool", lambda e: e.tensor_tensor(out=sst[:, 2:3], in0=sst[:, 1:2], in1=neghalf[:, 0:1],
                                                   op=ALU.pow), reads=[ssB, neghalfB], writes=[ssB])
            k.op("dve", lambda e: e.scalar_tensor_tensor(out=hbt[:, :], in0=x1t[:, :], scalar=sst[:, 2:3],
                                                         in1=g2b[:, :], op0=ALU.mult, op1=ALU.mult),
                 reads=[x1B, ssB, g2bB], writes=[hbB])
            pt, ptB = pTe[gi % 2]
            for kc in range(8):
                k.op("pe", lambda e, kc=kc: e.transpose(out=pt[:, kc * 128:(kc + 1) * 128],
                                                        in_=hbt[:, kc * 128:(kc + 1) * 128],
                                                        identity=ident[:, :]), reads=[hbB, identB], writes=[ptB])
            k.op("act", lambda e: e.activation(out=h2Tt[:, :, tl * 128:(tl + 1) * 128],
                                               in_=pt[:, :].rearrange("p (k n) -> p k n", k=8), func=AF.Copy),
                 reads=[ptB], writes=[h2TtB])
        k.dma("pool", T["h2T_d"][:, :, c0:c0 + n], h2Tt[:, :, 0:n], h2TtB, reads=[h2TtB])
    k.barrier()
    es1.close()

    es2 = ExitStack()
    Wout, WoutB = k.sb(es2, "Wout", [128, 22, D], BF16)
    wstg = [k.sb(es2, "wstg%d" % i, [128, D], F32) for i in range(2)]
    woparts = []
    for fb in range(22):
        st, sB = wstg[fb % 2]
        k.dma("sp", st[:, :], T["w_ffn_out"][fb * 128:(fb + 1) * 128, :], sB, writes=[sB])
        pb_ = Buf("wo%d" % fb)
        k.op("dve" if fb % 2 == 0 else "pool", lambda e, st=st: e.tensor_copy(out=Wout[:, fb, :], in_=st[:, :]),
             reads=[sB], writes=[pb_])
        woparts.append(pb_)
    wsf = WStream(k, es2, "wsF", 4, 128)
    h2h, h2hB = k.sb(es2, "h2h", [128, 8, 1152], BF16)
    actT, actTB = k.sb(es2, "actT", [128, 22, 1152], BF16)
    sg = [k.sb(es2, "sg%d" % i, [128, 512], F32) for i in range(2)]
    x1f = [k.sb(es2, "x1f%d" % i, [128, D], F32) for i in range(2)]
    yo = [k.sb(es2, "yo%d" % i, [128, D], F32) for i in range(2)]
    sgi = 0
    for half in range(2):
        t0 = 0 if half == 0 else 1152
        nt = 1152 if half == 0 else 1024
        k.dma("sp", h2h[:, :, 0:nt], T["h2T_d"][:, :, t0:t0 + nt], h2hB, writes=[h2hB])
        groups = [(c, min(512, nt - c)) for c in range(0, nt, 512)]
        for fb in range(22):
            Wg, WgB = wsf.load(T["w_ffn_in"], fb * 128, 128)
            Wu, WuB = wsf.load(T["w_ffn_in"], DFF + fb * 128, 128)
            for (c, n) in groups:
                pg, pgB = nps()
                for kc in range(8):
                    k.op("pe", lambda e, kc=kc: e.matmul(pg[:, 0:n], lhsT=Wg[:, kc, :], rhs=h2h[:, kc, c:c + n],
                                                         start=(kc == 0), stop=(kc == 7)),
                         reads=[WgB, h2hB], writes=[pgB])
                pu, puB = nps()
                for kc in range(8):
                    k.op("pe", lambda e, kc=kc: e.matmul(pu[:, 0:n], lhsT=Wu[:, kc, :], rhs=h2h[:, kc, c:c + n],
                                                         start=(kc == 0), stop=(kc == 7)),
                         reads=[WuB, h2hB], writes=[puB])
                st, sB = sg[sgi % 2]
                sgi += 1
                k.op("act", lambda e: e.activation(out=st[:, 0:n], in_=pg[:, 0:n], func=AF.Silu),
                     reads=[pgB], writes=[sB])
                k.op("dve", lambda e: e.tensor_tensor(out=actT[:, fb, c:c + n], in0=pu[:, 0:n], in1=st[:, 0:n],
                                                      op=ALU.mult), reads=[puB, sB], writes=[actTB])
        for tl in range(nt // 128):
            gi = (0 if half == 0 else 9) + tl
            xt, xB = x1f[gi % 2]
            yt, yB = yo[gi % 2]
            k.dma("sp", xt[:, :], T["x1_d"][gi * 128:(gi + 1) * 128, :], xB, writes=[xB])
            for hh in range(2):
                py, pyB = nps()
                for fb in range(22):
                    k.op("pe", lambda e, fb=fb: e.matmul(py[:, :], lhsT=actT[:, fb, tl * 128:(tl + 1) * 128],
                                                         rhs=Wout[:, fb, hh * 512:(hh + 1) * 512],
                                                         start=(fb == 0), stop=(fb == 21)),
                         reads=[actTB] + (woparts if fb == 0 else []), writes=[pyB])
                k.op("dve", lambda e: e.tensor_tensor(out=yt[:, hh * 512:(hh + 1) * 512], in0=py[:, :],
                                                      in1=xt[:, hh * 512:(hh + 1) * 512], op=ALU.add),
                     reads=[pyB, xB], writes=[yB])
            k.dma("pool", T["y_out"][gi * 128:(gi + 1) * 128, :], yt[:, :], yB, reads=[yB], is_output=True)
    k.barrier()
    es2.close()
    es.close()


def build(stage=99):
    k = KB()
    T = {}
    for nm, shp in (("xb", [NTOK, D]), ("w_in", [D, DIN]), ("g_norm1", [1, D]), ("g_k", [1, 128]),
                    ("g_q", [1, 128]), ("w_conv", [4, 2048]), ("b_conv", [1, 2048]),
                    ("state_conv", [2, 3, 2048]), ("tokvalid", [NTOK]), ("b_if", [1, 8]),
                    ("g_mnorm", [1, D]), ("state_m", [2, 4]), ("state_C", [2, 4, 256, 256]),
                    ("state_n", [2, 4, 256]), ("cache_kidx", [2, 1024, 64]), ("cache_k", [2, 1024, D]),
                    ("cache_v", [2, 1024, D]), ("w_a_out", [D, D]), ("w_b_out", [D, D]), ("w_o", [D, D]),
                    ("g_norm2", [1, D]), ("w_ffn_in", [D, 2 * DFF]), ("w_ffn_out", [DFF, D])):
        T[nm] = k.dram(nm, shp, F32, "ExternalInput")
    for nm, shp in (("k_out", [NTOK, D]), ("v_out", [NTOK, D]), ("ki_out", [NTOK, 64]),
                    ("cvk_out", [3, 1024, 3]), ("cvq_out", [3, 1024, 3]), ("m_out", [3, 4]),
                    ("C_out", [3, 4, 256, 256]), ("n_out", [3, 4, 256]), ("y_out", [NQ, D])):
        T[nm] = k.dram(nm, shp, F32, "ExternalOutput")
    for nm, shp, dt in (("KT_d", [8, 128, NTOK], BF16), ("V_d", [NTOK, D], BF16), ("mv_d", [NTOK, D], BF16),
                        ("kT_d", [8, 128, NTOK], BF16), ("g_d", [NTOK, 8], F32), ("kiT_d", [128, NTOK], BF16),
                        ("hTo_d", [128, 8, NH], BF16), ("qT_d", [128, 8, NQ], BF16), ("so_d", [NQ, D], BF16),
                        ("QT_d", [128, 8, NQ], BF16), ("qiT_d", [128, 4, NQ], BF16), ("iw_d", [128, NOWN * 8], F32),
                        ("gaT_d", [128, 8, NQ], BF16), ("gbT_d", [128, 8, NQ], BF16),
                        ("cc_d", [NCH, 64, 4], F32), ("mm_d", [NCH, 64, 4], F32), ("u_d", [NCH, 64, 4], F32),
                        ("gx_d", [NCH, 64, 4], F32), ("fl_d", [NCH, 64, 4], F32), ("dec_d", [NCH, 4], F32),
                        ("yaT_d", [128, 8, NQ], BF16), ("ybT_d", [128, 8, NQ], BF16),
                        ("bx_d", [128, 8], F32), ("mp_d", [4, 128], F32),
                        ("mk_d", [NOWN, 128, 64, 128], BF16), ("x1_d", [NQ, D], F32),
                        ("h2T_d", [128, 8, NQ], BF16)):
        T[nm] = k.dram(nm, shp, dt, "Internal")
    phaseA(k, T)
    if stage >= 2:
        phaseB(k, T)
    if stage >= 3:
        phaseC(k, T)
    if stage >= 4:
        phaseD1(k, T)
        phaseD2(k, T)
    if stage >= 5:
        phaseE(k, T)
    k.finish()
    return k.nc


_NC_CACHE = {}
import os
STAGE = int(os.environ.get('KSTAGE', '5'))
SUB = int(os.environ.get('KSUB', '99'))


def kernel(**inp):
    f32 = np.float32
    x_prompt = np.asarray(inp["x_prompt"], f32)
    x_sample = np.asarray(inp["x_sample"], f32)
    if "nc" not in _NC_CACHE:
        _NC_CACHE["nc"] = build(STAGE)
    nc = _NC_CACHE["nc"]
    ca = lambda a: np.ascontiguousarray(np.asarray(a, f32))
    in_maps = []
    for c in range(8):
        b, j = c // 4, c % 4
        npad = 128 * (3 - j)
        xbc = np.concatenate([np.zeros((npad, D), f32), x_prompt[b, 0:SEQ - npad],
                              x_sample[2 * c:2 * c + 2].reshape(128, D)], axis=0)
        tv = np.ones((NTOK,), f32)
        tv[:npad] = 0.0
        m = {
            "xb": ca(xbc), "w_in": ca(inp["w_in"][0]), "g_norm1": ca(inp["g_norm1"]), "g_k": ca(inp["g_k"]),
            "g_q": ca(inp["g_q"]), "w_conv": ca(inp["w_conv"][0]), "b_conv": ca(inp["b_conv"]),
            "state_conv": ca(inp["state_conv"][0, 2 * c:2 * c + 2]), "tokvalid": tv,
            "b_if": ca(inp["b_if"]), "g_mnorm": ca(inp["g_mnorm"]),
            "state_m": ca(inp["state_m"][0, 2 * c:2 * c + 2]),
            "state_C": ca(inp["state_C"][0, 2 * c:2 * c + 2]),
            "state_n": ca(inp["state_n"][0, 2 * c:2 * c + 2]),
            "cache_kidx": ca(inp["cache_kidx"][0, 2 * c:2 * c + 2]),
            "cache_k": ca(inp["cache_k"][0, 2 * c:2 * c + 2]).reshape(2, 1024, D),
            "cache_v": ca(inp["cache_v"][0, 2 * c:2 * c + 2]).reshape(2, 1024, D),
            "w_a_out": ca(inp["w_a_out"][0]), "w_b_out": ca(inp["w_b_out"][0]), "w_o": ca(inp["w_o"][0]),
            "g_norm2": ca(inp["g_norm2"]), "w_ffn_in": ca(inp["w_ffn_in"][0]),
            "w_ffn_out": ca(inp["w_ffn_out"][0]),
        }
        in_maps.append(m)
    res = run_bass_kernel_spmd(nc, in_maps, core_ids=list(range(8)))
    R = res.results
    z = lambda *s: np.zeros(s, f32)
    y_prompt, y_sample = z(2, SEQ, D), z(16, 64, D)
    k_prompt, v_prompt, ki_prompt = z(1, 2, SEQ, 8, 128), z(1, 2, SEQ, 8, 128), z(1, 2, SEQ, 64)
    k_sample, v_sample, ki_sample = z(1, 16, 64, 8, 128), z(1, 16, 64, 8, 128), z(1, 16, 64, 64)
    conv_prompt, conv_sample = z(1, 2, 3, 2048), z(1, 16, 3, 2048)
    C_prompt, n_prompt, m_prompt = z(1, 2, 4, 256, 256), z(1, 2, 4, 256), z(1, 2, 4)
    C_sample, n_sample, m_sample = z(1, 16, 4, 256, 256), z(1, 16, 4, 256), z(1, 16, 4)
    for c in range(8):
        b, j = c // 4, c % 4
        r = R[c]
        if j == 3:
            k_prompt[0, b] = r["k_out"][:SEQ].reshape(SEQ, 8, 128)
            v_prompt[0, b] = r["v_out"][:SEQ].reshape(SEQ, 8, 128)
            ki_prompt[0, b] = r["ki_out"][:SEQ]
            conv_prompt[0, b, :, 1024:] = r["cvk_out"][0].T
            if "cvq_out" in r:
                conv_prompt[0, b, :, :1024] = r["cvq_out"][0].T
            if "C_out" in r:
                C_prompt[0, b] = r["C_out"][0]
                n_prompt[0, b] = r["n_out"][0]
                m_prompt[0, b] = r["m_out"][0]
        k_sample[0, 2 * c:2 * c + 2] = r["k_out"][SEQ:].reshape(2, 64, 8, 128)
        v_sample[0, 2 * c:2 * c + 2] = r["v_out"][SEQ:].reshape(2, 64, 8, 128)
        ki_sample[0, 2 * c:2 * c + 2] = r["ki_out"][SEQ:].reshape(2, 64, 64)
        for s_ in range(2):
            conv_sample[0, 2 * c + s_, :, 1024:] = r["cvk_out"][1 + s_].T
            if "cvq_out" in r:
                conv_sample[0, 2 * c + s_, :, :1024] = r["cvq_out"][1 + s_].T
            if "C_out" in r:
                C_sample[0, 2 * c + s_] = r["C_out"][1 + s_]
                n_sample[0, 2 * c + s_] = r["n_out"][1 + s_]
                m_sample[0, 2 * c + s_] = r["m_out"][1 + s_]
        if "y_out" in r:
            yo = r["y_out"]
            for g in range(16):
                G = 4 * g + j
                y_prompt[b, 128 * G:128 * (G + 1)] = yo[128 * g:128 * (g + 1)]
            y_sample[2 * c:2 * c + 2] = yo[2048:].reshape(2, 64, D)
    return (y_prompt, y_sample, k_prompt, v_prompt, ki_prompt, C_prompt, n_prompt, m_prompt, conv_prompt,
            k_sample, v_sample, ki_sample, C_sample, n_sample, m_sample, conv_sample)
```

```python
import math
import numpy as np
from contextlib import ExitStack
import concourse.bass as bass
import concourse.mybir as mybir
from concourse.bass_utils import run_bass_kernel_spmd

F32 = mybir.dt.float32
BF16 = mybir.dt.bfloat16
AF = mybir.ActivationFunctionType
ALU = mybir.AluOpType
AX = mybir.AxisListType

D = 1024
SEQ = 8192
NTA = 65
NTOK = NTA * 128
NOWN = 17
NQ = NOWN * 128
NH = NOWN * 131
DIN = 9808
DFF = 2816
EPS = 1e-6
NEG = -1.0e30
O_MQ, O_MK, O_MV, O_MO, O_MI, O_MF = 0, 1024, 2048, 3072, 4096, 4100
O_AQ, O_AK, O_AV, O_IQ, O_IK, O_IW, O_GA, O_GB = 4104, 5128, 6152, 7176, 7688, 7752, 7760, 8784
NCH = 130
NBIS = 18
import os
SUB = int(os.environ.get('KSUB', '99'))


class Stop(Exception):
    pass


CUT = 99


def cut(n):
    if CUT == n:
        raise Stop()


class Buf:
    __slots__ = ("name", "w", "r", "dsem", "dcnt")

    def __init__(self, name):
        self.name = name
        self.w = []
        self.r = []
        self.dsem = None
        self.dcnt = 0


class KB:
    def __init__(self):
        self.nc = bass.Bass("TRN2", target_bir_lowering=False)
        nc = self.nc
        self.eng = {"pe": nc.tensor, "act": nc.scalar, "dve": nc.vector,
                    "pool": nc.gpsimd, "sp": nc.sync}
        self.sem = {e: nc.alloc_semaphore("s_" + e) for e in self.eng}
        self.cnt = {e: 0 for e in self.eng}
        self.seen = {e: {} for e in self.eng}
        self.nsem = 0
        self.out_events = {}
        self.dma_bufs = []
        self.sem_pool = []

    def sb(self, es, name, shape, dt):
        t = es.enter_context(self.nc.sbuf_tensor(name, shape, dt))
        return t, Buf(name)

    def ps(self, es, name, shape, dt):
        t = es.enter_context(self.nc.psum_tensor(name, shape, dt))
        return t, Buf(name)

    def dram(self, name, shape, dt, kind):
        return self.nc.dram_tensor(name, shape, dt, kind=kind).ap()

    def _wait(self, e, ev, war=False):
        sem, val, src = ev
        if src == e and (e == "pe" or war):
            return
        key = id(sem)
        if self.seen[e].get(key, 0) >= val:
            return
        self.eng[e].wait_ge(sem, val)
        self.seen[e][key] = val

    def _deps(self, e, reads, writes):
        for b in reads:
            for ev in b.w:
                self._wait(e, ev)
        for b in writes:
            for ev in b.w:
                self._wait(e, ev)
            for ev in b.r:
                self._wait(e, ev, war=True)

    @staticmethod
    def _addr(b, ev):
        b.r = [x for x in b.r if x[0] is not ev[0]] + [ev]

    def op(self, e, fn, reads=(), writes=()):
        self._deps(e, reads, writes)
        ins = fn(self.eng[e])
        self.cnt[e] += 1
        ins.then_inc(self.sem[e], 1)
        ev = (self.sem[e], self.cnt[e], e)
        for b in writes:
            b.w = [ev]
            b.r = []
        for b in reads:
            self._addr(b, ev)
        return ev

    def dma(self, q, out, in_, sbuf, reads=(), writes=(), is_output=False, **kw):
        self._deps(q, reads, writes)
        if sbuf.dsem is None:
            if self.sem_pool:
                sbuf.dsem, sbuf.dcnt = self.sem_pool.pop()
            else:
                sbuf.dsem = self.nc.alloc_semaphore("d_%d" % self.nsem)
                self.nsem += 1
                sbuf.dcnt = 0
            self.dma_bufs.append(sbuf)
        ins = self.eng[q].dma_start(out=out, in_=in_, **kw)
        sbuf.dcnt += 16
        ins.then_inc(sbuf.dsem, 16)
        ev = (sbuf.dsem, sbuf.dcnt, None)
        for b in writes:
            b.w = [ev]
            b.r = []
        for b in reads:
            self._addr(b, ev)
        if is_output:
            self.out_events[id(sbuf.dsem)] = ev
        return ev

    def barrier(self):
        evs = [(self.sem[e], self.cnt[e], e) for e in self.eng if self.cnt[e] > 0]
        evs += [(b.dsem, b.dcnt, None) for b in self.dma_bufs]
        for e in self.eng:
            for ev in evs:
                if ev[2] == e and e == "pe":
                    continue
                self._wait(e, ev)
        for b in self.dma_bufs:
            self.sem_pool.append((b.dsem, b.dcnt))
            b.dsem = None
        self.dma_bufs = []

    def finish(self):
        for ev in self.out_events.values():
            self._wait("sp", ev)


def mk_ident(k, es, name, dt):
    t, b = k.sb(es, name, [128, 128], dt)
    k.op("pool", lambda e: e.memset(t[:, :], 1.0), writes=[b])
    k.op("pool", lambda e: e.affine_select(
        out=t[:, :], in_=t[:, :], pattern=[[-1, 128]], compare_op=ALU.is_equal,
        fill=0.0, base=0, channel_multiplier=1), reads=[b], writes=[b])
    return t, b


class WStream:
    def __init__(self, k, es, name, nslots, maxcols):
        self.k = k
        self.stg = [k.sb(es, "%s_stg%d" % (name, i), [128, 8, 128], F32) for i in range(2)]
        self.slots = [k.sb(es, "%s_w%d" % (name, i), [128, 8, maxcols], BF16) for i in range(nslots)]
        self.si = 0
        self.ci = 0

    def load(self, wdram, col0, ncols):
        k = self.k
        wt, wB = self.slots[self.si % len(self.slots)]
        self.si += 1
        wv = wdram.rearrange("(kc p) c -> p kc c", p=128)
        off = 0
        while off < ncols:
            m = min(128, ncols - off)
            st, sB = self.stg[self.ci % 2]
            k.dma("sp", st[:, :, 0:m], wv[:, :, col0 + off:col0 + off + m], sB, writes=[sB])
            eng = "dve" if self.ci % 2 == 0 else "pool"
            k.op(eng, lambda e, st=st, m=m, off=off: e.tensor_copy(
                out=wt[:, :, off:off + m], in_=st[:, :, 0:m]), reads=[sB], writes=[wB])
            off += m
            self.ci += 1
        return wt, wB


def head_norm(k, ft, fB, stt, stB, sq, sqB, gain3, gainB, neghalf, neghalfB, nh=8, P=128):
    hd = D // nh
    v3 = lambda t: t[0:P, :].rearrange("p (h d) -> p h d", h=nh)
    k.op("dve", lambda e: e.tensor_tensor(out=sq[0:P, :], in0=ft[0:P, :], in1=ft[0:P, :], op=ALU.mult),
         reads=[fB], writes=[sqB])
    k.op("dve", lambda e: e.tensor_reduce(out=stt[0:P, 0:nh], in_=v3(sq), axis=AX.X, op=ALU.add),
         reads=[sqB], writes=[stB])
    k.op("dve", lambda e: e.tensor_scalar(out=stt[0:P, 0:nh], in0=stt[0:P, 0:nh], scalar1=1.0 / hd,
                                          scalar2=EPS, op0=ALU.mult, op1=ALU.add), reads=[stB], writes=[stB])
    k.op("pool", lambda e: e.tensor_tensor(out=stt[0:P, 8:8 + nh], in0=stt[0:P, 0:nh], in1=neghalf[0:P, 0:nh],
                                           op=ALU.pow), reads=[stB, neghalfB], writes=[stB])
    k.op("dve", lambda e: e.tensor_tensor(out=v3(ft), in0=v3(ft),
                                          in1=stt[0:P, 8:8 + nh].unsqueeze(2).to_broadcast([P, nh, hd]),
                                          op=ALU.mult), reads=[fB, stB], writes=[fB])
    k.op("dve", lambda e: e.tensor_tensor(out=v3(ft), in0=v3(ft), in1=gain3, op=ALU.mult),
         reads=[fB, gainB], writes=[fB])


def phaseA(k, T):
    es = ExitStack()
    xb, w_in = T["xb"], T["w_in"]
    ident, identB = mk_ident(k, es, "identA", BF16)
    identf, identfB = mk_ident(k, es, "identfA", F32)
    neghalf, neghalfB = k.sb(es, "neghalfA", [128, 8], F32)
    k.op("pool", lambda e: e.memset(neghalf[:, :], -0.5), writes=[neghalfB])
    g1b, g1bB = k.sb(es, "g1b", [128, D], F32)
    k.dma("sp", g1b[:, :], T["g_norm1"][0:1, :].to_broadcast([128, D]), g1bB, writes=[g1bB])
    gkb, gkbB = k.sb(es, "gkb", [128, 128], F32)
    k.dma("sp", gkb[:, :], T["g_k"][0:1, :].to_broadcast([128, 128]), gkbB, writes=[gkbB])
    wck, wckB = k.sb(es, "wck", [128, 8, 4], F32)
    for j in range(4):
        k.dma("sp", wck[:, :, j], T["w_conv"][j:j + 1, 1024:2048].rearrange("o (g p) -> p (o g)", p=128),
              wckB, writes=[wckB], allow_slow_non_contiguous=True)
    bck, bckB = k.sb(es, "bck", [128, 8], F32)
    k.dma("sp", bck[:, :], T["b_conv"][0:1, 1024:2048].rearrange("o (g p) -> p (o g)", p=128),
          bckB, writes=[bckB], allow_slow_non_contiguous=True)
    dgk, dgkB = k.sb(es, "dgk", [128, 8, 4, 128], BF16)
    for g in range(8):
        for j in range(4):
            k.op("dve", lambda e, g=g, j=j: e.tensor_scalar(
                out=dgk[:, g, j, :], in0=identf[:, :], scalar1=wck[:, g, j:j + 1], scalar2=None,
                op0=ALU.mult), reads=[identfB, wckB], writes=[dgkB])

    NWA = 4168
    WA, WAB = k.sb(es, "WA", [128, 8, NWA], BF16)
    xs = [k.sb(es, "xs%d" % i, [128, D], F32) for i in range(3)]
    wa_src = [(O_MK, 1024), (O_MV, 1024), (O_AK, 1024), (O_AV, 1024), (O_MI, 8), (O_IK, 64)]
    w_view = w_in.rearrange("(kc p) c -> p kc c", p=128)
    dst = 0
    ci = 0
    WAparts = []
    for (src, n) in wa_src:
        off = 0
        while off < n:
            m = min(128, n - off)
            xt, xB = xs[ci % 2]
            stg = xt[:, 0:8 * 128].rearrange("p (k n) -> p k n", k=8)
            k.dma("sp", stg[:, :, 0:m], w_view[:, :, src + off:src + off + m], xB, writes=[xB])
            pb = Buf("WAp%d" % ci)
            eng = "dve" if ci % 2 == 0 else "pool"
            k.op(eng, lambda e, stg=stg, m=m, dst=dst: e.tensor_copy(
                out=WA[:, :, dst:dst + m], in_=stg[:, :, 0:m]), reads=[xB], writes=[pb])
            WAparts.append(pb)
            dst += m
            off += m
            ci += 1
    C_MK, C_MV, C_AK, C_AV, C_G = 0, 1024, 2048, 3072, 4096

    hb = [k.sb(es, "hb%d" % i, [128, D], BF16) for i in range(2)]
    ss = [k.sb(es, "ss%d" % i, [128, 4], F32) for i in range(2)]
    hT = [k.sb(es, "hT%d" % i, [128, 8, 512], BF16) for i in range(2)]
    kf = [k.sb(es, "kf%d" % i, [128, D], F32) for i in range(2)]
    vf = [k.sb(es, "vf%d" % i, [128, D], F32) for i in range(2)]
    sq, sqB = k.sb(es, "sqA", [128, D], F32)
    kn = [k.sb(es, "kn%d" % i, [128, D], BF16) for i in range(2)]
    vb = [k.sb(es, "vb%d" % i, [128, D], BF16) for i in range(2)]
    mvb = [k.sb(es, "mvb%d" % i, [128, D], BF16) for i in range(2)]
    sm = [k.sb(es, "sm%d" % i, [128, 72], F32) for i in range(2)]
    kid = [k.sb(es, "kid%d" % i, [128, 128], BF16) for i in range(2)]
    kst = [k.sb(es, "kst%d" % i, [128, 16], F32) for i in range(2)]
    KTs, KTsB = k.sb(es, "KTs", [128, 8, 512], BF16)
    kiTs, kiTsB = k.sb(es, "kiTs", [128, 512], BF16)
    pre, preB = k.sb(es, "pre", [128, 8, 3 + 512], BF16)
    pres, presB = k.sb(es, "pres", [128, 8, 2, 67], BF16)
    kTs, kTsB = k.sb(es, "kTs", [128, 8, 512], BF16)
    hstg, hstgB = k.sb(es, "hstg", [128, 8, 2, 3], F32)
    cvo, cvoB = k.sb(es, "cvo", [128, 3, 8, 3], F32)
    psT = [k.ps(es, "psT%d" % i, [128, 1024], BF16) for i in range(2)]
    psM = [k.ps(es, "psM%d" % i, [128, 512], F32) for i in range(6)]
    pmi = [0]

    def next_ps():
        p = psM[pmi[0] % 6]
        pmi[0] += 1
        return p

    k.op("pool", lambda e: e.memset(pre[:, :, 0:3], 0.0), writes=[preB])
    for s_ in range(2):
        for r_ in range(3):
            k.dma("sp", hstg[:, :, s_, r_],
                  T["state_conv"][s_, r_:r_ + 1, 1024:2048].rearrange("o (g p) -> p (o g)", p=128),
                  hstgB, writes=[hstgB], allow_slow_non_contiguous=True)
    k.op("dve", lambda e: e.tensor_copy(out=pres[:, :, :, 0:3], in_=hstg[:, :, :, :]),
         reads=[hstgB], writes=[presB])

    tiles = [(bi, ti) for bi in range(17) for ti in range(4 if bi < 16 else 1)]

    def ctx(idx):
        bi, ti = tiles[idx]
        Tt = bi * 4 + ti
        return bi, ti, Tt, xs[idx % 3], hb[idx % 2], ss[idx % 2], hT[bi % 2]

    def load_x(idx):
        bi, ti, Tt, (xt, xB), _, _, _ = ctx(idx)
        k.dma("sp", xt[:, :], xb[Tt * 128:(Tt + 1) * 128, :], xB, writes=[xB])

    def prologue_a(idx):
        bi, ti, Tt, (xt, xB), (hbt, hbB), (sst, ssB), (hTt, hTB) = ctx(idx)
        k.op("act", lambda e: e.activation(out=hbt[:, :], in_=xt[:, :], func=AF.Square,
                                           accum_out=sst[:, 0:1]), reads=[xB], writes=[hbB, ssB])
        k.op("dve", lambda e: e.tensor_scalar(out=sst[:, 1:2], in0=sst[:, 0:1], scalar1=1.0 / D,
                                              scalar2=EPS, op0=ALU.mult, op1=ALU.add),
             reads=[ssB], writes=[ssB])
        k.op("pool", lambda e: e.tensor_tensor(out=sst[:, 2:3], in0=sst[:, 1:2], in1=neghalf[:, 0:1],
                                               op=ALU.pow), reads=[ssB, neghalfB], writes=[ssB])
        k.op("dve", lambda e: e.scalar_tensor_tensor(out=hbt[:, :], in0=xt[:, :], scalar=sst[:, 2:3],
                                                     in1=g1b[:, :], op0=ALU.mult, op1=ALU.mult),
             reads=[xB, ssB, g1bB], writes=[hbB])

    def prologue_b(idx):
        bi, ti, Tt, (xt, xB), (hbt, hbB), (sst, ssB), (hTt, hTB) = ctx(idx)
        pT, pTB = psT[0]
        for kc in range(8):
            k.op("pe", lambda e, kc=kc: e.transpose(out=pT[:, kc * 128:(kc + 1) * 128],
                                                    in_=hbt[:, kc * 128:(kc + 1) * 128],
                                                    identity=ident[:, :]),
                 reads=[hbB, identB], writes=[pTB])
        k.op("act", lambda e: e.activation(
            out=hTt[:, :, ti * 128:(ti + 1) * 128],
            in_=pT[:, :].rearrange("p (k n) -> p k n", k=8), func=AF.Copy),
            reads=[pTB], writes=[hTB])


    def tile_body(idx):
        bi, ti, Tt, (xt, xB), (hbt, hbB), (sst, ssB), (hTt, hTB) = ctx(idx)
        s = Tt % 2
        def tokmm(c0, n):
            pm, pmB = next_ps()
            for kc in range(8):
                k.op("pe", lambda e, kc=kc: e.matmul(
                    pm[:, 0:n], lhsT=hTt[:, kc, ti * 128:(ti + 1) * 128],
                    rhs=WA[:, kc, c0:c0 + n], start=(kc == 0), stop=(kc == 7)),
                    reads=[hTB] + (WAparts if kc == 0 else []), writes=[pmB])
            return pm, pmB

        mvt, mvB = mvb[s]
        for hh in range(2):
            pm, pmB = tokmm(C_MV + hh * 512, 512)
            k.op("act", lambda e, hh=hh, pm=pm: e.activation(
                out=mvt[:, hh * 512:(hh + 1) * 512], in_=pm[:, :], func=AF.Copy),
                reads=[pmB], writes=[mvB])
        k.dma("sp", T["mv_d"][Tt * 128:(Tt + 1) * 128, :], mvt[:, :], mvB, reads=[mvB])
        if idx > 0 and tiles[idx - 1][0] == bi:
            tile_tail(idx - 1)
        kft, kfB = kf[s]
        kstt, kstB = kst[s]
        for hh in range(2):
            pm, pmB = tokmm(C_AK + hh * 512, 512)
            k.op("act", lambda e, hh=hh, pm=pm: e.activation(
                out=kft[:, hh * 512:(hh + 1) * 512], in_=pm[:, :], func=AF.Copy),
                reads=[pmB], writes=[kfB])
        head_norm(k, kft, kfB, kstt, kstB, sq, sqB, gkb[:, :].unsqueeze(1).to_broadcast([128, 8, 128]), gkbB, neghalf, neghalfB)
        k.dma("sp", T["k_out"][Tt * 128:(Tt + 1) * 128, :], kft[:, :], kfB, reads=[kfB], is_output=True)
        knt, knB = kn[s]
        k.op("pool", lambda e: e.tensor_copy(out=knt[:, :], in_=kft[:, :]), reads=[kfB], writes=[knB])
        if idx + 1 < len(tiles):
            prologue_b(idx + 1)
        vft, vfB = vf[s]
        for hh in range(2):
            pm, pmB = tokmm(C_AV + hh * 512, 512)
            k.op("act", lambda e, hh=hh, pm=pm: e.activation(
                out=vft[:, hh * 512:(hh + 1) * 512], in_=pm[:, :], func=AF.Copy),
                reads=[pmB], writes=[vfB])
        k.dma("sp", T["v_out"][Tt * 128:(Tt + 1) * 128, :], vft[:, :], vfB, reads=[vfB], is_output=True)
        vbt, vbB = vb[s]
        k.op("pool", lambda e: e.tensor_copy(out=vbt[:, :], in_=vft[:, :]), reads=[vfB], writes=[vbB])
        k.dma("sp", T["V_d"][Tt * 128:(Tt + 1) * 128, :], vbt[:, :], vbB, reads=[vbB])
        smt, smB = sm[s]
        pm, pmB = tokmm(C_G, 72)
        k.op("act", lambda e, pm=pm: e.activation(out=smt[:, :], in_=pm[:, 0:72], func=AF.Copy),
             reads=[pmB], writes=[smB])
        k.dma("sp", T["g_d"][Tt * 128:(Tt + 1) * 128, :], smt[:, 0:8], smB, reads=[smB])
        k.dma("sp", T["ki_out"][Tt * 128:(Tt + 1) * 128, :], smt[:, 8:72], smB, reads=[smB],
              is_output=True)
        kidt, kidB = kid[s]
        k.op("dve", lambda e: e.tensor_copy(
            out=kidt[:, :].rearrange("p (a d) -> p a d", a=2),
            in_=smt[:, 8:72].unsqueeze(1).to_broadcast([128, 2, 64])), reads=[smB], writes=[kidB])

    def tile_tail(idx):
        bi, ti, Tt, (xt, xB), (hbt, hbB), (sst, ssB), (hTt, hTB) = ctx(idx)
        s = Tt % 2
        knt, knB = kn[s]
        kidt, kidB = kid[s]
        pT2, pT2B = psT[1]
        for h in range(8):
            k.op("pe", lambda e, h=h: e.transpose(out=pT2[:, h * 128:(h + 1) * 128],
                                                  in_=knt[:, h * 128:(h + 1) * 128],
                                                  identity=ident[:, :]),
                 reads=[knB, identB], writes=[pT2B])
        k.op("act", lambda e: e.activation(
            out=KTs[:, :, ti * 128:(ti + 1) * 128],
            in_=pT2[:, :].rearrange("p (k n) -> p k n", k=8), func=AF.Copy),
            reads=[pT2B], writes=[KTsB])
        pT2, pT2B = psT[1]
        k.op("pe", lambda e: e.transpose(out=pT2[:, 0:128], in_=kidt[:, :], identity=ident[:, :]),
             reads=[kidB, identB], writes=[pT2B])
        k.op("act", lambda e: e.activation(out=kiTs[:, ti * 128:(ti + 1) * 128], in_=pT2[:, 0:128],
                                           func=AF.Copy), reads=[pT2B], writes=[kiTsB])


    def block_epilogue(bi):
        ntile = 4 if bi < 16 else 1
        ntok = ntile * 128
        tok0 = bi * 512
        hTt, hTB = hT[bi % 2]
        if bi < 16:
            k.dma("sp", T["hTo_d"][:, :, bi * 131:(bi + 1) * 131], hTt[:, :, 381:512], hTB, reads=[hTB])
        else:
            k.dma("sp", T["hTo_d"][:, :, 16 * 131 + 3:17 * 131], hTt[:, :, 0:128], hTB, reads=[hTB])
            k.op("pool", lambda e: e.memset(pre[:, :, 0:3], 0.0), reads=[preB], writes=[preB])
            k.dma("sp", T["hTo_d"][:, :, 16 * 131:16 * 131 + 3], pre[:, :, 0:3], preB, reads=[preB])
        k.dma("sp", T["KT_d"][:, :, tok0:tok0 + ntok].rearrange("h d t -> d h t"), KTs[:, :, 0:ntok], KTsB,
              reads=[KTsB])
        k.dma("sp", T["kiT_d"][:, tok0:tok0 + ntok], kiTs[:, 0:ntok], kiTsB, reads=[kiTsB])
        for cg in range(8):
            pm, pmB = next_ps()
            for kc in range(8):
                k.op("pe", lambda e, kc=kc: e.matmul(
                    pm[:, 0:ntok], lhsT=WA[:, kc, C_MK + cg * 128:C_MK + (cg + 1) * 128],
                    rhs=hTt[:, kc, 0:ntok], start=(kc == 0), stop=(kc == 7)),
                    reads=[hTB] + (WAparts if kc == 0 else []), writes=[pmB])
            if bi < 16:
                k.op("act", lambda e: e.activation(out=pre[:, cg, 3:3 + ntok], in_=pm[:, 0:ntok],
                                                   func=AF.Copy), reads=[pmB], writes=[preB])
                if bi == 15:
                    k.op("act", lambda e: e.activation(out=cvo[:, 0, cg, :], in_=pm[:, 509:512], func=AF.Copy),
                         reads=[pmB], writes=[cvoB])
            else:
                k.op("act", lambda e: e.activation(
                    out=pres[:, cg, :, 3:67], in_=pm[:, 0:128].rearrange("p (s t) -> p s t", s=2),
                    func=AF.Copy), reads=[pmB], writes=[presB])
                k.op("act", lambda e: e.activation(
                    out=cvo[:, 1:3, cg, :], in_=pm[:, 0:128].rearrange("p (s t) -> p s t", s=2)[:, :, 61:64],
                    func=AF.Copy), reads=[pmB], writes=[cvoB])
        for cg in range(8):
            pm, pmB = next_ps()
            if bi < 16:
                for j in range(4):
                    k.op("pe", lambda e, j=j: e.matmul(
                        pm[:, 0:ntok], lhsT=dgk[:, cg, j, :], rhs=pre[:, cg, j:j + ntok],
                        start=(j == 0), stop=(j == 3)), reads=[dgkB, preB], writes=[pmB])
            else:
                for sq_i in range(2):
                    for j in range(4):
                        k.op("pe", lambda e, j=j, sq_i=sq_i: e.matmul(
                            pm[:, sq_i * 64:(sq_i + 1) * 64], lhsT=dgk[:, cg, j, :],
                            rhs=pres[:, cg, sq_i, j:j + 64], start=(j == 0), stop=(j == 3)),
                            reads=[dgkB, presB], writes=[pmB])
            k.op("act", lambda e: e.activation(out=kTs[:, cg, 0:ntok], in_=pm[:, 0:ntok], func=AF.Silu,
                                               bias=bck[:, cg:cg + 1]), reads=[pmB, bckB], writes=[kTsB])
        k.dma("sp", T["kT_d"][:, :, tok0:tok0 + ntok].rearrange("g p t -> p g t"), kTs[:, :, 0:ntok], kTsB,
              reads=[kTsB])
        if bi < 15:
            k.op("dve", lambda e: e.tensor_copy(out=pre[:, :, 0:3], in_=pre[:, :, 512:515]),
                 reads=[preB], writes=[preB])

    load_x(0)
    load_x(1)
    prologue_a(0)
    prologue_b(0)
    for idx in range(len(tiles)):
        if idx + 2 < len(tiles):
            load_x(idx + 2)
        if idx + 1 < len(tiles):
            prologue_a(idx + 1)
        tile_body(idx)
        bi, ti = tiles[idx]
        if idx + 1 == len(tiles) or tiles[idx + 1][0] != bi:
            tile_tail(idx)
            block_epilogue(bi)
    for a_ in range(3):
        k.dma("pool", T["cvk_out"][a_].rearrange("(g p) r -> p g r", p=128), cvo[:, a_, :, :], cvoB,
              reads=[cvoB], is_output=True, allow_slow_non_contiguous=True)
    k.barrier()
    es.close()


def phaseB(k, T):
    es = ExitStack()
    w_in = T["w_in"]
    ident, identB = mk_ident(k, es, "identB", BF16)
    identf, identfB = mk_ident(k, es, "identfB", F32)
    neghalf, neghalfB = k.sb(es, "neghalfB", [128, 8], F32)
    k.op("pool", lambda e: e.memset(neghalf[:, :], -0.5), writes=[neghalfB])
    hTo, hToB = k.sb(es, "hTo", [128, 8, NH], BF16)
    k.dma("sp", hTo[:, :, :], T["hTo_d"][:, :, :], hToB, writes=[hToB])
    ws = WStream(k, es, "wsB", 2, 1024)
    big, bigB = k.sb(es, "bigB", [128, 8, NQ], BF16)
    psM = [k.ps(es, "psMB%d" % i, [128, 512], F32) for i in range(6)]
    psT = [k.ps(es, "psTB%d" % i, [128, 1024], BF16) for i in range(2)]
    pmi = [0]

    def next_ps():
        p = psM[pmi[0] % 6]
        pmi[0] += 1
        return p

    wqk, wqkB = k.sb(es, "wqk", [128, 8, 4], F32)
    for j in range(4):
        k.dma("sp", wqk[:, :, j], T["w_conv"][j:j + 1, 0:1024].rearrange("o (g p) -> p (o g)", p=128),
              wqkB, writes=[wqkB], allow_slow_non_contiguous=True)
    bcq, bcqB = k.sb(es, "bcq", [128, 8], F32)
    k.dma("sp", bcq[:, :], T["b_conv"][0:1, 0:1024].rearrange("o (g p) -> p (o g)", p=128),
          bcqB, writes=[bcqB], allow_slow_non_contiguous=True)
    dgq, dgqB = k.sb(es, "dgq", [128, 8, 4, 128], BF16)
    for g in range(8):
        for j in range(4):
            k.op("dve", lambda e, g=g, j=j: e.tensor_scalar(
                out=dgq[:, g, j, :], in0=identf[:, :], scalar1=wqk[:, g, j:j + 1], scalar2=None,
                op0=ALU.mult), reads=[identfB, wqkB], writes=[dgqB])
    hq, hqB = k.sb(es, "hq", [128, 8, 2, 3], F32)
    for s_ in range(2):
        for r_ in range(3):
            k.dma("sp", hq[:, :, s_, r_],
                  T["state_conv"][s_, r_:r_ + 1, 0:1024].rearrange("o (g p) -> p (o g)", p=128),
                  hqB, writes=[hqB], allow_slow_non_contiguous=True)
    preq = [k.sb(es, "preq%d" % i, [128, 2240], BF16) for i in range(2)]
    prb = [k.sb(es, "prb%d" % i, [128, 96], BF16) for i in range(2)]
    cvq, cvqB = k.sb(es, "cvq", [128, 3, 8, 3], F32)
    cut(1)
    W, WB = ws.load(w_in, O_MQ, 1024)
    cut(2)
    for cg in range(8):
        pq, pqB = preq[cg % 2]
        pb_, pbB = prb[cg % 2]
        for grp in range(5):
            n = min(512, NH - grp * 512)
            pm, pmB = next_ps()
            for kc in range(8):
                k.op("pe", lambda e, kc=kc: e.matmul(
                    pm[:, 0:n], lhsT=W[:, kc, cg * 128:(cg + 1) * 128],
                    rhs=hTo[:, kc, grp * 512:grp * 512 + n], start=(kc == 0), stop=(kc == 7)),
                    reads=[hToB, WB], writes=[pmB])
            if cg == 0 and grp == 0:
                cut(31)
            k.op("act", lambda e: e.activation(out=pq[:, grp * 512:grp * 512 + n], in_=pm[:, 0:n],
                                               func=AF.Copy), reads=[pmB], writes=[pqB])
            if cg == 0 and grp == 0:
                cut(32)
            if cg == 0 and grp == 3:
                cut(33)
            if cg == 0 and grp == 4:
                cut(34)
            if grp == 4:
                for a_, o_ in ((0, 15 * 131 + 128 - 2048), (1, 16 * 131 + 3 + 61 - 2048),
                               (2, 16 * 131 + 3 + 125 - 2048))[0:int(os.environ.get("NCVQ", "3"))]:
                    k.op("act", lambda e, a_=a_, o_=o_: e.activation(out=cvq[:, a_, cg, :],
                                                                     in_=pm[:, o_:o_ + 3], func=AF.Copy),
                         reads=[pmB], writes=[cvqB])
        if cg == 0:
            cut(3)
        k.op("dve", lambda e: e.tensor_copy(out=pq[:, 16 * 131:16 * 131 + 3], in_=hq[:, cg, 0, :]),
             reads=[hqB], writes=[pqB])
        k.op("dve", lambda e: e.tensor_copy(out=pb_[:, 0:3], in_=hq[:, cg, 1, :]), reads=[hqB], writes=[pbB])
        k.op("dve", lambda e: e.tensor_copy(out=pb_[:, 3:67], in_=pq[:, 16 * 131 + 67:16 * 131 + 131]),
             reads=[pqB], writes=[pbB])
        if cg == 0:
            cut(4)
        for g0 in range(0, 17, 4):
            ng = min(4, 17 - g0)
            pm, pmB = next_ps()
            for gi in range(g0, g0 + ng):
                if gi < 16:
                    for j in range(4):
                        k.op("pe", lambda e, j=j, gi=gi: e.matmul(
                            pm[:, (gi - g0) * 128:(gi - g0 + 1) * 128], lhsT=dgq[:, cg, j, :],
                            rhs=pq[:, gi * 131 + j:gi * 131 + j + 128], start=(j == 0), stop=(j == 3)),
                            reads=[dgqB, pqB], writes=[pmB])
                else:
                    for j in range(4):
                        k.op("pe", lambda e, j=j: e.matmul(
                            pm[:, 0:64], lhsT=dgq[:, cg, j, :], rhs=pq[:, 16 * 131 + j:16 * 131 + j + 64],
                            start=(j == 0), stop=(j == 3)), reads=[dgqB, pqB], writes=[pmB])
                    for j in range(4):
                        k.op("pe", lambda e, j=j: e.matmul(
                            pm[:, 64:128], lhsT=dgq[:, cg, j, :], rhs=pb_[:, j:j + 64],
                            start=(j == 0), stop=(j == 3)), reads=[dgqB, pbB], writes=[pmB])
            k.op("act", lambda e: e.activation(out=big[:, cg, g0 * 128:(g0 + ng) * 128], in_=pm[:, 0:ng * 128],
                                               func=AF.Silu, bias=bcq[:, cg:cg + 1]),
                 reads=[pmB, bcqB], writes=[bigB])
    cut(5)
    k.dma("pool", T["qT_d"][:, :, :], big[:, :, :], bigB, reads=[bigB])
    cut(6)
    for a_ in range(3):
        k.dma("pool", T["cvq_out"][a_].rearrange("(g p) r -> p g r", p=128), cvq[:, a_, :, :], cvqB,
              reads=[cvqB], is_output=True, allow_slow_non_contiguous=True)

    if SUB <= 1:
        k.barrier()
        es.close()
        return
    sob = [k.sb(es, "sob%d" % i, [128, D], BF16) for i in range(2)]
    W, WB = ws.load(w_in, O_MO, 1024)
    for gi in range(NOWN):
        st, sB = sob[gi % 2]
        for hh in range(2):
            pm, pmB = next_ps()
            for kc in range(8):
                k.op("pe", lambda e, kc=kc: e.matmul(
                    pm[:, :], lhsT=hTo[:, kc, gi * 131 + 3:gi * 131 + 131],
                    rhs=W[:, kc, hh * 512:(hh + 1) * 512], start=(kc == 0), stop=(kc == 7)),
                    reads=[hToB, WB], writes=[pmB])
            k.op("act", lambda e: e.activation(out=st[:, hh * 512:(hh + 1) * 512], in_=pm[:, :],
                                               func=AF.Sigmoid), reads=[pmB], writes=[sB])
        k.dma("pool", T["so_d"][gi * 128:(gi + 1) * 128, :], st[:, :], sB, reads=[sB])

    if SUB <= 2:
        k.barrier()
        es.close()
        return
    gqb, gqbB = k.sb(es, "gqb", [128, 128], F32)
    k.dma("sp", gqb[:, :], T["g_q"][0:1, :].to_broadcast([128, 128]), gqbB, writes=[gqbB])
    k.op("dve", lambda e: e.tensor_scalar(out=gqb[:, :], in0=gqb[:, :], scalar1=128.0 ** -0.5, scalar2=None,
                                          op0=ALU.mult), reads=[gqbB], writes=[gqbB])
    qf = [k.sb(es, "qf%d" % i, [128, D], F32) for i in range(2)]
    qst = [k.sb(es, "qst%d" % i, [128, 16], F32) for i in range(2)]
    sq, sqB = k.sb(es, "sqB", [128, D], F32)
    qn = [k.sb(es, "qn%d" % i, [128, D], BF16) for i in range(2)]
    W, WB = ws.load(w_in, O_AQ, 1024)
    for gi in range(NOWN):
        qt, qB = qf[gi % 2]
        stt, stB = qst[gi % 2]
        for hh in range(2):
            pm, pmB = next_ps()
            for kc in range(8):
                k.op("pe", lambda e, kc=kc: e.matmul(
                    pm[:, :], lhsT=hTo[:, kc, gi * 131 + 3:gi * 131 + 131],
                    rhs=W[:, kc, hh * 512:(hh + 1) * 512], start=(kc == 0), stop=(kc == 7)),
                    reads=[hToB, WB], writes=[pmB])
            k.op("act", lambda e: e.activation(out=qt[:, hh * 512:(hh + 1) * 512], in_=pm[:, :],
                                               func=AF.Copy), reads=[pmB], writes=[qB])
        head_norm(k, qt, qB, stt, stB, sq, sqB, gqb[:, :].unsqueeze(1).to_broadcast([128, 8, 128]), gqbB, neghalf, neghalfB)
        qnt, qnB = qn[gi % 2]
        k.op("pool", lambda e: e.tensor_copy(out=qnt[:, :], in_=qt[:, :]), reads=[qB], writes=[qnB])
        pT, pTB = psT[gi % 2]
        for h in range(8):
            k.op("pe", lambda e, h=h: e.transpose(out=pT[:, h * 128:(h + 1) * 128],
                                                  in_=qnt[:, h * 128:(h + 1) * 128], identity=ident[:, :]),
                 reads=[qnB, identB], writes=[pTB])
        k.op("act", lambda e: e.activation(out=big[:, :, gi * 128:(gi + 1) * 128],
                                           in_=pT[:, :].rearrange("p (k n) -> p k n", k=8), func=AF.Copy),
             reads=[pTB], writes=[bigB])
    k.dma("pool", T["QT_d"][:, :, :], big[:, :, :], bigB, reads=[bigB])

    if SUB <= 3:
        k.barrier()
        es.close()
        return
    def feat_major(W, WB, ncg, func, scale=1.0):
        for cg in range(ncg):
            for g0 in range(0, 17, 4):
                ng = min(4, 17 - g0)
                pm, pmB = next_ps()
                for gi in range(g0, g0 + ng):
                    for kc in range(8):
                        k.op("pe", lambda e, kc=kc, gi=gi: e.matmul(
                            pm[:, (gi - g0) * 128:(gi - g0 + 1) * 128],
                            lhsT=W[:, kc, cg * 128:(cg + 1) * 128],
                            rhs=hTo[:, kc, gi * 131 + 3:gi * 131 + 131], start=(kc == 0), stop=(kc == 7)),
                            reads=[hToB, WB], writes=[pmB])
                k.op("act", lambda e: e.activation(out=big[:, cg, g0 * 128:(g0 + ng) * 128],
                                                   in_=pm[:, 0:ng * 128], func=func),
                     reads=[pmB], writes=[bigB])

    W, WB = ws.load(w_in, O_IQ, 512)
    feat_major(W, WB, 4, AF.Copy)
    k.dma("pool", T["qiT_d"][:, :, :], big[:, 0:4, :], bigB, reads=[bigB])

    if SUB <= 4:
        k.barrier()
        es.close()
        return
    iwa, iwaB = k.sb(es, "iwa", [128, NOWN, 8], F32)
    W, WB = ws.load(w_in, O_IW, 8)
    for gi in range(NOWN):
        pm, pmB = next_ps()
        for kc in range(8):
            k.op("pe", lambda e, kc=kc: e.matmul(
                pm[:, 0:8], lhsT=hTo[:, kc, gi * 131 + 3:gi * 131 + 131], rhs=W[:, kc, 0:8],
                start=(kc == 0), stop=(kc == 7)), reads=[hToB, WB], writes=[pmB])
        k.op("act", lambda e: e.activation(out=iwa[:, gi, :], in_=pm[:, 0:8], func=AF.Copy,
                                           scale=512.0 ** -0.5), reads=[pmB], writes=[iwaB])
    k.dma("pool", T["iw_d"][:, :], iwa[:, :, :].rearrange("p g h -> p (g h)"), iwaB, reads=[iwaB])

    if SUB <= 5:
        k.barrier()
        es.close()
        return
    for (o_, dname) in ((O_GA, "gaT_d"), (O_GB, "gbT_d")):
        W, WB = ws.load(w_in, o_, 1024)
        feat_major(W, WB, 8, AF.Sigmoid)
        k.dma("pool", T[dname][:, :, :], big[:, :, :], bigB, reads=[bigB])
    k.barrier()
    es.close()


def phaseC(k, T):
    es = ExitStack()
    ident, identB = mk_ident(k, es, "identC", BF16)
    identf, identfB = mk_ident(k, es, "identfC", F32)
    neghalf, neghalfB = k.sb(es, "neghalfC", [128, 8], F32)
    k.op("pool", lambda e: e.memset(neghalf[:, :], -0.5), writes=[neghalfB])
    ones64, ones64B = k.sb(es, "ones64", [128, 64], F32)
    k.op("pool", lambda e: e.memset(ones64[:, :], 1.0), writes=[ones64B])
    tri, triB = k.sb(es, "tri16", [64, 64], F32)
    k.op("pool", lambda e: e.memset(tri[:, :], 1.0 / 16.0), writes=[triB])
    k.op("pool", lambda e: e.affine_select(out=tri[:, :], in_=tri[:, :], pattern=[[1, 64]],
                                           compare_op=ALU.is_ge, fill=0.0, base=0, channel_multiplier=-1),
         reads=[triB], writes=[triB])
    bifb, bifbB = k.sb(es, "bifb", [128, 8], F32)
    k.dma("sp", bifb[:, :], T["b_if"][0:1, :].to_broadcast([128, 8]), bifbB, writes=[bifbB])
    gmnb, gmnbB = k.sb(es, "gmnb", [64, D], F32)
    k.dma("sp", gmnb[:, :], T["g_mnorm"][0:1, :].to_broadcast([64, D]), gmnbB, writes=[gmnbB])

    pX = k.ps(es, "pXC", [128, 512], F32)
    pK = k.ps(es, "pKC", [128, 1024], BF16)
    pC = [k.ps(es, "pCC%d" % i, [128, 2, 256], F32) for i in range(2)]
    pN = pX
    pS = k.ps(es, "pSC", [64, 256], F32)
    pI = k.ps(es, "pIC", [64, 2, 512], F32)
    pA = k.ps(es, "pAC", [64, 512], F32)

    def gate_prep(P, tag, g_rows, vm_rows, mp_tile, mp_B, row0):
        G, GB = k.sb(es, "G" + tag, [P, 64, 8], F32)
        k.dma("sp", G[:, :, :], T["g_d"][g_rows[0]:g_rows[1], :].rearrange("(c l) q -> c l q", l=64), GB,
              writes=[GB])
        vm, vmB = k.sb(es, "vm" + tag, [P, 64], F32)
        k.dma("sp", vm[:, :], T["tokvalid"][vm_rows[0]:vm_rows[1]].rearrange("(c l) -> c l", l=64), vmB,
              writes=[vmB])
        pen, penB = k.sb(es, "pen" + tag, [P, 64], F32)
        k.op("dve", lambda e: e.tensor_scalar(out=pen[:, :], in0=vm[:, :], scalar1=-1.0, scalar2=1.0e30,
                                              op0=ALU.add, op1=ALU.mult), reads=[vmB], writes=[penB])
        k.op("dve", lambda e: e.tensor_tensor(out=G[:, :, :], in0=G[:, :, :],
                                              in1=bifb[0:P, :].unsqueeze(1).to_broadcast([P, 64, 8]),
                                              op=ALU.add), reads=[GB, bifbB], writes=[GB])
        E, EB = k.sb(es, "E" + tag, [P, 64, 4], F32)
        k.op("act", lambda e: e.activation(out=E[:, :, :], in_=G[:, :, 4:8], func=AF.Exp, scale=-1.0),
             reads=[GB], writes=[EB])
        k.op("act", lambda e: e.activation(out=E[:, :, :], in_=E[:, :, :], func=AF.Ln, bias=1.0, scale=1.0),
             reads=[EB], writes=[EB])
        vm3 = vm[:, :].unsqueeze(2).to_broadcast([P, 64, 4])
        LF, LFB = k.sb(es, "LF" + tag, [P, 64, 4], F32)
        k.op("dve", lambda e: e.scalar_tensor_tensor(out=LF[:, :, :], in0=E[:, :, :], scalar=-1.0, in1=vm3,
                                                     op0=ALU.mult, op1=ALU.mult), reads=[EB, vmB], writes=[LFB])
        LI, LIB = k.sb(es, "LI" + tag, [P, 64, 4], F32)
        k.op("dve", lambda e: e.tensor_tensor(out=LI[:, :, :], in0=G[:, :, 0:4], in1=vm3, op=ALU.mult),
             reads=[GB, vmB], writes=[LIB])
        k.op("dve", lambda e: e.tensor_tensor(out=LI[:, :, :], in0=LI[:, :, :],
                                              in1=pen[:, :].unsqueeze(2).to_broadcast([P, 64, 4]), op=ALU.add),
             reads=[LIB, penB], writes=[LIB])
        Bc, BcB = k.sb(es, "Bc" + tag, [P, 64, 4], F32)
        for h in range(4):
            k.op("dve", lambda e, h=h: e.tensor_tensor_scan(out=Bc[:, :, h], data0=ones64[0:P, :],
                                                            data1=LF[:, :, h], initial=0.0, op0=ALU.mult,
                                                            op1=ALU.add), reads=[LFB, ones64B], writes=[BcB])
        Cc, CcB = k.sb(es, "Cc" + tag, [P, 64, 4], F32)
        k.op("dve", lambda e: e.tensor_tensor(out=Cc[:, :, :], in0=LI[:, :, :], in1=Bc[:, :, :],
                                              op=ALU.subtract), reads=[LIB, BcB], writes=[CcB])
        CM, CMB = k.sb(es, "CM" + tag, [P, 64, 4], F32)
        for h in range(4):
            k.op("dve", lambda e, h=h: e.tensor_tensor_scan(out=CM[:, :, h], data0=ones64[0:P, :],
                                                            data1=Cc[:, :, h], initial=-3.0e38, op0=ALU.mult,
                                                            op1=ALU.max), reads=[CcB, ones64B], writes=[CMB])
        BX, BXB = k.sb(es, "BX" + tag, [P, 8], F32)
        k.op("dve", lambda e: e.tensor_copy(out=BX[:, 0:4], in_=Bc[:, 63, :]), reads=[BcB], writes=[BXB])
        k.op("dve", lambda e: e.tensor_copy(out=BX[:, 4:8], in_=CM[:, 63, :]), reads=[CMB], writes=[BXB])
        mnew = None
        if mp_tile is None:
            BT, BTB = k.sb(es, "BT" + tag, [4, 128], F32)
            XT, XTB = k.sb(es, "XT" + tag, [4, 128], F32)
            bxd = Buf("bx_d")
            k.dma("pool", T["bx_d"][:, :], BX[:, :], BXB, reads=[BXB], writes=[bxd])
            k.dma("sp", BT[:, :], T["bx_d"][:, 0:4].rearrange("c q -> q c"), BTB, reads=[bxd], writes=[BTB],
                  allow_slow_non_contiguous=True)
            k.dma("sp", XT[:, :], T["bx_d"][:, 4:8].rearrange("c q -> q c"), XTB, reads=[bxd], writes=[XTB],
                  allow_slow_non_contiguous=True)
            MN, MNB = k.sb(es, "MN" + tag, [4, 128], F32)
            k.op("dve", lambda e: e.tensor_tensor_scan(out=MN[:, :], data0=XT[:, :], data1=BT[:, :],
                                                       initial=0.0, op0=ALU.max, op1=ALU.add),
                 reads=[XTB, BTB], writes=[MNB])
            MP, MPB = k.sb(es, "MP" + tag, [4, 128], F32)
            k.op("dve", lambda e: e.memset(MP[:, 0:1], 0.0), writes=[MPB])
            k.op("dve", lambda e: e.tensor_copy(out=MP[:, 1:128], in_=MN[:, 0:127]), reads=[MNB], writes=[MPB])
            mpd = Buf("mp_d")
            k.dma("pool", T["mp_d"][:, :], MP[:, :], MPB, reads=[MPB], writes=[mpd])
            mp, mpB = k.sb(es, "mp" + tag, [P, 4], F32)
            k.dma("sp", mp[:, :], T["mp_d"].rearrange("q c -> c q"), mpB, reads=[mpd], writes=[mpB],
                  allow_slow_non_contiguous=True)
            k.dma("pool", T["m_out"][0:1, :].rearrange("o h -> h o"), MN[:, 127:128], MNB, reads=[MNB],
                  is_output=True, allow_slow_non_contiguous=True)
        else:
            mp, mpB = mp_tile, mp_B
            mnew, mnewB = k.sb(es, "mnew" + tag, [P, 4], F32)
            k.op("dve", lambda e: e.tensor_tensor(out=mnew[:, :], in0=mp[:, :], in1=BX[:, 4:8], op=ALU.max),
                 reads=[mpB, BXB], writes=[mnewB])
            k.op("dve", lambda e: e.tensor_tensor(out=mnew[:, :], in0=mnew[:, :], in1=BX[:, 0:4], op=ALU.add),
                 reads=[mnewB, BXB], writes=[mnewB])
            k.dma("pool", T["m_out"][1:3, :], mnew[:, :], mnewB, reads=[mnewB], is_output=True)
        mp3 = mp[:, :].unsqueeze(1).to_broadcast([P, 64, 4])
        MM, MMB = k.sb(es, "MM" + tag, [P, 64, 4], F32)
        k.op("dve", lambda e: e.tensor_tensor(out=MM[:, :, :], in0=CM[:, :, :], in1=mp3, op=ALU.max),
             reads=[CMB, mpB], writes=[MMB])
        U, UB = k.sb(es, "U" + tag, [P, 64, 4], F32)
        k.op("dve", lambda e: e.tensor_tensor(out=U[:, :, :], in0=MM[:, :, :], in1=mp3, op=ALU.subtract),
             reads=[MMB, mpB], writes=[UB])
        k.op("act", lambda e: e.activation(out=U[:, :, :], in_=U[:, :, :], func=AF.Exp, scale=-1.0),
             reads=[UB], writes=[UB])
        GX, GXB = k.sb(es, "GX" + tag, [P, 64, 4], F32)
        k.op("dve", lambda e: e.tensor_tensor(out=GX[:, :, :], in0=Cc[:, :, :],
                                              in1=MM[:, 63, :].unsqueeze(1).to_broadcast([P, 64, 4]),
                                              op=ALU.subtract), reads=[CcB, MMB], writes=[GXB])
        k.op("dve", lambda e: e.tensor_scalar(out=GX[:, :, :], in0=GX[:, :, :], scalar1=0.0, scalar2=-80.0,
                                              op0=ALU.min, op1=ALU.max), reads=[GXB], writes=[GXB])
        k.op("act", lambda e: e.activation(out=GX[:, :, :], in_=GX[:, :, :], func=AF.Exp,
                                           bias=-math.log(16.0), scale=1.0), reads=[GXB], writes=[GXB])
        FL, FLB = k.sb(es, "FL" + tag, [P, 64, 4], F32)
        k.op("dve", lambda e: e.tensor_tensor(out=FL[:, :, :], in0=Bc[:, :, :], in1=MM[:, :, :], op=ALU.add),
             reads=[BcB, MMB], writes=[FLB])
        k.op("act", lambda e: e.activation(out=FL[:, :, :], in_=FL[:, :, :], func=AF.Exp, scale=-1.0),
             reads=[FLB], writes=[FLB])
        DEC, DECB = k.sb(es, "DEC" + tag, [P, 4], F32)
        k.op("dve", lambda e: e.tensor_copy(out=DEC[:, :], in_=U[:, 63, :]), reads=[UB], writes=[DECB])
        r0, r1 = row0, row0 + P
        k.dma("pool", T["cc_d"][r0:r1], Cc[:, :, :], CcB, reads=[CcB], writes=[T["B_cc"]])
        k.dma("pool", T["mm_d"][r0:r1], MM[:, :, :], MMB, reads=[MMB], writes=[T["B_mm"]])
        k.dma("pool", T["u_d"][r0:r1], U[:, :, :], UB, reads=[UB], writes=[T["B_u"]])
        k.dma("pool", T["gx_d"][r0:r1], GX[:, :, :], GXB, reads=[GXB], writes=[T["B_gx"]])
        k.dma("pool", T["fl_d"][r0:r1], FL[:, :, :], FLB, reads=[FLB], writes=[T["B_fl"]])
        k.dma("pool", T["dec_d"][r0:r1], DEC[:, :], DECB, reads=[DECB], writes=[T["B_dec"]])

    for nm in ("cc", "mm", "u", "gx", "fl", "dec"):
        T["B_" + nm] = Buf("B_" + nm)
    gate_prep(128, "p", (0, SEQ), (0, SEQ), None, None, 0)
    mps, mpsB = k.sb(es, "mps", [2, 4], F32)
    k.dma("sp", mps[:, :], T["state_m"][:, :], mpsB, writes=[mpsB])
    gate_prep(2, "s", (SEQ, NTOK), (SEQ, NTOK), mps, mpsB, 128)

    gxa, gxaB = k.sb(es, "gxa", [64, NCH, 4], F32)
    gxparts = []
    for c0 in range(0, NCH, 16):
        c1 = min(NCH, c0 + 16)
        pb_ = Buf("gxa%d" % c0)
        k.dma("sp", gxa[:, c0:c1, :], T["gx_d"][c0:c1].rearrange("c l h -> l c h"), pb_, reads=[T["B_gx"]],
              writes=[pb_], allow_slow_non_contiguous=True)
        gxparts.append(pb_)
    deca, decaB = k.sb(es, "deca", [128, NCH * 4], F32)
    k.dma("sp", deca[:, :], T["dec_d"].rearrange("c h -> (c h)").unsqueeze(0).to_broadcast([128, NCH * 4]),
          decaB, reads=[T["B_dec"]], writes=[decaB])

    CT, CTB = k.sb(es, "CT", [128, 8, 256], F32)
    CTh = [Buf("CTh%d" % i) for i in range(4)]
    nT, nTB = k.sb(es, "nT", [128, 8], F32)
    CTb, CTbB = k.sb(es, "CTb", [128, 8, 257], BF16)
    kTb = [k.sb(es, "kTb%d" % i, [128, 8, 512], BF16) for i in range(2)]
    vx = [k.sb(es, "vx%d" % i, [64, 8, 4, 257], BF16) for i in range(2)]
    for (t, b) in vx:
        k.op("pool", lambda e, t=t: e.memset(t[:, :, :, 256:257], 1.0), writes=[b])
    kM = [k.sb(es, "kM%d" % i, [64, D], BF16) for i in range(2)]
    gv = [k.sb(es, "gv%d" % i, [64, 4, 257], BF16) for i in range(2)]
    qTt = [k.sb(es, "qTt%d" % i, [128, 8, 128], BF16) for i in range(2)]
    sot = [k.sb(es, "sot%d" % i, [64, 2, D], BF16) for i in range(2)]
    cs2 = [k.sb(es, "cs2%d" % i, [64, 2, 4], F32) for i in range(2)]
    u2 = [k.sb(es, "u2%d" % i, [64, 2, 4], F32) for i in range(2)]
    fl2 = [k.sb(es, "fl2%d" % i, [64, 2, 4], F32) for i in range(2)]
    Mb = [k.sb(es, "Mb%d" % i, [64, 2, 64, 4], F32) for i in range(2)]
    EA, EAB = k.sb(es, "EA", [64, 4, 64], F32)
    PT, PTB = k.sb(es, "PT", [64, 4, 64], BF16)
    ta, taB = k.sb(es, "ta", [64, 2, 257], F32)
    hn, hnB = k.sb(es, "hn", [64, 4, 257], F32)
    dn, dnB = k.sb(es, "dn", [64, 8], F32)
    hs, hsB = k.sb(es, "hs", [64, D], F32)
    sqc, sqcB = k.sb(es, "sqc", [64, D], F32)
    stc, stcB = k.sb(es, "stc", [64, 16], F32)
    yab, yabB = k.sb(es, "yab", [64, 2, D], BF16)
    yaT, yaTB = k.sb(es, "yaT", [128, 8, 128], BF16)
    CO, COB = k.sb(es, "CO", [128, 8, 256], F32)

    def zero_state():
        k.op("pool", lambda e: e.memset(CT[:, :, :], 0.0), writes=CTh)
        k.op("pool", lambda e: e.memset(nT[:, :], 0.0), writes=[nTB])

    def store_state(cdst, ndst):
        for hb_ in range(8):
            h, eb = hb_ // 2, hb_ % 2
            for db in range(2):
                k.op("pe", lambda e, db=db: e.transpose(out=pX[0][:, db * 128:(db + 1) * 128],
                                                        in_=CT[:, 2 * h + db, eb * 128:(eb + 1) * 128],
                                                        identity=identf[:, :]),
                     reads=CTh + [identfB], writes=[pX[1]])
            k.op("act", lambda e: e.activation(out=CO[:, hb_, :], in_=pX[0][:, 0:256], func=AF.Copy),
                 reads=[pX[1]], writes=[COB])
        k.dma("pool", cdst.rearrange("h (eb e) d -> e (h eb) d", e=128), CO[:, :, :], COB, reads=[COB],
              is_output=True)
        k.dma("pool", ndst.rearrange("h (db p) -> p (h db)", p=128), nT[:, :], nTB, reads=[nTB],
              is_output=True, allow_slow_non_contiguous=True)

    def load_state(csrc, nsrc):
        k.dma("sp", CO[:, :, :], csrc.rearrange("h (eb e) d -> e (h eb) d", e=128), COB, writes=[COB])
        for hd_ in range(8):
            h, db = hd_ // 2, hd_ % 2
            for eb in range(2):
                k.op("pe", lambda e, eb=eb: e.transpose(out=pX[0][:, eb * 128:(eb + 1) * 128],
                                                        in_=CO[:, 2 * h + eb, db * 128:(db + 1) * 128],
                                                        identity=identf[:, :]),
                     reads=[COB, identfB], writes=[pX[1]])
            k.op("act", lambda e: e.activation(out=CT[:, hd_, :], in_=pX[0][:, 0:256], func=AF.Copy),
                 reads=[pX[1]], writes=CTh)
        k.dma("sp", nT[:, :], nsrc.rearrange("h (db p) -> p (h db)", p=128), nTB, writes=[nTB],
              allow_slow_non_contiguous=True)

    state = {"blk": -1}

    def load_block(blk, nchunks):
        kt, ktB = kTb[blk % 2]
        vt, vtB = vx[blk % 2]
        ntok = nchunks * 64
        k.dma("sp", kt[:, :, 0:ntok], T["kT_d"][:, :, blk * 512:blk * 512 + ntok].rearrange("g p t -> p g t"),
              ktB, writes=[ktB])
        for c in range(nchunks):
            r0 = blk * 512 + c * 64
            k.dma("sp", vt[:, c, :, 0:256], T["mv_d"][r0:r0 + 64, :].rearrange("l (h e) -> l h e", h=4), vtB,
                  writes=[vtB])
        return kt, ktB, vt, vtB

    def chunk_common(ch, kt, ktB, vt, vtB, cL):
        kMt, kMB = kM[ch % 2]
        for cg in range(8):
            k.op("pe", lambda e, cg=cg: e.transpose(out=pK[0][0:64, cg * 128:(cg + 1) * 128],
                                                    in_=kt[:, cg, cL * 64:(cL + 1) * 64], identity=ident[:, :]),
                 reads=[ktB, identB], writes=[pK[1]])
        k.op("act", lambda e: e.activation(out=kMt[:, :], in_=pK[0][0:64, :], func=AF.Copy),
             reads=[pK[1]], writes=[kMB])
        gvt, gvB = gv[ch % 2]
        k.op("dve", lambda e: e.tensor_tensor(out=gvt[:, :, :], in0=vt[:, cL, :, :],
                                              in1=gxa[:, ch, :].unsqueeze(2).to_broadcast([64, 4, 257]),
                                              op=ALU.mult), reads=[vtB] + gxparts, writes=[gvB])
        return kMt, kMB, gvt, gvB

    dnT, dnTB = k.sb(es, "dnT", [128, 8], F32)
    evc = [k.sb(es, "evc%d" % i, [128, 2, 256], F32) for i in range(2)]

    def state_update(ch, kMt, kMB, gvt, gvB):
        for h in range(4):
            for db in range(2):
                k.op("pe", lambda e, db=db: e.matmul(pN[0][:, 2 * h + db:2 * h + db + 1],
                                                     lhsT=kMt[:, h * 256 + db * 128:h * 256 + (db + 1) * 128],
                                                     rhs=gvt[:, h, 256:257], start=True, stop=True),
                     reads=[kMB, gvB], writes=[pN[1]])
        k.op("act", lambda e: e.activation(out=dnT[:, :], in_=pN[0][:, 0:8], func=AF.Copy),
             reads=[pN[1]], writes=[dnTB])
        for h in range(4):
            pc, pcB = pC[h % 2]
            for db in range(2):
                k.op("pe", lambda e, db=db: e.matmul(pc[:, db, :],
                                                     lhsT=kMt[:, h * 256 + db * 128:h * 256 + (db + 1) * 128],
                                                     rhs=gvt[:, h, 0:256], start=True, stop=True),
                     reads=[kMB, gvB], writes=[pcB])
            dsc = deca[:, ch * 4 + h:ch * 4 + h + 1]
            if h < 4:
                k.op("dve", lambda e: e.scalar_tensor_tensor(out=CT[:, 2 * h:2 * h + 2, :],
                                                             in0=CT[:, 2 * h:2 * h + 2, :], scalar=dsc,
                                                             in1=pc[:, :, :], op0=ALU.mult, op1=ALU.add),
                     reads=[CTh[h], decaB, pcB], writes=[CTh[h]])
            else:
                et, eB = evc[h % 2]
                k.op("act", lambda e: e.activation(out=et[:, :, :], in_=pc[:, :, :], func=AF.Copy),
                     reads=[pcB], writes=[eB])
                k.op("pool", lambda e: e.tensor_scalar(out=CT[:, 2 * h:2 * h + 2, :],
                                                       in0=CT[:, 2 * h:2 * h + 2, :], scalar1=dsc, scalar2=None,
                                                       op0=ALU.mult), reads=[CTh[h], decaB], writes=[CTh[h]])
                k.op("pool", lambda e: e.tensor_tensor(out=CT[:, 2 * h:2 * h + 2, :],
                                                       in0=CT[:, 2 * h:2 * h + 2, :], in1=et[:, :, :],
                                                       op=ALU.add), reads=[CTh[h], eB], writes=[CTh[h]])
            k.op("dve", lambda e: e.scalar_tensor_tensor(out=nT[:, 2 * h:2 * h + 2], in0=nT[:, 2 * h:2 * h + 2],
                                                         scalar=dsc, in1=dnT[:, 2 * h:2 * h + 2],
                                                         op0=ALU.mult, op1=ALU.add),
                 reads=[nTB, decaB, dnTB], writes=[nTB])

    def tile_loads(gi, ch0, slot):
        qt, qB = qTt[slot]
        k.dma("sp", qt[:, :, :], T["qT_d"][:, :, gi * 128:(gi + 1) * 128], qB, writes=[qB])
        st, sB = sot[slot]
        k.dma("sp", st[:, :, :], T["so_d"][gi * 128:(gi + 1) * 128, :].rearrange("(c l) f -> l c f", l=64), sB,
              writes=[sB])
        outs = []
        for (tl, dn_, bname) in ((cs2, "cc_d", "B_cc"), (u2, "u_d", "B_u"), (fl2, "fl_d", "B_fl")):
            t_, b_ = tl[slot]
            k.dma("sp", t_[:, :, :], T[dn_][ch0:ch0 + 2].rearrange("c l h -> l c h"), b_, reads=[T[bname]],
                  writes=[b_], allow_slow_non_contiguous=True)
            outs.append((t_, b_))
        mt, mB = Mb[slot]
        k.dma("sp", mt[:, :, :, :].rearrange("p a l h -> p (a l h)"),
              T["mm_d"][ch0:ch0 + 2].rearrange("c l h -> (c l h)").unsqueeze(0).to_broadcast([64, 512]),
              mB, reads=[T["B_mm"]], writes=[mB])
        return (qt, qB, st, sB) + tuple(outs) + ((mt, mB),)

    def chunk_output(cc, tl, kt, ktB, cL, vt, vtB):
        qt, qB, st, sB, (cst, csB), (ut, uB), (flt, flB), (mt, mB) = tl
        k.op("act", lambda e: e.activation(out=CTb[:, :, 0:256], in_=CT[:, :, :], func=AF.Copy),
             reads=CTh, writes=[CTbB])
        k.op("dve", lambda e: e.tensor_copy(out=CTb[:, :, 256], in_=nT[:, :]), reads=[nTB], writes=[CTbB])
        for h in range(4):
            for db in range(2):
                k.op("pe", lambda e, db=db: e.matmul(pS[0][:, h * 64:(h + 1) * 64],
                                                     lhsT=kt[:, 2 * h + db, cL * 64:(cL + 1) * 64],
                                                     rhs=qt[:, 2 * h + db, cc * 64:(cc + 1) * 64],
                                                     start=(db == 0), stop=(db == 1)),
                     reads=[ktB, qB], writes=[pS[1]])
            k.op("dve", lambda e: e.tensor_scalar(out=EA[:, h, :], in0=mt[:, cc, :, h],
                                                  scalar1=cst[:, cc, h:h + 1], scalar2=0.0, op0=ALU.subtract,
                                                  op1=ALU.max), reads=[mB, csB], writes=[EAB])
        k.op("act", lambda e: e.activation(out=EA[:, :, :], in_=EA[:, :, :], func=AF.Exp, scale=-1.0),
             reads=[EAB], writes=[EAB])
        k.op("dve", lambda e: e.tensor_tensor(out=EA[:, :, :], in0=EA[:, :, :],
                                              in1=tri[:, :].unsqueeze(1).to_broadcast([64, 4, 64]), op=ALU.mult),
             reads=[EAB, triB], writes=[EAB])
        k.op("dve", lambda e: e.tensor_tensor(out=PT[:, :, :].rearrange("p h l -> p (h l)"), in0=pS[0][:, :],
                                              in1=EA[:, :, :].rearrange("p h l -> p (h l)"), op=ALU.mult),
             reads=[pS[1], EAB], writes=[PTB])
        for hp in range(2):
            for hh in range(2):
                h = hp * 2 + hh
                for db in range(2):
                    k.op("pe", lambda e, db=db: e.matmul(pI[0][:, hh, 0:257],
                                                         lhsT=qt[:, 2 * h + db, cc * 64:(cc + 1) * 64],
                                                         rhs=CTb[:, 2 * h + db, :], start=(db == 0),
                                                         stop=(db == 1)),
                         reads=[qB, CTbB], writes=[pI[1]])
            for hh in range(2):
                h = hp * 2 + hh
                k.op("pe", lambda e: e.matmul(pA[0][:, 0:257], lhsT=PT[:, h, :], rhs=vt[:, cL, h, :],
                                              start=True, stop=True), reads=[PTB, vtB], writes=[pA[1]])
                k.op("act", lambda e: e.activation(out=ta[:, hh, :], in_=pA[0][:, 0:257], func=AF.Copy),
                     reads=[pA[1]], writes=[taB])
                k.op("dve", lambda e: e.scalar_tensor_tensor(out=hn[:, h, :], in0=pI[0][:, hh, 0:257],
                                                             scalar=ut[:, cc, h:h + 1], in1=ta[:, hh, :],
                                                             op0=ALU.mult, op1=ALU.add),
                     reads=[pI[1], uB, taB], writes=[hnB])
        k.op("dve", lambda e: e.tensor_scalar(out=dn[:, 0:4], in0=hn[:, :, 256], scalar1=-1.0, scalar2=None,
                                              op0=ALU.mult), reads=[hnB], writes=[dnB])
        k.op("dve", lambda e: e.tensor_tensor(out=dn[:, 0:4], in0=dn[:, 0:4], in1=hn[:, :, 256], op=ALU.max),
             reads=[hnB, dnB], writes=[dnB])
        k.op("dve", lambda e: e.tensor_tensor(out=dn[:, 0:4], in0=dn[:, 0:4], in1=flt[:, cc, :], op=ALU.max),
             reads=[dnB, flB], writes=[dnB])
        k.op("dve", lambda e: e.reciprocal(out=dn[:, 4:8], in_=dn[:, 0:4]), reads=[dnB], writes=[dnB])
        for h in range(4):
            k.op("dve", lambda e: e.tensor_scalar(out=hs[:, h * 256:(h + 1) * 256], in0=hn[:, h, 0:256],
                                                  scalar1=dn[:, 4 + h:5 + h], scalar2=None, op0=ALU.mult),
                 reads=[hnB, dnB], writes=[hsB])
        head_norm(k, hs, hsB, stc, stcB, sqc, sqcB, gmnb[:, :].rearrange("p (h d) -> p h d", h=4), gmnbB, neghalf, neghalfB, nh=4, P=64)
        k.op("dve", lambda e: e.tensor_tensor(out=yab[:, cc, :], in0=hs[:, :], in1=st[:, cc, :], op=ALU.mult),
             reads=[hsB, sB], writes=[yabB])

    def tile_finish(gi):
        for cg in range(8):
            for cc in range(2):
                k.op("pe", lambda e, cg=cg, cc=cc: e.transpose(
                    out=pK[0][:, cg * 128 + cc * 64:cg * 128 + (cc + 1) * 64],
                    in_=yab[:, cc, cg * 128:(cg + 1) * 128], identity=ident[0:64, 0:64]),
                    reads=[yabB, identB], writes=[pK[1]])
        k.op("act", lambda e: e.activation(out=yaT[:, :, :], in_=pK[0][:, :].rearrange("p (g t) -> p g t", g=8),
                                           func=AF.Copy), reads=[pK[1]], writes=[yaTB])
        k.dma("pool", T["yaT_d"][:, :, gi * 128:(gi + 1) * 128], yaT[:, :, :], yaTB, reads=[yaTB])

    zero_state()
    chunks = [(blk, cL) for blk in range(16) for cL in range(8)]
    blkres = {}

    def get_common(i):
        blk, cL = chunks[i]
        if cL == 0:
            blkres[blk] = (load_block(blk, 8), tile_loads(blk, blk * 8 + 6, blk % 2))
        (kt, ktB, vt, vtB), tl = blkres[blk]
        return chunk_common(blk * 8 + cL, kt, ktB, vt, vtB, cL)

    com = get_common(0)
    for i, (blk, cL) in enumerate(chunks):
        nxt = get_common(i + 1) if i + 1 < len(chunks) else None
        (kt, ktB, vt, vtB), tl = blkres[blk]
        ch = blk * 8 + cL
        if cL >= 6:
            chunk_output(cL - 6, tl, kt, ktB, cL, vt, vtB)
        state_update(ch, *com)
        if cL == 7:
            tile_finish(blk)
        com = nxt
    store_state(T["C_out"][0], T["n_out"][0])
    kt, ktB, vt, vtB = load_block(16, 2)
    tl = tile_loads(16, 128, 0)
    for sq_ in range(2):
        load_state(T["state_C"][sq_], T["state_n"][sq_])
        ch = 128 + sq_
        kMt, kMB, gvt, gvB = chunk_common(ch, kt, ktB, vt, vtB, sq_)
        chunk_output(sq_, tl, kt, ktB, sq_, vt, vtB)
        state_update(ch, kMt, kMB, gvt, gvB)
        store_state(T["C_out"][1 + sq_], T["n_out"][1 + sq_])
    tile_finish(16)
    k.barrier()
    es.close()


def phaseD1(k, T):
    es = ExitStack()
    ident, identB = mk_ident(k, es, "identD", BF16)
    identf, identfB = mk_ident(k, es, "identfD", F32)
    kiT, kiTB = k.sb(es, "kiT", [128, NTOK], BF16)
    k.dma("sp", kiT[:, :], T["kiT_d"][:, :], kiTB, writes=[kiTB])
    qiT, qiTB = k.sb(es, "qiT", [128, 4, NQ], BF16)
    k.dma("sp", qiT[:, :, :], T["qiT_d"][:, :, :], qiTB, writes=[qiTB])
    iwa, iwaB = k.sb(es, "iwaD", [128, NOWN * 8], F32)
    k.dma("sp", iwa[:, :], T["iw_d"][:, :], iwaB, writes=[iwaB])
    pk, pkB = k.sb(es, "pk", [128, 512], F32)
    k.dma("sp", pk[:, :], T["tokvalid"][0:512].unsqueeze(0).to_broadcast([128, 512]), pkB, writes=[pkB])
    k.op("dve", lambda e: e.tensor_scalar(out=pk[:, :], in0=pk[:, :], scalar1=-1.0, scalar2=1.0e30,
                                          op0=ALU.add, op1=ALU.mult), reads=[pkB], writes=[pkB])
    p2, p2B = k.sb(es, "p2", [128, 32], F32)
    for i in range(NBIS):
        k.op("pool", lambda e, i=i: e.memset(p2[:, i:i + 1], 2.0 ** (-i)), writes=[p2B])
    SCs = [k.sb(es, "SC%d" % i, [128, SEQ], F32) for i in range(2)]
    MKs = [k.sb(es, "MK%d" % i, [128, SEQ], BF16) for i in range(2)]
    mTs, mTsB = k.sb(es, "mTs", [128, 64, 128], BF16)
    R = [k.sb(es, "R%d" % i, [128, 512], BF16) for i in range(6)]
    dws = [k.sb(es, "dw%d" % i, [128, 8, 128], BF16) for i in range(2)]
    sv, svB = k.sb(es, "sv", [128, 8], F32)
    svm, svmB = k.sb(es, "svm", [128, 8], F32)
    svc, svcB = k.sb(es, "svc", [128, 8], F32)
    svs, svsB = k.sb(es, "svs", [128, 8], F32)
    Wt, WtB = k.sb(es, "Wt", [128, 32], F32)
    kis, kisB = k.sb(es, "kis", [128, NQ], BF16)
    ck, ckB = k.sb(es, "ck", [128, 8, 64], F32)
    kd, kdB = k.sb(es, "kd", [128, 8, 2, 64], BF16)
    pr = [k.ps(es, "prD%d" % i, [128, 512], F32) for i in range(4)]
    pacc = [k.ps(es, "paccD%d" % i, [128, 512], F32) for i in range(2)]
    pT = [k.ps(es, "pTD%d" % i, [128, 1024], BF16) for i in range(2)]
    cnts = {"r": 0, "pr": 0, "acc": 0, "pt": 0}

    for sq_ in range(2):
        k.dma("sp", ck[:, :, :], T["cache_kidx"][sq_].rearrange("(t p) d -> p t d", p=128), ckB, writes=[ckB])
        k.op("dve", lambda e: e.tensor_copy(out=kd[:, :, :, :],
                                            in_=ck[:, :, :].unsqueeze(2).to_broadcast([128, 8, 2, 64])),
             reads=[ckB], writes=[kdB])
        pt, ptB = pT[sq_]
        for st in range(8):
            k.op("pe", lambda e, st=st: e.transpose(out=pt[:, st * 128:(st + 1) * 128],
                                                    in_=kd[:, st, :, :].rearrange("p a d -> p (a d)"),
                                                    identity=ident[:, :]), reads=[kdB, identB], writes=[ptB])
        k.op("act", lambda e: e.activation(out=kis[:, sq_ * 1024:(sq_ + 1) * 1024], in_=pt[:, :], func=AF.Copy),
             reads=[ptB], writes=[kisB])
    k.op("dve", lambda e: e.tensor_copy(out=kis[:, 2048:NQ], in_=kiT[:, SEQ:NTOK]), reads=[kiTB], writes=[kisB])

    def scores(gi, keyT, keyB, S):
        SC, SCB = SCs[gi % 2]
        dw, dwB = dws[gi % 2]
        blocks = [(c0, min(512, S - c0)) for c0 in range(0, S, 512)]
        for h in range(8):
            k.op("pool", lambda e, h=h: e.tensor_scalar(out=dw[:, h, :], in0=identf[:, :],
                                                        scalar1=iwa[:, gi * 8 + h:gi * 8 + h + 1], scalar2=None,
                                                        op0=ALU.mult), reads=[identfB, iwaB], writes=[dwB])
        for (c0, n) in blocks:
            pa, paB = pacc[cnts["acc"] % 2]
            cnts["acc"] += 1
            pps = {}

            def sm(h):
                hp, base = h // 2, (h % 2) * 64
                pp, ppB = pr[cnts["pr"] % 4]
                cnts["pr"] += 1
                k.op("pe", lambda e: e.matmul(pp[:, 0:n], lhsT=qiT[base:base + 64, hp, gi * 128:(gi + 1) * 128],
                                              rhs=keyT[base:base + 64, c0:c0 + n], start=True, stop=True),
                     reads=[qiTB, keyB], writes=[ppB])
                pps[h] = (pp, ppB)
            sm(0)
            sm(1)
            sm(2)
            for h in range(8):
                pp, ppB = pps[h]
                rt, rB = R[cnts["r"] % 6]
                cnts["r"] += 1
                k.op("act", lambda e: e.activation(out=rt[:, 0:n], in_=pp[:, 0:n], func=AF.Relu),
                     reads=[ppB], writes=[rB])
                if h + 3 < 8:
                    sm(h + 3)
                k.op("pe", lambda e: e.matmul(pa[:, 0:n], lhsT=dw[:, h, :], rhs=rt[:, 0:n], start=(h == 0),
                                              stop=(h == 7)), reads=[dwB, rB], writes=[paB])
            k.op("act", lambda e: e.activation(out=SC[:, c0:c0 + n], in_=pa[:, 0:n], func=AF.Copy),
                 reads=[paB], writes=[SCB])
            yield

    def select(gi, S, sample):
        SC, SCB = SCs[gi % 2]
        MK, MKB = MKs[gi % 2]
        k.op("dve", lambda e: e.tensor_reduce(out=sv[:, 0:1], in_=SC[:, 0:S], axis=AX.X, op=ALU.max,
                                              apply_absolute_value=True), reads=[SCB], writes=[svB])
        if not sample:
            k.op("dve", lambda e: e.tensor_tensor(out=SC[:, 0:512], in0=SC[:, 0:512], in1=pk[:, :], op=ALU.add),
                 reads=[SCB, pkB], writes=[SCB])
            k.op("pool", lambda e: e.memset(SC[0:64, S - 64:S], NEG), reads=[SCB], writes=[SCB])
        else:
            k.op("pool", lambda e: e.memset(SC[0:64, 1024:2048], NEG), reads=[SCB], writes=[SCB])
            k.op("pool", lambda e: e.memset(SC[0:64, 2112:2176], NEG), reads=[SCB], writes=[SCB])
            k.op("pool", lambda e: e.memset(SC[64:128, 0:1024], NEG), reads=[SCB], writes=[SCB])
            k.op("pool", lambda e: e.memset(SC[64:128, 2048:2112], NEG), reads=[SCB], writes=[SCB])
        k.op("dve", lambda e: e.tensor_tensor(out=Wt[:, 0:NBIS], in0=p2[:, 0:NBIS],
                                              in1=sv[:, 0:1].to_broadcast([128, NBIS]), op=ALU.mult),
             reads=[p2B, svB], writes=[WtB])
        k.op("dve", lambda e: e.tensor_scalar(out=sv[:, 1:2], in0=sv[:, 0:1], scalar1=-1.0, scalar2=None,
                                              op0=ALU.mult), reads=[svB], writes=[svB])
        Sd = max(128, int(round(S * 0.78 / 128.0)) * 128)
        if Sd >= S:
            Sd = S
        nA = S - Sd
        mkaB = Buf("mka")
        for i in range(NBIS):
            k.op("dve", lambda e, i=i: e.tensor_tensor(out=svm[:, 2:3], in0=sv[:, 1:2], in1=Wt[:, i:i + 1],
                                                       op=ALU.add), reads=[svB, WtB], writes=[svmB])
            if nA > 0:
                k.op("act", lambda e: e.activation(out=MK[:, Sd:S], in_=SC[:, Sd:S], func=AF.Sign,
                                                   bias=svm[:, 2:3], scale=-1.0, accum_out=svs[:, 5:6]),
                     reads=[SCB, svmB], writes=[mkaB, svsB])
            k.op("dve", lambda e: e.tensor_scalar(out=MK[:, 0:Sd], in0=SC[:, 0:Sd], scalar1=svm[:, 2:3],
                                                  scalar2=None, op0=ALU.is_ge, op1=ALU.add,
                                                  accum_out=svc[:, 3:4]), reads=[SCB, svmB], writes=[MKB, svcB])
            if nA > 0:
                k.op("dve", lambda e: e.scalar_tensor_tensor(out=sv[:, 6:7], in0=svc[:, 3:4], scalar=2.0,
                                                             in1=svs[:, 5:6], op0=ALU.mult, op1=ALU.subtract),
                     reads=[svcB, svsB], writes=[svB])
                k.op("dve", lambda e: e.tensor_scalar(out=sv[:, 4:5], in0=sv[:, 6:7], scalar1=511.0 - nA,
                                                      scalar2=None, op0=ALU.is_ge), reads=[svB], writes=[svB])
            else:
                k.op("dve", lambda e: e.tensor_scalar(out=sv[:, 4:5], in0=svc[:, 3:4], scalar1=255.5,
                                                      scalar2=None, op0=ALU.is_ge), reads=[svcB], writes=[svB])
            k.op("dve", lambda e, i=i: e.scalar_tensor_tensor(out=sv[:, 1:2], in0=sv[:, 4:5],
                                                              scalar=Wt[:, i:i + 1], in1=sv[:, 1:2],
                                                              op0=ALU.mult, op1=ALU.add),
                 reads=[svB, WtB], writes=[svB])
            yield
        k.op("dve", lambda e: e.tensor_scalar(out=MK[:, 0:S], in0=SC[:, 0:S], scalar1=sv[:, 1:2], scalar2=None,
                                              op0=ALU.is_ge), reads=[SCB, svB, mkaB], writes=[MKB])
        nsb = S // 128
        for s0 in range(0, nsb, 8):
            ns = min(8, nsb - s0)
            pt, ptB = pT[cnts["pt"] % 2]
            cnts["pt"] += 1
            for sb in range(s0, s0 + ns):
                k.op("pe", lambda e, sb=sb: e.transpose(out=pt[:, (sb - s0) * 128:(sb - s0 + 1) * 128],
                                                        in_=MK[:, sb * 128:(sb + 1) * 128], identity=ident[:, :]),
                     reads=[MKB, identB], writes=[ptB])
            k.op("act", lambda e: e.activation(out=mTs[:, s0:s0 + ns, :],
                                               in_=pt[:, 0:ns * 128].rearrange("p (a q) -> p a q", q=128),
                                               func=AF.Copy), reads=[ptB], writes=[mTsB])
        k.dma("pool", T["mk_d"][gi, :, 0:nsb, :], mTs[:, 0:nsb, :], mTsB, reads=[mTsB])

    tiles = [(gi, kiT, kiTB, 512 * (gi + 1), False) for gi in range(16)] + [(16, kis, kisB, NQ, True)]

    def drain(g):
        for _ in g:
            pass

    def interleave(ga, na, gb, nb):
        ia = ib = 0
        da = db = False
        while not (da and db):
            if not da and (db or ia * nb <= ib * na):
                try:
                    next(ga)
                    ia += 1
                except StopIteration:
                    da = True
            else:
                try:
                    next(gb)
                    ib += 1
                except StopIteration:
                    db = True

    drain(scores(*tiles[0][0:4]))
    for i, (gi, kt_, ktB_, S, smp) in enumerate(tiles):
        sel = select(gi, S, smp)
        if i + 1 < len(tiles):
            nxt = tiles[i + 1]
            interleave(scores(*nxt[0:4]), (nxt[3] + 511) // 512, sel, NBIS)
        else:
            drain(sel)
    k.barrier()
    es.close()


def phaseD2(k, T):
    es = ExitStack()
    ones, onesB = k.sb(es, "onesD", [128, 128], BF16)
    k.op("pool", lambda e: e.memset(ones[:, :], 1.0), writes=[onesB])
    pS = [k.ps(es, "pSD%d" % i, [128, 512], F32) for i in range(4)]
    pO = [k.ps(es, "pOD%d" % i, [128, 512], F32) for i in range(2)]
    pD = [k.ps(es, "pDD%d" % i, [128, 512], F32) for i in range(2)]
    PTt = [k.sb(es, "PTt%d" % i, [128, 512], BF16) for i in range(6)]
    oacc, oaccB = k.sb(es, "oacc", [128, 8, 512], F32)
    dacc, daccB = k.sb(es, "dacc", [128, 8, 512], F32)
    ybs, ybsB = k.sb(es, "ybs", [128, 8, 512], BF16)
    cn = {"s": 0, "p": 0, "o": 0, "k": 0, "v": 0}

    def stage1(st):
        for f in st.get("pre", ()):
            f()
        Q_ap = st["Q"]
        nq = Q_ap.shape[-1]
        ps_, psB = pS[cn["s"] % 4]
        cn["s"] += 1
        k.op("pe", lambda e: e.matmul(ps_[:, 0:nq], lhsT=st["KT"], rhs=Q_ap, start=True, stop=True),
             reads=[st["KTB"], st["QB"]], writes=[psB])
        pt, ptB = PTt[cn["p"] % 6]
        cn["p"] += 1
        k.op("act", lambda e: e.activation(out=pt[:, 0:nq], in_=ps_[:, 0:nq], func=AF.Exp),
             reads=[psB], writes=[ptB])
        eng = "dve" if cn["p"] % 5 != 0 else "pool"
        k.op(eng, lambda e: e.tensor_tensor(out=pt[:, 0:nq].rearrange("p (a q) -> p a q", q=128),
                                            in0=pt[:, 0:nq].rearrange("p (a q) -> p a q", q=128),
                                            in1=st["m"], op=ALU.mult), reads=[ptB, st["mB"]], writes=[ptB])
        st["pt"], st["ptB"], st["nq"] = pt, ptB, nq

    def stage2(st):
        pt, ptB, nq, q0 = st["pt"], st["ptB"], st["nq"], st["q0"]
        po, poB, pd, pdB = st["po"]
        k.op("pe", lambda e: e.matmul(po[:, q0:q0 + nq], lhsT=st["V"], rhs=pt[:, 0:nq], start=st["first"],
                                      stop=st["last"]), reads=[st["VB"], ptB], writes=[poB])
        k.op("pe", lambda e: e.matmul(pd[:, q0:q0 + nq], lhsT=ones[:, :], rhs=pt[:, 0:nq], start=st["first"],
                                      stop=st["last"]), reads=[onesB, ptB], writes=[pdB])
        for f in st.get("post", ()):
            f()

    def run_steps(steps, look=4):
        n = len(steps)
        for i in range(min(look, n)):
            stage1(steps[i])
        for i in range(n):
            stage2(steps[i])
            if i + look < n:
                stage1(steps[i + look])

    es2 = ExitStack()
    QTg = [k.sb(es2, "QTg%d" % i, [128, 8, 512], BF16) for i in range(2)]
    mT = [k.sb(es2, "mT%d" % i, [128, 4, 16, 128], BF16) for i in range(2)]
    Vc = [k.sb(es2, "Vc%d" % i, [128, 16, 256], BF16) for i in range(2)]
    KTc = [k.sb(es2, "KTc%d" % i, [128, 2048], BF16) for i in range(3)]
    mi = 0
    for G in range(4):
        qg, qgB = QTg[G % 2]
        k.dma("sp", qg[:, :, :], T["QT_d"][:, :, 512 * G:512 * (G + 1)], qgB, writes=[qgB])
        steps = []
        for kc in range(G + 1):
            mt, mtB = mT[mi % 2]
            mi += 1

            def ld_mask(mt=mt, mtB=mtB, kc=kc, G=G):
                for ti in range(4):
                    nv = 16 if kc < G else 4 * (ti + 1)
                    k.dma("sp", mt[:, ti, 0:nv, :], T["mk_d"][4 * G + ti, :, kc * 16:kc * 16 + nv, :], mtB,
                          writes=[mtB])
            for hp in range(4):
                vc, vcB = Vc[cn["v"] % 2]
                cn["v"] += 1

                def ld_v(vc=vc, vcB=vcB, kc=kc, hp=hp):
                    k.dma("sp", vc[:, :, :],
                          T["V_d"][kc * 2048:(kc + 1) * 2048, hp * 256:(hp + 1) * 256].rearrange(
                              "(sb s) d -> s sb d", s=128), vcB, writes=[vcB])
                for hh in range(2):
                    h = 2 * hp + hh
                    kt, ktB = KTc[cn["k"] % 3]
                    cn["k"] += 1

                    def ld_k(kt=kt, ktB=ktB, kc=kc, h=h):
                        k.dma("sp", kt[:, :], T["KT_d"][h, :, kc * 2048:(kc + 1) * 2048], ktB, writes=[ktB])
                    pos = pO[cn["o"] % 2] + pD[cn["o"] % 2]
                    cn["o"] += 1

                    def post(kc=kc, h=h, pos=pos):
                        po, poB, pd, pdB = pos
                        if kc == 0:
                            k.op("act", lambda e: e.activation(out=oacc[:, h, :], in_=po[:, :], func=AF.Copy),
                                 reads=[poB], writes=[oaccB])
                            k.op("act", lambda e: e.activation(out=dacc[:, h, :], in_=pd[:, :], func=AF.Copy),
                                 reads=[pdB], writes=[daccB])
                        else:
                            k.op("dve", lambda e: e.tensor_tensor(out=oacc[:, h, :], in0=po[:, :],
                                                                  in1=oacc[:, h, :], op=ALU.add),
                                 reads=[poB, oaccB], writes=[oaccB])
                            k.op("dve", lambda e: e.tensor_tensor(out=dacc[:, h, :], in0=pd[:, :],
                                                                  in1=dacc[:, h, :], op=ALU.add),
                                 reads=[pdB, daccB], writes=[daccB])
                    for sb in range(16):
                        q0 = 0 if kc < G else 128 * (sb // 4)
                        st = {"KT": kt[:, sb * 128:(sb + 1) * 128], "KTB": ktB,
                              "V": vc[:, sb, hh * 128:(hh + 1) * 128], "VB": vcB,
                              "Q": qg[:, h, q0:512], "QB": qgB, "m": mt[:, q0 // 128:4, sb, :], "mB": mtB,
                              "q0": q0, "first": sb == 0, "last": sb == 15, "po": pos}
                        pre = []
                        if sb == 0:
                            if hp == 0 and hh == 0:
                                pre.append(ld_mask)
                            if hh == 0:
                                pre.append(ld_v)
                            pre.append(ld_k)
                        st["pre"] = pre
                        if sb == 15:
                            st["post"] = [post]
                        steps.append(st)
        run_steps(steps)
        k.op("dve", lambda e: e.reciprocal(out=dacc[:, :, :], in_=dacc[:, :, :]), reads=[daccB], writes=[daccB])
        k.op("dve", lambda e: e.tensor_tensor(out=ybs[:, :, :], in0=oacc[:, :, :], in1=dacc[:, :, :],
                                              op=ALU.mult), reads=[oaccB, daccB], writes=[ybsB])
        k.dma("pool", T["ybT_d"][:, :, 512 * G:512 * (G + 1)], ybs[:, :, :], ybsB, reads=[ybsB])
    k.barrier()
    es2.close()

    es3 = ExitStack()
    ident, identB = mk_ident(k, es3, "identD2", BF16)
    KTs, KTsB = k.sb(es3, "KTsS", [128, 8, NQ], BF16)
    Vs, VsB = k.sb(es3, "VsS", [128, 17, D], BF16)
    mS, mSB = k.sb(es3, "mS", [128, 17, 128], BF16)
    qs, qsB = k.sb(es3, "qsS", [128, 8, 128], BF16)
    stf = [k.sb(es3, "stf%d" % i, [128, D], F32) for i in range(2)]
    stb = [k.sb(es3, "stb%d" % i, [128, D], BF16) for i in range(2)]
    pT = pS[0]
    pTb = pS[3][0][:, :].bitcast(BF16)
    pTbB = pS[3][1]
    k.dma("sp", mS[:, :, :], T["mk_d"][16, :, 0:17, :], mSB, writes=[mSB])
    k.dma("sp", qs[:, :, :], T["QT_d"][:, :, 2048:NQ], qsB, writes=[qsB])
    vparts = []
    i = 0
    for sq_ in range(2):
        for st in range(8):
            ft, fB = stf[i % 2]
            bt, bB = stb[i % 2]
            k.dma("sp", ft[:, :], T["cache_k"][sq_, st * 128:(st + 1) * 128, :], fB, writes=[fB])
            k.op("pool", lambda e: e.tensor_copy(out=bt[:, :], in_=ft[:, :]), reads=[fB], writes=[bB])
            for h in range(8):
                k.op("pe", lambda e, h=h: e.transpose(out=pTb[:, h * 128:(h + 1) * 128],
                                                      in_=bt[:, h * 128:(h + 1) * 128], identity=ident[:, :]),
                     reads=[bB, identB], writes=[pTbB])
            c0 = sq_ * 1024 + st * 128
            k.op("act", lambda e: e.activation(out=KTs[:, :, c0:c0 + 128],
                                               in_=pTb[:, :].rearrange("p (h t) -> p h t", h=8), func=AF.Copy),
                 reads=[pTbB], writes=[KTsB])
            i += 1
            ft, fB = stf[i % 2]
            k.dma("sp", ft[:, :], T["cache_v"][sq_, st * 128:(st + 1) * 128, :], fB, writes=[fB])
            vb_ = Buf("vsp%d" % i)
            k.op("dve", lambda e: e.tensor_copy(out=Vs[:, sq_ * 8 + st, :], in_=ft[:, :]), reads=[fB],
                 writes=[vb_])
            vparts.append(vb_)
            i += 1
    nb = Buf("ktnew")
    k.dma("sp", KTs[:, :, 2048:NQ], T["KT_d"][:, :, SEQ:NTOK].rearrange("h d t -> d h t"), nb, reads=[KTsB],
          writes=[nb])
    vn = Buf("vnew")
    k.dma("sp", Vs[:, 16, :], T["V_d"][SEQ:NTOK, :], vn, writes=[vn])
    steps = []
    for h in range(8):
        pos = pO[h % 2] + pD[h % 2]

        def post(h=h, pos=pos):
            po, poB, pd, pdB = pos
            k.op("act", lambda e: e.activation(out=dacc[:, h, 0:128], in_=pd[:, 0:128], func=AF.Copy),
                 reads=[pdB], writes=[daccB])
            k.op("dve", lambda e: e.reciprocal(out=dacc[:, h, 0:128], in_=dacc[:, h, 0:128]), reads=[daccB],
                 writes=[daccB])
            k.op("dve", lambda e: e.tensor_tensor(out=ybs[:, h, 0:128], in0=po[:, 0:128], in1=dacc[:, h, 0:128],
                                                  op=ALU.mult), reads=[poB, daccB], writes=[ybsB])
        for sb in range(17):
            st = {"KT": KTs[:, h, sb * 128:(sb + 1) * 128], "KTB": KTsB if sb < 16 else nb,
                  "V": Vs[:, sb, h * 128:(h + 1) * 128], "VB": vparts[sb] if sb < 16 else vn,
                  "Q": qs[:, h, :], "QB": qsB, "m": mS[:, sb:sb + 1, :], "mB": mSB, "q0": 0,
                  "first": sb == 0, "last": sb == 16, "po": pos}
            if sb == 16:
                st["post"] = [post]
            steps.append(st)
    run_steps(steps)
    k.dma("pool", T["ybT_d"][:, :, 2048:NQ], ybs[:, :, 0:128], ybsB, reads=[ybsB])
    k.barrier()
    es3.close()
    es.close()


def phaseE(k, T):
    es = ExitStack()
    ident, identB = mk_ident(k, es, "identE", BF16)
    neghalf, neghalfB = k.sb(es, "neghalfE", [128, 8], F32)
    k.op("pool", lambda e: e.memset(neghalf[:, :], -0.5), writes=[neghalfB])
    g2b, g2bB = k.sb(es, "g2b", [128, D], F32)
    k.dma("sp", g2b[:, :], T["g_norm2"][0:1, :].to_broadcast([128, D]), g2bB, writes=[g2bB])
    pM = [k.ps(es, "pME%d" % i, [128, 512], F32) for i in range(6)]
    pTe = [k.ps(es, "pTE%d" % i, [128, 1024], BF16) for i in range(2)]
    pc = [0]

    def nps():
        p = pM[pc[0] % 6]
        pc[0] += 1
        return p

    def own_rows(gi):
        return (4 * gi + 3) * 128 if gi < 16 else SEQ

    es1 = ExitStack()
    ws = WStream(k, es1, "wsE", 3, 1024)
    Wa, WaB = ws.load(T["w_a_out"], 0, 1024)
    Wb, WbB = ws.load(T["w_b_out"], 0, 1024)
    Wo, WoB = ws.load(T["w_o"], 0, 1024)
    ins4 = [k.sb(es1, "in4_%d" % i, [128, 8, 512], BF16) for i in range(4)]
    mixT, mixTB = k.sb(es1, "mixT", [128, 8, 512], BF16)
    t1 = [k.sb(es1, "t1_%d" % i, [128, 512], F32) for i in range(2)]
    t2 = [k.sb(es1, "t2_%d" % i, [128, 512], F32) for i in range(2)]
    xs = [k.sb(es1, "xsE%d" % i, [128, D], F32) for i in range(2)]
    x1 = [k.sb(es1, "x1E%d" % i, [128, D], F32) for i in range(2)]
    h2b = [k.sb(es1, "h2b%d" % i, [128, D], BF16) for i in range(2)]
    ssE = [k.sb(es1, "ssE%d" % i, [128, 4], F32) for i in range(2)]
    h2Tt, h2TtB = k.sb(es1, "h2Tt", [128, 8, 512], BF16)
    ti_glob = 0
    for tg in range(5):
        n = 512 if tg < 4 else 128
        c0 = tg * 512
        srcs = []
        for i, nm in enumerate(("yaT_d", "ybT_d", "gaT_d", "gbT_d")):
            t_, b_ = ins4[i]
            k.dma("sp", t_[:, :, 0:n], T[nm][:, :, c0:c0 + n], b_, writes=[b_])
            srcs.append((t_, b_))
        (ya, yaB), (yb, ybB), (ga, gaB), (gb, gbB) = srcs
        for cg in range(8):
            pa, paB = nps()
            for kc in range(8):
                k.op("pe", lambda e, kc=kc: e.matmul(pa[:, 0:n], lhsT=Wa[:, kc, cg * 128:(cg + 1) * 128],
                                                     rhs=ya[:, kc, 0:n], start=(kc == 0), stop=(kc == 7)),
                     reads=[WaB, yaB], writes=[paB])
            pb, pbB = nps()
            for kc in range(8):
                k.op("pe", lambda e, kc=kc: e.matmul(pb[:, 0:n], lhsT=Wb[:, kc, cg * 128:(cg + 1) * 128],
                                                     rhs=yb[:, kc, 0:n], start=(kc == 0), stop=(kc == 7)),
                     reads=[WbB, ybB], writes=[pbB])
            a1, a1B = t1[cg % 2]
            a2, a2B = t2[cg % 2]
            k.op("dve", lambda e: e.tensor_tensor(out=a1[:, 0:n], in0=pa[:, 0:n], in1=ga[:, cg, 0:n], op=ALU.mult),
                 reads=[paB, gaB], writes=[a1B])
            k.op("dve", lambda e: e.tensor_tensor(out=a2[:, 0:n], in0=pb[:, 0:n], in1=gb[:, cg, 0:n], op=ALU.mult),
                 reads=[pbB, gbB], writes=[a2B])
            k.op("pool", lambda e: e.tensor_tensor(out=mixT[:, cg, 0:n], in0=a1[:, 0:n], in1=a2[:, 0:n],
                                                   op=ALU.add), reads=[a1B, a2B], writes=[mixTB])
        for tl in range(n // 128):
            gi = ti_glob
            ti_glob += 1
            s = gi % 2
            xt, xB = xs[s]
            x1t, x1B = x1[s]
            r0 = own_rows(gi)
            k.dma("sp", xt[:, :], T["xb"][r0:r0 + 128, :], xB, writes=[xB])
            for hh in range(2):
                po, poB = nps()
                for kc in range(8):
                    k.op("pe", lambda e, kc=kc: e.matmul(po[:, :], lhsT=mixT[:, kc, tl * 128:(tl + 1) * 128],
                                                         rhs=Wo[:, kc, hh * 512:(hh + 1) * 512],
                                                         start=(kc == 0), stop=(kc == 7)),
                         reads=[mixTB, WoB], writes=[poB])
                k.op("dve", lambda e: e.tensor_tensor(out=x1t[:, hh * 512:(hh + 1) * 512], in0=po[:, :],
                                                      in1=xt[:, hh * 512:(hh + 1) * 512], op=ALU.add),
                     reads=[poB, xB], writes=[x1B])
            k.dma("pool", T["x1_d"][gi * 128:(gi + 1) * 128, :], x1t[:, :], x1B, reads=[x1B])
            hbt, hbB = h2b[s]
            sst, ssB = ssE[s]
            k.op("act", lambda e: e.activation(out=hbt[:, :], in_=x1t[:, :], func=AF.Square,
                                               accum_out=sst[:, 0:1]), reads=[x1B], writes=[hbB, ssB])
            k.op("dve", lambda e: e.tensor_scalar(out=sst[:, 1:2], in0=sst[:, 0:1], scalar1=1.0 / D, scalar2=EPS,
                                                  op0=ALU.mult, op1=ALU.add), reads=[ssB], writes=[ssB])
            k.op("pool", lambda e: e.tensor_tensor(out=sst[:, 2:3], in0=sst[:, 1:2], in1=neghalf[:, 0:1],
                                                   op=ALU.pow), reads=[ssB, neghalfB], writes=[ssB])
            k.op("dve", lambda e: e.scalar_tensor_tensor(out=hbt[:, :], in0=x1t[:, :], scalar=sst[:, 2:3],
                                                         in1=g2b[:, :], op0=ALU.mult, op1=ALU.mult),
                 reads=[x1B, ssB, g2bB], writes=[hbB])
            pt, ptB = pTe[gi % 2]
            for kc in range(8):
                k.op("pe", lambda e, kc=kc: e.transpose(out=pt[:, kc * 128:(kc + 1) * 128],
                                                        in_=hbt[:, kc * 128:(kc + 1) * 128],
                                                        identity=ident[:, :]), reads=[hbB, identB], writes=[ptB])
            k.op("act", lambda e: e.activation(out=h2Tt[:, :, tl * 128:(tl + 1) * 128],
                                               in_=pt[:, :].rearrange("p (k n) -> p k n", k=8), func=AF.Copy),
                 reads=[ptB], writes=[h2TtB])
        k.dma("pool", T["h2T_d"][:, :, c0:c0 + n], h2Tt[:, :, 0:n], h2TtB, reads=[h2TtB])
    k.barrier()
    es1.close()

    es2 = ExitStack()
    Wout, WoutB = k.sb(es2, "Wout", [128, 22, D], BF16)
    wstg = [k.sb(es2, "wstg%d" % i, [128, D], F32) for i in range(2)]
    woparts = []
    for fb in range(22):
        st, sB = wstg[fb % 2]
        k.dma("sp", st[:, :], T["w_ffn_out"][fb * 128:(fb + 1) * 128, :], sB, writes=[sB])
        pb_ = Buf("wo%d" % fb)
        k.op("dve" if fb % 2 == 0 else "pool", lambda e, st=st: e.tensor_copy(out=Wout[:, fb, :], in_=st[:, :]),
             reads=[sB], writes=[pb_])
        woparts.append(pb_)
    wsf = WStream(k, es2, "wsF", 4, 128)
    h2h, h2hB = k.sb(es2, "h2h", [128, 8, 1152], BF16)
    actT, actTB = k.sb(es2, "actT", [128, 22, 1152], BF16)
    sg = [k.sb(es2, "sg%d" % i, [128, 512], F32) for i in range(2)]
    x1f = [k.sb(es2, "x1f%d" % i, [128, D], F32) for i in range(2)]
    yo = [k.sb(es2, "yo%d" % i, [128, D], F32) for i in range(2)]
    sgi = 0
    for half in range(2):
        t0 = 0 if half == 0 else 1152
        nt = 1152 if half == 0 else 1024
        k.dma("sp", h2h[:, :, 0:nt], T["h2T_d"][:, :, t0:t0 + nt], h2hB, writes=[h2hB])
        groups = [(c, min(512, nt - c)) for c in range(0, nt, 512)]
        for fb in range(22):
            Wg, WgB = wsf.load(T["w_ffn_in"], fb * 128, 128)
            Wu, WuB = wsf.load(T["w_ffn_in"], DFF + fb * 128, 128)
            for (c, n) in groups:
                pg, pgB = nps()
                for kc in range(8):
                    k.op("pe", lambda e, kc=kc: e.matmul(pg[:, 0:n], lhsT=Wg[:, kc, :], rhs=h2h[:, kc, c:c + n],
                                                         start=(kc == 0), stop=(kc == 7)),
                         reads=[WgB, h2hB], writes=[pgB])
                pu, puB = nps()
                for kc in range(8):
                    k.op("pe", lambda e, kc=kc: e.matmul(pu[:, 0:n], lhsT=Wu[:, kc, :], rhs=h2h[:, kc, c:c + n],
                                                         start=(kc == 0), stop=(kc == 7)),
                         reads=[WuB, h2hB], writes=[puB])
                st, sB = sg[sgi % 2]
                sgi += 1
                k.op("act", lambda e: e.activation(out=st[:, 0:n], in_=pg[:, 0:n], func=AF.Silu),
                     reads=[pgB], writes=[sB])
                k.op("dve", lambda e: e.tensor_tensor(out=actT[:, fb, c:c + n], in0=pu[:, 0:n], in1=st[:, 0:n],
                                                      op=ALU.mult), reads=[puB, sB], writes=[actTB])
        for tl in range(nt // 128):
            gi = (0 if half == 0 else 9) + tl
            xt, xB = x1f[gi % 2]
            yt, yB = yo[gi % 2]
            k.dma("sp", xt[:, :], T["x1_d"][gi * 128:(gi + 1) * 128, :], xB, writes=[xB])
            for hh in range(2):
                py, pyB = nps()
                for fb in range(22):
                    k.op("pe", lambda e, fb=fb: e.matmul(py[:, :], lhsT=actT[:, fb, tl * 128:(tl + 1) * 128],
                                                         rhs=Wout[:, fb, hh * 512:(hh + 1) * 512],
                                                         start=(fb == 0), stop=(fb == 21)),
                         reads=[actTB] + (woparts if fb == 0 else []), writes=[pyB])
                k.op("dve", lambda e: e.tensor_tensor(out=yt[:, hh * 512:(hh + 1) * 512], in0=py[:, :],
                                                      in1=xt[:, hh * 512:(hh + 1) * 512], op=ALU.add),
                     reads=[pyB, xB], writes=[yB])
            k.dma("pool", T["y_out"][gi * 128:(gi + 1) * 128, :], yt[:, :], yB, reads=[yB], is_output=True)
    k.barrier()
    es2.close()
    es.close()


def build(stage=99):
    k = KB()
    T = {}
    for nm, shp in (("xb", [NTOK, D]), ("w_in", [D, DIN]), ("g_norm1", [1, D]), ("g_k", [1, 128]),
                    ("g_q", [1, 128]), ("w_conv", [4, 2048]), ("b_conv", [1, 2048]),
                    ("state_conv", [2, 3, 2048]), ("tokvalid", [NTOK]), ("b_if", [1, 8]),
                    ("g_mnorm", [1, D]), ("state_m", [2, 4]), ("state_C", [2, 4, 256, 256]),
                    ("state_n", [2, 4, 256]), ("cache_kidx", [2, 1024, 64]), ("cache_k", [2, 1024, D]),
                    ("cache_v", [2, 1024, D]), ("w_a_out", [D, D]), ("w_b_out", [D, D]), ("w_o", [D, D]),
                    ("g_norm2", [1, D]), ("w_ffn_in", [D, 2 * DFF]), ("w_ffn_out", [DFF, D])):
        T[nm] = k.dram(nm, shp, F32, "ExternalInput")
    for nm, shp in (("k_out", [NTOK, D]), ("v_out", [NTOK, D]), ("ki_out", [NTOK, 64]),
                    ("cvk_out", [3, 1024, 3]), ("cvq_out", [3, 1024, 3]), ("m_out", [3, 4]),
                    ("C_out", [3, 4, 256, 256]), ("n_out", [3, 4, 256]), ("y_out", [NQ, D])):
        T[nm] = k.dram(nm, shp, F32, "ExternalOutput")
    for nm, shp, dt in (("KT_d", [8, 128, NTOK], BF16), ("V_d", [NTOK, D], BF16), ("mv_d", [NTOK, D], BF16),
                        ("kT_d", [8, 128, NTOK], BF16), ("g_d", [NTOK, 8], F32), ("kiT_d", [128, NTOK], BF16),
                        ("hTo_d", [128, 8, NH], BF16), ("qT_d", [128, 8, NQ], BF16), ("so_d", [NQ, D], BF16),
                        ("QT_d", [128, 8, NQ], BF16), ("qiT_d", [128, 4, NQ], BF16), ("iw_d", [128, NOWN * 8], F32),
                        ("gaT_d", [128, 8, NQ], BF16), ("gbT_d", [128, 8, NQ], BF16),
                        ("cc_d", [NCH, 64, 4], F32), ("mm_d", [NCH, 64, 4], F32), ("u_d", [NCH, 64, 4], F32),
                        ("gx_d", [NCH, 64, 4], F32), ("fl_d", [NCH, 64, 4], F32), ("dec_d", [NCH, 4], F32),
                        ("yaT_d", [128, 8, NQ], BF16), ("ybT_d", [128, 8, NQ], BF16),
                        ("bx_d", [128, 8], F32), ("mp_d", [4, 128], F32),
                        ("mk_d", [NOWN, 128, 64, 128], BF16), ("x1_d", [NQ, D], F32),
                        ("h2T_d", [128, 8, NQ], BF16)):
        T[nm] = k.dram(nm, shp, dt, "Internal")
    phaseA(k, T)
    if stage >= 2:
        phaseB(k, T)
    if stage >= 3:
        phaseC(k, T)
    if stage >= 4:
        phaseD1(k, T)
        phaseD2(k, T)
    if stage >= 5:
        phaseE(k, T)
    k.finish()
    return k.nc


_NC_CACHE = {}
import os
STAGE = int(os.environ.get('KSTAGE', '5'))
SUB = int(os.environ.get('KSUB', '99'))


def kernel(**inp):
    f32 = np.float32
    x_prompt = np.asarray(inp["x_prompt"], f32)
    x_sample = np.asarray(inp["x_sample"], f32)
    if "nc" not in _NC_CACHE:
        _NC_CACHE["nc"] = build(STAGE)
    nc = _NC_CACHE["nc"]
    ca = lambda a: np.ascontiguousarray(np.asarray(a, f32))
    in_maps = []
    for c in range(8):
        b, j = c // 4, c % 4
        npad = 128 * (3 - j)
        xbc = np.concatenate([np.zeros((npad, D), f32), x_prompt[b, 0:SEQ - npad],
                              x_sample[2 * c:2 * c + 2].reshape(128, D)], axis=0)
        tv = np.ones((NTOK,), f32)
        tv[:npad] = 0.0
        m = {
            "xb": ca(xbc), "w_in": ca(inp["w_in"][0]), "g_norm1": ca(inp["g_norm1"]), "g_k": ca(inp["g_k"]),
            "g_q": ca(inp["g_q"]), "w_conv": ca(inp["w_conv"][0]), "b_conv": ca(inp["b_conv"]),
            "state_conv": ca(inp["state_conv"][0, 2 * c:2 * c + 2]), "tokvalid": tv,
            "b_if": ca(inp["b_if"]), "g_mnorm": ca(inp["g_mnorm"]),
            "state_m": ca(inp["state_m"][0, 2 * c:2 * c + 2]),
            "state_C": ca(inp["state_C"][0, 2 * c:2 * c + 2]),
            "state_n": ca(inp["state_n"][0, 2 * c:2 * c + 2]),
            "cache_kidx": ca(inp["cache_kidx"][0, 2 * c:2 * c + 2]),
            "cache_k": ca(inp["cache_k"][0, 2 * c:2 * c + 2]).reshape(2, 1024, D),
            "cache_v": ca(inp["cache_v"][0, 2 * c:2 * c + 2]).reshape(2, 1024, D),
            "w_a_out": ca(inp["w_a_out"][0]), "w_b_out": ca(inp["w_b_out"][0]), "w_o": ca(inp["w_o"][0]),
            "g_norm2": ca(inp["g_norm2"]), "w_ffn_in": ca(inp["w_ffn_in"][0]),
            "w_ffn_out": ca(inp["w_ffn_out"][0]),
        }
        in_maps.append(m)
    res = run_bass_kernel_spmd(nc, in_maps, core_ids=list(range(8)))
    R = res.results
    z = lambda *s: np.zeros(s, f32)
    y_prompt, y_sample = z(2, SEQ, D), z(16, 64, D)
    k_prompt, v_prompt, ki_prompt = z(1, 2, SEQ, 8, 128), z(1, 2, SEQ, 8, 128), z(1, 2, SEQ, 64)
    k_sample, v_sample, ki_sample = z(1, 16, 64, 8, 128), z(1, 16, 64, 8, 128), z(1, 16, 64, 64)
    conv_prompt, conv_sample = z(1, 2, 3, 2048), z(1, 16, 3, 2048)
    C_prompt, n_prompt, m_prompt = z(1, 2, 4, 256, 256), z(1, 2, 4, 256), z(1, 2, 4)
    C_sample, n_sample, m_sample = z(1, 16, 4, 256, 256), z(1, 16, 4, 256), z(1, 16, 4)
    for c in range(8):
        b, j = c // 4, c % 4
        r = R[c]
        if j == 3:
            k_prompt[0, b] = r["k_out"][:SEQ].reshape(SEQ, 8, 128)
            v_prompt[0, b] = r["v_out"][:SEQ].reshape(SEQ, 8, 128)
            ki_prompt[0, b] = r["ki_out"][:SEQ]
            conv_prompt[0, b, :, 1024:] = r["cvk_out"][0].T
            if "cvq_out" in r:
                conv_prompt[0, b, :, :1024] = r["cvq_out"][0].T
            if "C_out" in r:
                C_prompt[0, b] = r["C_out"][0]
                n_prompt[0, b] = r["n_out"][0]
                m_prompt[0, b] = r["m_out"][0]
        k_sample[0, 2 * c:2 * c + 2] = r["k_out"][SEQ:].reshape(2, 64, 8, 128)
        v_sample[0, 2 * c:2 * c + 2] = r["v_out"][SEQ:].reshape(2, 64, 8, 128)
        ki_sample[0, 2 * c:2 * c + 2] = r["ki_out"][SEQ:].reshape(2, 64, 64)
        for s_ in range(2):
            conv_sample[0, 2 * c + s_, :, 1024:] = r["cvk_out"][1 + s_].T
            if "cvq_out" in r:
                conv_sample[0, 2 * c + s_, :, :1024] = r["cvq_out"][1 + s_].T
            if "C_out" in r:
                C_sample[0, 2 * c + s_] = r["C_out"][1 + s_]
                n_sample[0, 2 * c + s_] = r["n_out"][1 + s_]
                m_sample[0, 2 * c + s_] = r["m_out"][1 + s_]
        if "y_out" in r:
            yo = r["y_out"]
            for g in range(16):
                G = 4 * g + j
                y_prompt[b, 128 * G:128 * (G + 1)] = yo[128 * g:128 * (g + 1)]
            y_sample[2 * c:2 * c + 2] = yo[2048:].reshape(2, 64, D)
    return (y_prompt, y_sample, k_prompt, v_prompt, ki_prompt, C_prompt, n_prompt, m_prompt, conv_prompt,
            k_sample, v_sample, ki_sample, C_sample, n_sample, m_sample, conv_sample)
```
